# Optimizing a Trainium2 kernel written in Bass

```python
import jax, jax.numpy as jnp
from jax import lax
import numpy as np

D_MODEL = 1024
BATCH = 8
SEQ = 2048
DEPTH = 4

GRID_W = 64
CTX_LEN = 256
N_MIXERS = 3
HEAD_DIM = 64
N_Q_HEADS = D_MODEL // HEAD_DIM
N_KV_HEADS = N_Q_HEADS // 4
Q_BLOCK = 128
ROPE_THETA = 10000.0
RET_HEAD_DIM = 256
RET_HEADS = D_MODEL // RET_HEAD_DIM
RET_V_DIM = 2 * RET_HEAD_DIM
RET_CHUNK = 128
RET_DECAY_BASE = 5.0
D_FF = ((8 * D_MODEL // 3 + 127) // 128) * 128
NORM_EPS = 1e-6
N_CONV_LAYERS = len(range(0, DEPTH, N_MIXERS))
N_ATTN_LAYERS = len(range(1, DEPTH, N_MIXERS))
N_RET_LAYERS = len(range(2, DEPTH, N_MIXERS))

kernel_name = "hybrid_shortconv_gqa_retention_dit"


def rmsnorm(x, g=None):
    xf = x.astype(jnp.float32)
    y = xf * lax.rsqrt(jnp.mean(xf * xf, axis=-1, keepdims=True) + NORM_EPS)
    if g is not None:
        y = y * g.astype(jnp.float32)
    return y.astype(x.dtype)


def modulate(x, g, shift, scale):
    return rmsnorm(x, g) * (1 + scale) + shift


def dwconv3(x, w):
    xp = jnp.pad(x, ((0, 0), (1, 1), (0, 0)))
    return xp[:, :-2] * w[0] + xp[:, 1:-1] * w[1] + xp[:, 2:] * w[2]


def axial_angles(rows, cols, head_dim):
    quarter = head_dim // 4
    inv = ROPE_THETA ** (-jnp.arange(quarter, dtype=jnp.float32) / quarter)
    ang = jnp.stack([rows[:, None] * inv, cols[:, None] * inv], axis=1)
    return jnp.cos(ang), jnp.sin(ang)


def apply_axial_rope(x, cos, sin):
    B, L, H, d = x.shape
    xr = x.reshape(B, L, H, 2, 2, d // 4)
    c = cos[None, :, None].astype(x.dtype)
    s = sin[None, :, None].astype(x.dtype)
    x1 = xr[..., 0, :]
    x2 = xr[..., 1, :]
    out = jnp.stack([x1 * c - x2 * s, x1 * s + x2 * c], axis=-2)
    return out.reshape(B, L, H, d)


def short_conv_mixer(h, w_in, conv_k, w_out):
    b_gate, c_gate, v = jnp.split(h @ w_in, 3, axis=-1)
    return (b_gate * dwconv3(c_gate * v, conv_k)) @ w_out


def gqa_project(t, w_qkv, q_g, k_g, with_q):
    B, L, _ = t.shape
    nq = N_Q_HEADS * HEAD_DIM
    nkv = N_KV_HEADS * HEAD_DIM
    if with_q:
        q, k, v = jnp.split(t @ w_qkv, [nq, nq + nkv], axis=-1)
        q = rmsnorm(q.reshape(B, L, N_Q_HEADS, HEAD_DIM), q_g)
    else:
        k, v = jnp.split(t @ w_qkv[:, nq:], 2, axis=-1)
        q = None
    k = rmsnorm(k.reshape(B, L, N_KV_HEADS, HEAD_DIM), k_g)
    v = v.reshape(B, L, N_KV_HEADS, HEAD_DIM)
    return q, k, v


def sdpa_grouped(q, k, v):
    s = jnp.einsum('bqhgd,bkhd->bhgqk', q, k).astype(jnp.float32) * (q.shape[-1] ** -0.5)
    p = jax.nn.softmax(s, axis=-1).astype(v.dtype)
    return jnp.einsum('bhgqk,bkhd->bqhgd', p, v)


def gqa_mixer(h, hc, w_qkv, q_g, k_g, w_out, cos, sin, ctx_out):
    B, S, _ = h.shape
    G = N_Q_HEADS // N_KV_HEADS
    q, k, v = gqa_project(h, w_qkv, q_g, k_g, True)
    q = apply_axial_rope(q, cos, sin)
    k = apply_axial_rope(k, cos, sin)
    qc, kc, vc = gqa_project(hc, w_qkv, q_g, k_g, ctx_out)
    keys = jnp.concatenate([k, kc], axis=1)
    vals = jnp.concatenate([v, vc], axis=1)
    qb = q.reshape(B, S // Q_BLOCK, Q_BLOCK, N_KV_HEADS, G, HEAD_DIM).swapaxes(0, 1)
    ob = lax.map(lambda q_blk: sdpa_grouped(q_blk, keys, vals), qb)
    y = ob.swapaxes(0, 1).reshape(B, S, D_MODEL) @ w_out
    yc = None
    if ctx_out:
        Lc = hc.shape[1]
        oc = sdpa_grouped(qc.reshape(B, Lc, N_KV_HEADS, G, HEAD_DIM), kc, vc)
        yc = oc.reshape(B, Lc, D_MODEL) @ w_out
    return y, yc


def ret_project(t, w_in, with_qg):
    B, L, _ = t.shape
    nqk = RET_HEADS * RET_HEAD_DIM
    nv = RET_HEADS * RET_V_DIM
    if with_qg:
        q, k, v, g = jnp.split(t @ w_in, [nqk, 2 * nqk, 2 * nqk + nv], axis=-1)
        q = q.reshape(B, L, RET_HEADS, RET_HEAD_DIM)
    else:
        k, v = jnp.split(t @ w_in[:, nqk:2 * nqk + nv], [nqk], axis=-1)
        q = g = None
    k = k.reshape(B, L, RET_HEADS, RET_HEAD_DIM) * (RET_HEAD_DIM ** -0.5)
    v = v.reshape(B, L, RET_HEADS, RET_V_DIM)
    return q, k, v, g


def retention_scan(q, k, v, log_g, state0, strict):
    B, L, H, dk = q.shape
    dv = v.shape[-1]
    C = RET_CHUNK
    n = L // C
    pos = jnp.arange(C, dtype=jnp.float32)
    diff = pos[:, None] - pos[None, :]
    mask = diff > 0 if strict else diff >= 0
    intra_decay = jnp.where(mask[None], jnp.exp(jnp.where(mask, diff, 0.0)[None] * log_g[:, None, None]), 0.0).astype(q.dtype)
    q_decay = jnp.exp((pos + 1)[:, None] * log_g[None]).astype(q.dtype)
    k_decay = jnp.exp((C - 1 - pos)[:, None] * log_g[None]).astype(q.dtype)
    chunk_decay = jnp.exp(C * log_g).astype(q.dtype)[None, :, None, None]
    qc = q.reshape(B, n, C, H, dk)
    kc = k.reshape(B, n, C, H, dk)
    vc = v.reshape(B, n, C, H, dv)
    scores = jnp.einsum('bnihd,bnjhd->bnhij', qc, kc) * intra_decay
    intra = jnp.einsum('bnhij,bnjhe->bnihe', scores, vc)
    xs = ((qc * q_decay[:, :, None]).swapaxes(0, 1),
          (kc * k_decay[:, :, None]).swapaxes(0, 1),
          vc.swapaxes(0, 1))

    def step(state, inp):
        qd, kd, vv = inp
        inter = jnp.einsum('bihd,bhde->bihe', qd, state)
        state = chunk_decay * state + jnp.einsum('bjhd,bjhe->bhde', kd, vv)
        return state, inter

    state, inter = lax.scan(step, state0, xs)
    out = intra + inter.swapaxes(0, 1)
    return out.reshape(B, L, H, dv), state


def retention_final_state(k, v, log_g):
    L = k.shape[1]
    w = jnp.exp((L - 1 - jnp.arange(L, dtype=jnp.float32))[:, None] * log_g[None]).astype(k.dtype)
    return jnp.einsum('blhd,blhe->bhde', k * w[:, :, None], v)


def ret_output(y, g, w_out):
    B, L = y.shape[:2]
    yn = rmsnorm(y).reshape(B, L, RET_HEADS * RET_V_DIM)
    return (jax.nn.silu(g) * yn) @ w_out


def retention_mixer(h, hc, w_in, decay_exp, w_out, cos, sin, ctx_out):
    log_g = jnp.log1p(-jnp.exp2(-decay_exp.astype(jnp.float32)))
    flip = lambda t: jnp.flip(t, axis=1)
    q, k, v, g = ret_project(h, w_in, True)
    q = apply_axial_rope(q, cos, sin)
    k = apply_axial_rope(k, cos, sin)
    qc, kc, vc, gc = ret_project(hc, w_in, ctx_out)
    yc = None
    if ctx_out:
        B = hc.shape[0]
        zero = jnp.zeros((B, RET_HEADS, RET_HEAD_DIM, RET_V_DIM), vc.dtype)
        yc_f, st_f = retention_scan(qc, kc, vc, log_g[0], zero, False)
        yc_b, st_b = retention_scan(flip(qc), flip(kc), flip(vc), log_g[1], zero, True)
        yc = ret_output(yc_f + flip(yc_b), gc, w_out)
    else:
        st_f = retention_final_state(kc, vc, log_g[0])
        st_b = retention_final_state(flip(kc), flip(vc), log_g[1])
    y_f, _ = retention_scan(q, k, v, log_g[0], st_f, False)
    y_b, _ = retention_scan(flip(q), flip(k), flip(v), log_g[1], st_b, True)
    return ret_output(y_f + flip(y_b), g, w_out), yc


def conv_ffn(h, w_up, conv_k, conv_b, w_down):
    u = dwconv3(h @ w_up, conv_k) + conv_b
    val, gate = jnp.split(u, 2, axis=-1)
    return (jax.nn.silu(gate) * val) @ w_down


def setup_inputs(seed: int = 0) -> dict:
    key = jax.random.key(seed)
    ks = iter(jax.random.split(key, 32))
    f32 = jnp.float32

    def normal(shape):
        return jax.random.normal(next(ks), shape, f32)

    def dense(shape, fan_in, gain=1.0):
        return normal(shape) * (gain * fan_in ** -0.5)

    qkv_w = (N_Q_HEADS + 2 * N_KV_HEADS) * HEAD_DIM
    ret_in_w = 2 * RET_HEADS * RET_HEAD_DIM + 2 * RET_HEADS * RET_V_DIM
    ret_v_w = RET_HEADS * RET_V_DIM
    return {
        "x": normal((BATCH, SEQ, D_MODEL)),
        "c": normal((BATCH, D_MODEL)),
        "ctx": normal((BATCH, CTX_LEN, D_MODEL)),
        "c_ctx": normal((D_MODEL,)),
        "ada_w": dense((DEPTH, D_MODEL, 6 * D_MODEL), D_MODEL, 0.5),
        "ada_b": 0.02 * normal((DEPTH, 6 * D_MODEL)),
        "norm_mix_g": 1.0 + 0.05 * normal((DEPTH, D_MODEL)),
        "norm_ffn_g": 1.0 + 0.05 * normal((DEPTH, D_MODEL)),
        "final_norm_g": 1.0 + 0.05 * normal((D_MODEL,)),
        "conv_w_in": dense((N_CONV_LAYERS, D_MODEL, 3 * D_MODEL), D_MODEL),
        "conv_k": dense((N_CONV_LAYERS, 3, D_MODEL), 3),
        "conv_w_out": dense((N_CONV_LAYERS, D_MODEL, D_MODEL), D_MODEL),
        "attn_w_qkv": dense((N_ATTN_LAYERS, D_MODEL, qkv_w), D_MODEL),
        "attn_q_norm_g": 1.0 + 0.05 * normal((N_ATTN_LAYERS, HEAD_DIM)),
        "attn_k_norm_g": 1.0 + 0.05 * normal((N_ATTN_LAYERS, HEAD_DIM)),
        "attn_w_out": dense((N_ATTN_LAYERS, D_MODEL, D_MODEL), D_MODEL),
        "ret_w_in": dense((N_RET_LAYERS, D_MODEL, ret_in_w), D_MODEL),
        "ret_decay": RET_DECAY_BASE + jnp.arange(RET_HEADS, dtype=f32) + 0.1 * normal((N_RET_LAYERS, 2, RET_HEADS)),
        "ret_w_out": dense((N_RET_LAYERS, ret_v_w, D_MODEL), ret_v_w),
        "ffn_w_up": dense((DEPTH, D_MODEL, 2 * D_FF), D_MODEL),
        "ffn_conv_k": dense((DEPTH, 3, 2 * D_FF), 3),
        "ffn_conv_b": 0.02 * normal((DEPTH, 2 * D_FF)),
        "ffn_w_down": dense((DEPTH, D_FF, D_MODEL), D_FF),
    }


def reference(x, c, ctx, c_ctx, ada_w, ada_b, norm_mix_g, norm_ffn_g, final_norm_g,
              conv_w_in, conv_k, conv_w_out, attn_w_qkv, attn_q_norm_g, attn_k_norm_g, attn_w_out,
              ret_w_in, ret_decay, ret_w_out, ffn_w_up, ffn_conv_k, ffn_conv_b, ffn_w_down):
    S = x.shape[1]
    ROWS = S // GRID_W
    rows = jnp.repeat(jnp.arange(ROWS, dtype=jnp.float32), GRID_W)
    cols = jnp.tile(jnp.arange(GRID_W, dtype=jnp.float32), ROWS)
    attn_cos, attn_sin = axial_angles(rows, cols, HEAD_DIM)
    ret_cos, ret_sin = axial_angles(rows, cols, RET_HEAD_DIM)

    kinds = [i % N_MIXERS for i in range(DEPTH)]
    reads_ctx = [kd in (1, 2) for kd in kinds]
    silu_c = jax.nn.silu(c)
    silu_cc = jax.nn.silu(c_ctx)
    cx = ctx
    for i in range(DEPTH):
        kind = kinds[i]
        j = i // N_MIXERS
        ctx_out = any(reads_ctx[i + 1:])
        ctx_in = reads_ctx[i] or ctx_out
        sh_m, sc_m, g_m, sh_f, sc_f, g_f = [m[:, None, :] for m in jnp.split(silu_c @ ada_w[i] + ada_b[i], 6, axis=-1)]
        h = modulate(x, norm_mix_g[i], sh_m, sc_m)
        hc = None
        if ctx_in:
            shc_m, scc_m, gc_m, shc_f, scc_f, gc_f = jnp.split(silu_cc @ ada_w[i] + ada_b[i], 6, axis=-1)
            hc = modulate(cx, norm_mix_g[i], shc_m, scc_m)
        if kind == 0:
            y = short_conv_mixer(h, conv_w_in[j], conv_k[j], conv_w_out[j])
            yc = short_conv_mixer(hc, conv_w_in[j], conv_k[j], conv_w_out[j]) if ctx_out else None
        elif kind == 1:
            y, yc = gqa_mixer(h, hc, attn_w_qkv[j], attn_q_norm_g[j], attn_k_norm_g[j], attn_w_out[j],
                              attn_cos, attn_sin, ctx_out)
        else:
            y, yc = retention_mixer(h, hc, ret_w_in[j], ret_decay[j], ret_w_out[j],
                                    ret_cos, ret_sin, ctx_out)
        x = x + g_m * y
        x = x + g_f * conv_ffn(modulate(x, norm_ffn_g[i], sh_f, sc_f),
                               ffn_w_up[i], ffn_conv_k[i], ffn_conv_b[i], ffn_w_down[i])
        if ctx_out:
            cx = cx + gc_m * yc
            cx = cx + gc_f * conv_ffn(modulate(cx, norm_ffn_g[i], shc_f, scc_f),
                                      ffn_w_up[i], ffn_conv_k[i], ffn_conv_b[i], ffn_w_down[i])
    return rmsnorm(x, final_norm_g)
```

```python
import numpy as np
from contextlib import ExitStack
import concourse.bass as bass
import concourse.mybir as mybir
from concourse.bass_utils import run_bass_kernel_spmd

F32 = mybir.dt.float32
BF16 = mybir.dt.bfloat16
ALU = mybir.AluOpType
AF = mybir.ActivationFunctionType

ENGS = ("pe", "act", "dve", "pool", "sp")

D = 1024
KC = 8
SEQ = 2048
CTXL = 256
NCOL = 2308
LAT0 = 1
CTX0 = 2051
DFF = 2816
NPAIR = 22
EPS = 1e-6
SEG = [0, 513, 1025, 1537, 2050, NCOL]
PLAIN_LAT = [(1 + 512 * b, 512, 0) for b in range(4)]
PLAIN_CTX = [(CTX0, 256, 1)]
HALO_LAT = [(410 * b, min(412, 2050 - 410 * b), 0) for b in range(5)]
HALO_CTX = [(2050, 258, 1)]

LW = 264
O_ADAB, O_GMIX, O_GFFN, O_FK, O_FB, O_MK = 0, 48, 56, 64, 196, 240
O_FIN = 4 * LW
O_QG = O_FIN + 8
O_KG = O_QG + 1
O_DEC = O_KG + 1
NV = O_DEC + 8

RET_NH = 4
RET_DBG = 0
DBG_SKIP_FFN = False
RSLOT = 2048
NSLOT = 5


class Buf:
    __slots__ = ("name", "w", "r", "dsem", "excl")

    def __init__(self, name="", excl=False):
        self.name = name
        self.w = None
        self.r = {}
        self.dsem = None
        self.excl = excl


class Prog:
    def __init__(self, nc, eng_sems, dma_sems):
        self.nc = nc
        self.eng = {"pe": nc.tensor, "act": nc.scalar, "dve": nc.vector, "pool": nc.gpsimd, "sp": nc.sync}
        self.cnt = {e: 0 for e in ENGS}
        self.semh = dict(eng_sems)
        self.dma_free = list(dma_sems)
        self.dcnt = {}
        self.seen = {e: {} for e in ENGS}
        self.n_op = 0
        self.n_wait = 0

    def _deps(self, reads, writes, eng=None):
        deps = {}
        for b in reads:
            t = b.w
            if t is not None and deps.get(t[0], 0) < t[1]:
                deps[t[0]] = t[1]
            if b.excl:
                for k, v in b.r.items():
                    if k != eng and deps.get(k, 0) < v:
                        deps[k] = v
        for b in writes:
            t = b.w
            if t is not None and deps.get(t[0], 0) < t[1]:
                deps[t[0]] = t[1]
            for k, v in b.r.items():
                if deps.get(k, 0) < v:
                    deps[k] = v
        return deps

    def _waits(self, eng, deps):
        seen = self.seen[eng]
        for k, v in deps.items():
            if k == "pe" and eng == "pe":
                continue
            if seen.get(k, 0) < v:
                seen[k] = v
                self.eng[eng].wait_ge(self.semh[k], v)
                self.n_wait += 1

    def _mark(self, tok, reads, writes):
        k, v = tok
        for b in writes:
            b.w = tok
            b.r = {}
        for b in reads:
            if b.r.get(k, 0) < v:
                b.r[k] = v

    def op(self, eng, fn, reads=(), writes=(), inc=True):
        self._waits(eng, self._deps(reads, writes, eng))
        tok = (eng, self.cnt[eng] + 1)
        ins = fn(self.eng[eng])
        if inc:
            self.cnt[eng] += 1
            ins.then_inc(self.semh[eng], 1)
        self.n_op += 1
        self._mark(tok, reads, writes)
        return tok

    def _dsem(self, b):
        if b.dsem is None:
            h = self.dma_free.pop()
            key = ("dma", len(self.dcnt))
            self.semh[key] = h
            self.dcnt[key] = 0
            b.dsem = key
        return b.dsem

    def dma(self, queue, fn, reads=(), writes=(), owner=None):
        owner = owner or (writes[0] if writes else reads[0])
        key = self._dsem(owner)
        self._waits(queue, self._deps(reads, writes))
        self.dcnt[key] += 16
        tok = (key, self.dcnt[key])
        fn(self.eng[queue]).then_inc(self.semh[key], 16)
        self._mark(tok, reads, writes)
        return tok

    def barrier(self):
        deps = {e: self.cnt[e] for e in ("pe", "act", "dve", "pool") if self.cnt[e] > 0}
        for k, v in self.dcnt.items():
            if v > 0:
                deps[k] = v
        for e in ENGS:
            d = dict(deps)
            self._waits(e, d)


def segs(bufs, s, e):
    out = []
    for i in range(len(SEG) - 1):
        if s < SEG[i + 1] and e > SEG[i]:
            out.append(bufs[i])
    return out


def build(layers=(0, 1, 2, 3)):
    nc = bass.Bass("TRN2", target_bir_lowering=False)

    def din(name, shape):
        return nc.dram_tensor(name, list(shape), F32, kind="ExternalInput").ap()

    x_d = din("x", [SEQ, D])
    ctx_d = din("ctx", [CTXL, D])
    cvec_d = din("cvec", [128, KC * 2])
    vec_d = din("vec", [128, NV])
    ada_w = din("ada_w", [4, D, 6 * D])
    conv_w_in = din("conv_w_in", [2, D, 3 * D])
    conv_w_out = din("conv_w_out", [2, D, D])
    attn_wqkv = din("attn_wqkv", [D, 1536])
    attn_w_out = din("attn_w_out", [D, D])
    ret_w_in = din("ret_w_in", [D, 6 * D])
    ret_w_out = din("ret_w_out", [2 * D, D])
    ffn_w_up = din("ffn_w_up", [4, D, 2 * DFF])
    ffn_w_down = din("ffn_w_down", [4, DFF, D])
    rope_a = din("rope_a", [2, 128, SEQ])
    rope_r = din("rope_r", [2, 128, 96])
    iota_d = din("iota_d", [128, 512])
    cmat_d = din("cmat", [128, 3 * 128])
    out_d = nc.dram_tensor("out", [SEQ, D], F32, kind="ExternalOutput").ap()

    with ExitStack() as es:
        uniq = [0]

        def sb(name, shape, dt, stack=es):
            uniq[0] += 1
            return stack.enter_context(nc.sbuf_tensor(f"{name}_{uniq[0]}", list(shape), dt))

        XT = sb("XT", [128, KC, NCOL], F32)
        HT = sb("HT", [128, KC, NCOL], BF16)
        RING = sb("RING", [128, NSLOT, RSLOT], BF16)
        VEC = sb("VEC", [128, NV], F32)
        CV = sb("CVEC", [128, KC, 2], F32)
        SC = sb("SC", [128, KC, 2], BF16)
        MOD = sb("MOD", [128, 48, 2], F32)
        AM = sb("AM", [128, 2, KC, 2], F32)
        ONES = sb("ONES", [128, 128], BF16)
        IDF = sb("IDF", [128, 128], F32)
        EPSV = sb("EPSV", [128, 2], F32)
        CM = sb("CM", [128, 3, 128], BF16)
        PS = [es.enter_context(nc.psum_tensor(f"PS{i}", [128, 512], F32)) for i in range(8)]

        sems = {e: es.enter_context(nc.semaphore("s_" + e)) for e in ("pe", "act", "dve", "pool")}
        dsems = [es.enter_context(nc.semaphore(f"d{i}")) for i in range(60)]
        es.enter_context(nc.Block())
        p = Prog(nc, sems, dsems)

        bXT = [Buf(f"XT{i}") for i in range(5)]
        bHT = [Buf(f"HT{i}") for i in range(5)]
        bRING = [Buf(f"R{i}") for i in range(NSLOT)]
        bPS = [Buf(f"PS{i}", excl=True) for i in range(8)]
        bVEC, bCV, bSC, bMOD, bAM, bONES, bIDF, bEPS, bCM = [Buf(n) for n in "VEC CV SC MOD AM ONES IDF EPS CM".split()]
        st = {"slot": 0, "bank": 0}

        def next_bank():
            b = st["bank"]
            st["bank"] = (b + 1) % 8
            return b

        def ring_load(parts):
            s = st["slot"]
            st["slot"] = (s + 1) % NSLOT
            for (c0, kcn, n, tot, src) in parts:
                dst = RING[:, s, 0:kcn * tot].rearrange("p (k n) -> p k n", n=tot)[:, :, c0:c0 + n]
                p.dma("pool", lambda e, dst=dst, src=src: e.dma_start(out=dst, in_=src), writes=[bRING[s]])
            return s

        def wview(s, kcn, tot):
            return RING[:, s, 0:kcn * tot].rearrange("p (k n) -> p k n", n=tot)

        def mm_group(out_ap, bank, pairs, reads):
            n = len(pairs)
            for i, (l, r) in enumerate(pairs):
                p.op("pe", lambda e, l=l, r=r, i=i: e.matmul(out_ap, lhsT=l, rhs=r, start=(i == 0), stop=(i == n - 1)),
                     reads=reads, writes=[bPS[bank]], inc=(i == n - 1))

        p.dma("sp", lambda e: e.dma_start(out=VEC[:], in_=vec_d), writes=[bVEC])
        p.dma("sp", lambda e: e.dma_start(out=CV[:].rearrange("p k t -> p (k t)"), in_=cvec_d), writes=[bCV])
        p.dma("pool", lambda e: e.dma_start(out=CM[:].rearrange("p a b -> p (a b)"), in_=cmat_d), writes=[bCM])
        p.op("dve", lambda e: e.memset(ONES[:], 1.0), writes=[bONES])
        p.op("dve", lambda e: e.memset(EPSV[:, 0:1], 1024.0 * EPS), writes=[bEPS])
        p.op("dve", lambda e: e.memset(EPSV[:, 1:2], EPS), reads=[bEPS], writes=[bEPS])
        p.op("pool", lambda e: e.memset(IDF[:], 0.0), writes=[bIDF])
        p.op("pool", lambda e: e.affine_select(out=IDF[:], in_=IDF[:], pattern=[[-1, 128]], compare_op=ALU.not_equal,
                                               fill=1.0, base=0, channel_multiplier=1), reads=[bIDF], writes=[bIDF])
        p.op("pool", lambda e: e.memset(HT[:].rearrange("p k n -> p (k n)"), 0.0), writes=bHT)
        p.op("pool", lambda e: e.memset(XT[:].rearrange("p k n -> p (k n)"), 0.0), writes=bXT)
        p.op("act", lambda e: e.activation(out=SC[:], in_=CV[:], func=AF.Silu), reads=[bCV], writes=[bSC])

        with ExitStack() as ph:
            XS = [sb(f"XS{i}", [128, D], F32, ph) for i in range(2)]
            bXS = [Buf("XS0"), Buf("XS1")]
            for t in range(18):
                src = x_d[128 * t:128 * t + 128, :] if t < 16 else ctx_d[128 * (t - 16):128 * (t - 16) + 128, :]
                col = LAT0 + 128 * t if t < 16 else CTX0 + 128 * (t - 16)
                xs, bxs = XS[t % 2], bXS[t % 2]
                p.dma("sp", lambda e, xs=xs, src=src: e.dma_start(out=xs[:], in_=src), writes=[bxs])
                for half in range(2):
                    bk = next_bank()
                    for q in range(4):
                        kc = half * 4 + q
                        p.op("pe", lambda e, bk=bk, q=q, kc=kc, xs=xs: e.transpose(out=PS[bk][:, 128 * q:128 * q + 128],
                                                                                     in_=xs[:, 128 * kc:128 * kc + 128], identity=IDF[:]),
                             reads=[bxs, bIDF], writes=[bPS[bk]], inc=(q == 3))
                    eng = "act" if half == 0 else "dve"
                    dst = XT[:, half * 4:half * 4 + 4, col:col + 128]
                    srcp = PS[bk][:].rearrange("p (k n) -> p k n", n=128)
                    if eng == "act":
                        p.op("act", lambda e, dst=dst, srcp=srcp: e.activation(out=dst, in_=srcp, func=AF.Copy),
                             reads=[bPS[bk]], writes=segs(bXT, col, col + 128))
                    else:
                        p.op("dve", lambda e, dst=dst, srcp=srcp: e.tensor_copy(out=dst, in_=srcp),
                             reads=[bPS[bk]], writes=segs(bXT, col, col + 128))
            p.barrier()

        def ada_phase(L):
            psa = PS[next_bank()]
            bk = (st["bank"] - 1) % 8
            wsrc = ada_w[L].rearrange("(k p) n -> p k n", p=128)
            for pi in range(24):
                s = ring_load([(0, KC, 256, 256, wsrc[:, :, 256 * pi:256 * pi + 256])])
                wv = wview(s, KC, 256)
                for mt in range(2):
                    m = 2 * pi + mt
                    mm_group(psa[:, 2 * m:2 * m + 2], bk,
                             [(wv[:, kc, 128 * mt:128 * mt + 128], SC[:, kc, :]) for kc in range(KC)],
                             reads=[bRING[s], bSC])
            p.op("dve", lambda e: e.tensor_tensor(out=MOD[:], in0=psa[:, 0:96].rearrange("p (m t) -> p m t", t=2),
                                                  in1=VEC[:, L * LW + O_ADAB:L * LW + O_ADAB + 48].unsqueeze(2).broadcast_to([128, 48, 2]),
                                                  op=ALU.add), reads=[bPS[bk], bVEC], writes=[bMOD])
            for which, (osc, og) in enumerate(((8, O_GMIX), (32, O_GFFN))):
                p.op("dve", lambda e, which=which, osc=osc: e.tensor_scalar(out=AM[:, which], in0=MOD[:, osc:osc + 8, :], scalar1=1.0, scalar2=32.0,
                                                                              op0=ALU.add, op1=ALU.mult), reads=[bMOD, bAM], writes=[bAM])
                p.op("dve", lambda e, which=which, og=og: e.tensor_tensor(out=AM[:, which], in0=AM[:, which],
                                                                            in1=VEC[:, L * LW + og:L * LW + og + 8].unsqueeze(2).broadcast_to([128, 8, 2]),
                                                                            op=ALU.mult), reads=[bAM, bVEC], writes=[bAM])

        def norm_phase(which, blocks):
            osh = 0 if which == 0 else 24
            with ExitStack() as ph:
                SQ = [sb(f"SQ{i}", [128, KC, 512], BF16, ph) for i in range(2)]
                RR = [sb(f"RR{i}", [128, 512], F32, ph) for i in range(2)]
                TM = [sb(f"TM{i}", [128, 4, 512], F32, ph) for i in range(2)]
                bSQ = [Buf(), Buf()]
                bRR = [Buf(), Buf()]
                bTM = [Buf(), Buf()]
                for bi, (s, w, ic) in enumerate(blocks):
                    sq, bsq, rr, brr = SQ[bi % 2], bSQ[bi % 2], RR[bi % 2], bRR[bi % 2]
                    xb = segs(bXT, s, s + w)
                    hb = segs(bHT, s, s + w)
                    p.op("act", lambda e, sq=sq, s=s, w=w: e.activation(out=sq[:, :, 0:w], in_=XT[:, :, s:s + w], func=AF.Square),
                         reads=xb, writes=[bsq])
                    bk = next_bank()
                    mm_group(PS[bk][:, 0:w], bk, [(ONES[:], sq[:, kc, 0:w]) for kc in range(KC)], reads=[bsq, bONES])
                    p.op("act", lambda e, rr=rr, bk=bk, w=w: e.activation(out=rr[:, 0:w], in_=PS[bk][:, 0:w], func=AF.Sqrt, bias=EPSV[:, 0:1], scale=1.0),
                         reads=[bPS[bk], bEPS], writes=[brr])
                    p.op("dve", lambda e, rr=rr, w=w: e.reciprocal(out=rr[:, 0:w], in_=rr[:, 0:w]), reads=[brr], writes=[brr])
                    for half in range(2):
                        tm, btm = TM[half], bTM[half]
                        p.op("dve", lambda e, tm=tm, half=half, s=s, w=w, rr=rr: e.tensor_tensor(
                            out=tm[:, :, 0:w], in0=XT[:, 4 * half:4 * half + 4, s:s + w],
                            in1=rr[:, 0:w].unsqueeze(1).broadcast_to([128, 4, w]), op=ALU.mult),
                            reads=xb + [brr], writes=[btm])
                        for q in range(4):
                            kc = 4 * half + q
                            a_ap = AM[:, which, kc, ic:ic + 1]
                            b_ap = MOD[:, osh + kc, ic:ic + 1]
                            if q % 2 == 0:
                                p.op("dve", lambda e, tm=tm, q=q, kc=kc, s=s, w=w, a_ap=a_ap, b_ap=b_ap: e.tensor_scalar(
                                    out=HT[:, kc, s:s + w], in0=tm[:, q, 0:w], scalar1=a_ap, scalar2=b_ap, op0=ALU.mult, op1=ALU.add),
                                    reads=[btm, bAM, bMOD], writes=hb)
                            else:
                                p.op("act", lambda e, tm=tm, q=q, kc=kc, s=s, w=w, a_ap=a_ap, b_ap=b_ap: e.activation(
                                    out=HT[:, kc, s:s + w], in_=tm[:, q, 0:w], func=AF.Identity, scale=a_ap, bias=b_ap),
                                    reads=[btm, bAM, bMOD], writes=hb)
                p.barrier()

        def resid_add(bk, m, s, w, ic, gcol):
            g_ap = MOD[:, gcol + m, ic:ic + 1]
            xb = segs(bXT, s, s + w)
            p.op("dve", lambda e: e.scalar_tensor_tensor(out=XT[:, m, s:s + w], in0=PS[bk][:, 0:w], scalar=g_ap,
                                                         in1=XT[:, m, s:s + w], op0=ALU.mult, op1=ALU.add),
                 reads=[bPS[bk], bMOD] + xb, writes=xb)

        def ffn_phase(L, halo, plain):
            vb = L * LW
            wup = ffn_w_up[L].rearrange("(k p) n -> p k n", p=128)
            groups = [list(range(0, 6)), list(range(6, 12)), list(range(12, 17)), list(range(17, 22))]
            with ExitStack() as ph:
                FT = sb("FT", [128, 6, NCOL], BF16, ph)
                bFT = [Buf(f"FT{i}") for i in range(5)]
                VA = [sb(f"VA{i}", [128, 412], F32, ph) for i in range(2)]
                GA = [sb(f"GA{i}", [128, 412], F32, ph) for i in range(2)]
                SG = [sb(f"SG{i}", [128, 412], F32, ph) for i in range(2)]
                bVA, bGA, bSG = [Buf(), Buf()], [Buf(), Buf()], [Buf(), Buf()]
                it = 0
                for grp in groups:
                    gp = len(grp)
                    for il, i in enumerate(grp):
                        s_ = ring_load([(0, KC, 128, 256, wup[:, :, 128 * i:128 * i + 128]),
                                        (128, KC, 128, 256, wup[:, :, DFF + 128 * i:DFF + 128 * i + 128])])
                        wv = wview(s_, KC, 256)
                        for (s, w, ic) in halo:
                            ba, bg = next_bank(), next_bank()
                            hb = segs(bHT, s, s + w)
                            mm_group(PS[ba][:, 0:w], ba, [(wv[:, kc, 0:128], HT[:, kc, s:s + w]) for kc in range(KC)], reads=[bRING[s_]] + hb)
                            mm_group(PS[bg][:, 0:w], bg, [(wv[:, kc, 128:256], HT[:, kc, s:s + w]) for kc in range(KC)], reads=[bRING[s_]] + hb)
                            va, ga, sg = VA[it % 2], GA[it % 2], SG[it % 2]
                            bva, bga, bsg = bVA[it % 2], bGA[it % 2], bSG[it % 2]
                            it += 1
                            n = w - 2

                            def kcol(d, ch):
                                return VEC[:, vb + O_FK + d * 44 + ch:vb + O_FK + d * 44 + ch + 1]
                            for (acc, bacc, bkk, ch) in ((va, bva, ba, i), (ga, bga, bg, NPAIR + i)):
                                p.op("act", lambda e, acc=acc, bkk=bkk, ch=ch, n=n: e.activation(
                                    out=acc[:, 0:n], in_=PS[bkk][:, 1:1 + n], func=AF.Identity, scale=kcol(1, ch),
                                    bias=VEC[:, vb + O_FB + ch:vb + O_FB + ch + 1]), reads=[bPS[bkk], bVEC], writes=[bacc])
                                for d in (0, 2):
                                    p.op("dve", lambda e, acc=acc, bkk=bkk, ch=ch, n=n, d=d: e.scalar_tensor_tensor(
                                        out=acc[:, 0:n], in0=PS[bkk][:, d:d + n], scalar=kcol(d, ch), in1=acc[:, 0:n],
                                        op0=ALU.mult, op1=ALU.add), reads=[bPS[bkk], bVEC, bacc], writes=[bacc])
                            p.op("act", lambda e, sg=sg, ga=ga, n=n: e.activation(out=sg[:, 0:n], in_=ga[:, 0:n], func=AF.Silu),
                                 reads=[bga], writes=[bsg])
                            p.op("dve", lambda e, sg=sg, va=va, n=n, il=il, s=s: e.tensor_tensor(
                                out=FT[:, il, s + 1:s + 1 + n], in0=va[:, 0:n], in1=sg[:, 0:n], op=ALU.mult),
                                reads=[bva, bsg], writes=segs(bFT, s + 1, s + 1 + n))
                    i0 = grp[0]
                    wdn = ffn_w_down[L][128 * i0:128 * (i0 + gp), :].rearrange("(k p) n -> p k n", p=128)
                    for hf in range(3 if gp == 6 else 2):
                        ncl = 1024 // 3 if gp == 6 else 512
                        pass
                    ncols = 256
                    for pj in range(1024 // ncols):
                        s_ = ring_load([(0, gp, ncols, ncols, wdn[:, :, ncols * pj:ncols * pj + ncols])])
                        wv = wview(s_, gp, ncols)
                        for mt in range(ncols // 128):
                            m = pj * (ncols // 128) + mt
                            for (s, w, ic) in plain:
                                bk = next_bank()
                                mm_group(PS[bk][:, 0:w], bk, [(wv[:, k, 128 * mt:128 * mt + 128], FT[:, k, s:s + w]) for k in range(gp)],
                                         reads=[bRING[s_]] + segs(bFT, s, s + w))
                                resid_add(bk, m, s, w, ic, 40)
                p.barrier()

        def conv_phase(L, halo, plain):
            j = L // 3
            vb = L * LW
            win = conv_w_in[j].rearrange("(k p) n -> p k n", p=128)
            wout = conv_w_out[j].rearrange("(k p) n -> p k n", p=128)
            with ExitStack() as ph:
                MT = sb("MT", [128, KC, NCOL], BF16, ph)
                bMT = [Buf(f"MT{i}") for i in range(5)]
                CS = [sb(f"CS{i}", [128, 412], F32, ph) for i in range(2)]
                CVV = [sb(f"CVV{i}", [128, 412], F32, ph) for i in range(2)]
                TT = [sb(f"TT{i}", [128, 412], F32, ph) for i in range(2)]
                bCS, bCVV, bTT = [Buf(), Buf()], [Buf(), Buf()], [Buf(), Buf()]
                it = 0
                for jc in range(KC):
                    s1 = ring_load([(0, KC, 128, 256, win[:, :, D + 128 * jc:D + 128 * jc + 128]),
                                    (128, KC, 128, 256, win[:, :, 2 * D + 128 * jc:2 * D + 128 * jc + 128])])
                    s2 = ring_load([(0, KC, 128, 128, win[:, :, 128 * jc:128 * jc + 128])])
                    w1 = wview(s1, KC, 256)
                    w2 = wview(s2, KC, 128)
                    for (s, w, ic) in halo:
                        bc, bv, bb = next_bank(), next_bank(), next_bank()
                        hb = segs(bHT, s, s + w)
                        mm_group(PS[bc][:, 0:w], bc, [(w1[:, kc, 0:128], HT[:, kc, s:s + w]) for kc in range(KC)], reads=[bRING[s1]] + hb)
                        mm_group(PS[bv][:, 0:w], bv, [(w1[:, kc, 128:256], HT[:, kc, s:s + w]) for kc in range(KC)], reads=[bRING[s1]] + hb)
                        mm_group(PS[bb][:, 0:w], bb, [(w2[:, kc, 0:128], HT[:, kc, s:s + w]) for kc in range(KC)], reads=[bRING[s2]] + hb)
                        cs, cvv, tt = CS[it % 2], CVV[it % 2], TT[it % 2]
                        bcs, bcvv, btt = bCS[it % 2], bCVV[it % 2], bTT[it % 2]
                        it += 1
                        n = w - 2

                        def kcol(d):
                            return VEC[:, vb + O_MK + d * 8 + jc:vb + O_MK + d * 8 + jc + 1]
                        p.op("act", lambda e, cs=cs, bc=bc, w=w: e.activation(out=cs[:, 0:w], in_=PS[bc][:, 0:w], func=AF.Copy),
                             reads=[bPS[bc]], writes=[bcs])
                        p.op("dve", lambda e, cvv=cvv, cs=cs, bv=bv, w=w: e.tensor_tensor(out=cvv[:, 0:w], in0=cs[:, 0:w], in1=PS[bv][:, 0:w], op=ALU.mult),
                             reads=[bcs, bPS[bv]], writes=[bcvv])
                        p.op("act", lambda e, tt=tt, cvv=cvv, n=n: e.activation(out=tt[:, 0:n], in_=cvv[:, 1:1 + n], func=AF.Copy, scale=kcol(1)),
                             reads=[bcvv, bVEC], writes=[btt])
                        for d in (0, 2):
                            p.op("dve", lambda e, tt=tt, cvv=cvv, n=n, d=d: e.scalar_tensor_tensor(
                                out=tt[:, 0:n], in0=cvv[:, d:d + n], scalar=kcol(d), in1=tt[:, 0:n], op0=ALU.mult, op1=ALU.add),
                                reads=[bcvv, bVEC, btt], writes=[btt])
                        p.op("dve", lambda e, tt=tt, bb=bb, n=n, s=s: e.tensor_tensor(
                            out=MT[:, jc, s + 1:s + 1 + n], in0=tt[:, 0:n], in1=PS[bb][:, 1:1 + n], op=ALU.mult),
                            reads=[btt, bPS[bb]], writes=segs(bMT, s + 1, s + 1 + n))
                for pj in range(4):
                    s_ = ring_load([(0, KC, 256, 256, wout[:, :, 256 * pj:256 * pj + 256])])
                    wv = wview(s_, KC, 256)
                    for mt in range(2):
                        m = 2 * pj + mt
                        for (s, w, ic) in plain:
                            bk = next_bank()
                            mm_group(PS[bk][:, 0:w], bk, [(wv[:, kc, 128 * mt:128 * mt + 128], MT[:, kc, s:s + w]) for kc in range(KC)],
                                     reads=[bRING[s_]] + segs(bMT, s, s + w))
                            resid_add(bk, m, s, w, ic, 16)
                p.barrier()


        def attn_phase(L):
            wq_all = attn_wqkv.rearrange("(k p) n -> p k n", p=128)
            plain_all = PLAIN_LAT + PLAIN_CTX
            with ExitStack() as ph:
                KT = sb("KT", [128, 2, NCOL], BF16, ph)
                VA = sb("VA", [128, 18, 4, 128], BF16, ph)
                QT = [sb(f"QT{i}", [128, NCOL], BF16, ph) for i in range(2)]
                OT = sb("OT", [128, NCOL], BF16, ph)
                TAB = [sb(f"TAB{i}", [128, 2, 512], F32, ph) for i in range(2)]
                PT = [sb(f"PT{i}", [128, 512], BF16, ph) for i in range(4)]
                SQ = sb("aSQ", [128, 512], BF16, ph)
                QG = sb("aQG", [128, 512], BF16, ph)
                RR = sb("aRR", [128, 512], F32, ph)
                T1 = sb("aT1", [128, 512], F32, ph)
                T2 = sb("aT2", [128, 512], F32, ph)
                RC = [sb(f"aRC{i}", [128, 512], F32, ph) for i in range(2)]
                bKT = [Buf(), Buf()]
                bVA, bOT = Buf(), Buf()
                bQT = [Buf(), Buf()]
                bTAB = [Buf(), Buf()]
                bPT = [Buf() for _ in range(4)]
                bSQ, bQG, bRR, bT1, bT2 = Buf(), Buf(), Buf(), Buf(), Buf()
                bRC = [Buf(), Buf()]
                cnt = {"tab": 0, "pt": 0, "rc": 0}

                p.op("pool", lambda e: e.memset(VA[:].rearrange("p a b c -> p (a b c)"), 1.0), writes=[bVA])

                def qk_post(bk, dst_ap, bdst, gcol, s, w, rope):
                    g_ap = VEC[:, gcol:gcol + 1]
                    p.op("act", lambda e: e.activation(out=SQ[:, 0:w], in_=PS[bk][:, 0:w], func=AF.Square), reads=[bPS[bk]], writes=[bSQ])
                    b2 = next_bank()
                    mm_group(PS[b2][:, 0:w], b2, [(CM[:, 0, :], SQ[:, 0:w])], reads=[bSQ, bCM])
                    p.op("act", lambda e: e.activation(out=RR[:, 0:w], in_=PS[b2][:, 0:w], func=AF.Sqrt, bias=EPSV[:, 1:2], scale=1.0 / 64),
                         reads=[bPS[b2], bEPS], writes=[bRR])
                    p.op("dve", lambda e: e.reciprocal(out=RR[:, 0:w], in_=RR[:, 0:w]), reads=[bRR], writes=[bRR])
                    if not rope:
                        p.op("dve", lambda e: e.scalar_tensor_tensor(out=dst_ap, in0=PS[bk][:, 0:w], scalar=g_ap, in1=RR[:, 0:w],
                                                                     op0=ALU.mult, op1=ALU.mult), reads=[bPS[bk], bVEC, bRR], writes=[bdst])
                        return
                    ti = cnt["tab"] % 2
                    cnt["tab"] += 1
                    tab, btab = TAB[ti], bTAB[ti]
                    t0 = s - LAT0
                    p.dma("sp", lambda e: e.dma_start(out=tab[:, 0, :], in_=rope_a[0][:, t0:t0 + 512]), writes=[btab])
                    p.dma("sp", lambda e: e.dma_start(out=tab[:, 1, :], in_=rope_a[1][:, t0:t0 + 512]), writes=[btab])
                    p.op("act", lambda e: e.activation(out=QG[:, 0:w], in_=PS[bk][:, 0:w], func=AF.Copy, scale=g_ap), reads=[bPS[bk], bVEC], writes=[bQG])
                    b3 = next_bank()
                    mm_group(PS[b3][:, 0:w], b3, [(CM[:, 1, :], QG[:, 0:w])], reads=[bQG, bCM])
                    p.op("dve", lambda e: e.scalar_tensor_tensor(out=T1[:, 0:w], in0=PS[bk][:, 0:w], scalar=g_ap, in1=tab[:, 0, 0:w],
                                                                 op0=ALU.mult, op1=ALU.mult), reads=[bPS[bk], bVEC, btab], writes=[bT1])
                    p.op("dve", lambda e: e.tensor_tensor(out=T2[:, 0:w], in0=PS[b3][:, 0:w], in1=tab[:, 1, 0:w], op=ALU.mult),
                         reads=[bPS[b3], btab], writes=[bT2])
                    p.op("dve", lambda e: e.tensor_tensor(out=T1[:, 0:w], in0=T1[:, 0:w], in1=T2[:, 0:w], op=ALU.add), reads=[bT1, bT2], writes=[bT1])
                    p.op("dve", lambda e: e.tensor_tensor(out=dst_ap, in0=T1[:, 0:w], in1=RR[:, 0:w], op=ALU.mult), reads=[bT1, bRR], writes=[bdst])

                for g2 in range(2):
                    s_ = ring_load([(0, KC, 128, 128, wq_all[:, :, 1024 + 128 * g2:1024 + 128 * g2 + 128])])
                    wv = wview(s_, KC, 128)
                    for (s, w, ic) in plain_all:
                        bk = next_bank()
                        mm_group(PS[bk][:, 0:w], bk, [(wv[:, kc, :], HT[:, kc, s:s + w]) for kc in range(KC)], reads=[bRING[s_]] + segs(bHT, s, s + w))
                        qk_post(bk, KT[:, g2, s:s + w], bKT[g2], O_KG, s, w, rope=(ic == 0))
                s_ = ring_load([(0, KC, 256, 256, wq_all[:, :, 1280:1536])])
                wv = wview(s_, KC, 256)
                for kt in range(18):
                    col = LAT0 + 128 * kt if kt < 16 else CTX0 + 128 * (kt - 16)
                    bk = next_bank()
                    mm_group(PS[bk][:, 0:256], bk, [(HT[:, kc, col:col + 128], wv[:, kc, :]) for kc in range(KC)],
                             reads=[bRING[s_]] + segs(bHT, col, col + 128))
                    p.op("act", lambda e, kt=kt, bk=bk: e.activation(out=VA[:, kt, :, 0:64], in_=PS[bk][:, 0:256].rearrange("p (h d) -> p h d", d=64), func=AF.Copy),
                         reads=[bPS[bk]], writes=[bVA])

                def q_proj(j):
                    g2, r = j // 4, j % 4
                    ha, hb = 8 * g2 + r, 8 * g2 + 4 + r
                    s_ = ring_load([(0, KC, 64, 128, wq_all[:, :, 64 * ha:64 * ha + 64]),
                                    (64, KC, 64, 128, wq_all[:, :, 64 * hb:64 * hb + 64])])
                    wv = wview(s_, KC, 128)
                    for (s, w, ic) in plain_all:
                        bk = next_bank()
                        mm_group(PS[bk][:, 0:w], bk, [(wv[:, kc, :], HT[:, kc, s:s + w]) for kc in range(KC)], reads=[bRING[s_]] + segs(bHT, s, s + w))
                        qk_post(bk, QT[j % 2][:, s:s + w], bQT[j % 2], O_QG, s, w, rope=(ic == 0))

                def attend(j):
                    g2, r = j // 4, j % 4
                    ha, hb = 8 * g2 + r, 8 * g2 + 4 + r
                    qt, bqt = QT[j % 2], bQT[j % 2]
                    for (s, w, ic) in plain_all:
                        kts = list(range(18)) if ic == 0 else [16, 17]
                        oa = [next_bank(), next_bank()]
                        pend = None
                        for idx, kt in enumerate(kts):
                            kcol = LAT0 + 128 * kt if kt < 16 else CTX0 + 128 * (kt - 16)
                            cur = []
                            for hh in range(2):
                                bs = next_bank()
                                while bs in oa:
                                    bs = next_bank()
                                lo = 64 * hh
                                mm_group(PS[bs][:, 0:w], bs, [(KT[lo:lo + 64, g2, kcol:kcol + 128], qt[lo:lo + 64, s:s + w])], reads=[bKT[g2], bqt])
                                pi = cnt["pt"] % 4
                                cnt["pt"] += 1
                                p.op("act", lambda e, pi=pi, bs=bs: e.activation(out=PT[pi][:, 0:w], in_=PS[bs][:, 0:w], func=AF.Exp, scale=0.125),
                                     reads=[bPS[bs]], writes=[bPT[pi]])
                                cur.append((hh, pi, kt, idx))
                            if pend is not None:
                                for (hh, pi, kt_, idx_) in pend:
                                    p.op("pe", lambda e, hh=hh, pi=pi, kt_=kt_, idx_=idx_: e.matmul(
                                        PS[oa[hh]][:, 0:w], lhsT=VA[:, kt_, 2 * g2 + hh, :], rhs=PT[pi][:, 0:w], start=(idx_ == 0), stop=(idx_ == len(kts) - 1)),
                                        reads=[bVA, bPT[pi]], writes=[bPS[oa[hh]]], inc=True)
                            pend = cur
                        for (hh, pi, kt_, idx_) in pend:
                            p.op("pe", lambda e, hh=hh, pi=pi, kt_=kt_, idx_=idx_: e.matmul(
                                PS[oa[hh]][:, 0:w], lhsT=VA[:, kt_, 2 * g2 + hh, :], rhs=PT[pi][:, 0:w], start=(idx_ == 0), stop=(idx_ == len(kts) - 1)),
                                reads=[bVA, bPT[pi]], writes=[bPS[oa[hh]]], inc=True)
                        for hh in range(2):
                            ri = cnt["rc"] % 2
                            cnt["rc"] += 1
                            rc, brc = RC[ri], bRC[ri]
                            p.op("dve", lambda e, rc=rc, hh=hh: e.reciprocal(out=rc[0:64, 0:w], in_=PS[oa[hh]][64:128, 0:w]), reads=[bPS[oa[hh]]], writes=[brc])
                            p.op("dve", lambda e, rc=rc, hh=hh: e.tensor_tensor(out=OT[64 * hh:64 * hh + 64, s:s + w], in0=PS[oa[hh]][0:64, 0:w],
                                                                                in1=rc[0:64, 0:w], op=ALU.mult), reads=[bPS[oa[hh]], brc], writes=[bOT])
                    s_ = st["slot"]
                    st["slot"] = (s_ + 1) % NSLOT
                    for hh, hd in enumerate((ha, hb)):
                        p.dma("pool", lambda e, hh=hh, hd=hd: e.dma_start(out=RING[64 * hh:64 * hh + 64, s_, 0:1024], in_=attn_w_out[64 * hd:64 * hd + 64, :]),
                              writes=[bRING[s_]])
                    for m in range(8):
                        for (s, w, ic) in plain_all:
                            bk = next_bank()
                            mm_group(PS[bk][:, 0:w], bk, [(RING[:, s_, 128 * m:128 * m + 128], OT[:, s:s + w])], reads=[bRING[s_], bOT])
                            resid_add(bk, m, s, w, ic, 16)

                q_proj(0)
                for j in range(8):
                    if j + 1 < 8:
                        q_proj(j + 1)
                    attend(j)
                p.barrier()


        def ret_phase(L):
            import math
            LN16 = math.log(16.0)
            win = ret_w_in.rearrange("(k p) n -> p k n", p=128)
            with ExitStack() as ph:
                QTh = sb("QTh", [128, 2, SEQ], BF16, ph)
                KTh = sb("KTh", [128, 2, NCOL], BF16, ph)
                VH = sb("VH", [128, 18, 512], BF16, ph)
                GS = sb("GS", [128, 4, 512], BF16, ph)
                SQY = sb("SQY", [128, 4, 512], BF16, ph)
                Z = sb("Z", [128, 4, 512], BF16, ph)
                TMP = sb("rTMP", [128, 512], F32, ph)
                PT = [sb(f"rPT{i}", [128, 512], BF16, ph) for i in range(3)]
                MK = [sb(f"rMK{i}", [128, 512], F32, ph) for i in range(3)]
                QB = sb("rQB", [128, 512], BF16, ph)
                IOTA = sb("IOTA", [128, 512], F32, ph)
                RTAB = sb("RTAB", [128, 2, 96], F32, ph)
                RRr = sb("rRR", [128, 512], F32, ph)
                LG = sb("LG", [128, 8], F32, ph)
                NLG = sb("NLG", [128, 8], F32, ph)
                LT = sb("LT", [128, 8], F32, ph)
                BIAS = sb("BIAS", [128, 8], F32, ph)
                bQTh, bKTh, bVH, bGS, bSQY, bZ, bTMP, bQB, bIOTA, bRTAB, bRRr, bLG, bLT = [Buf() for _ in range(13)]
                bPT = [Buf() for _ in range(3)]
                bMK = [Buf() for _ in range(3)]
                bBIAS = [Buf() for _ in range(8)]
                cnt = {"pt": 0, "mk": 0, "bias": 0, "rb": 0}

                def rb():
                    b = 4 + cnt["rb"] % 4
                    cnt["rb"] += 1
                    return b

                def nmk():
                    i = cnt["mk"] % 3
                    cnt["mk"] += 1
                    return i

                def nbias():
                    i = cnt["bias"] % 8
                    cnt["bias"] += 1
                    return i

                p.dma("sp", lambda e: e.dma_start(out=IOTA[:], in_=iota_d), writes=[bIOTA])
                p.dma("sp", lambda e: e.dma_start(out=RTAB[:, 0, :], in_=rope_r[0]), writes=[bRTAB])
                p.dma("sp", lambda e: e.dma_start(out=RTAB[:, 1, :], in_=rope_r[1]), writes=[bRTAB])
                p.op("act", lambda e: e.activation(out=LT[:], in_=VEC[:, O_DEC:O_DEC + 8], func=AF.Exp, scale=-math.log(2.0)), reads=[bVEC], writes=[bLT])
                p.op("dve", lambda e: e.tensor_scalar(out=LG[:], in0=LT[:], scalar1=1.0 / 6, scalar2=0.2, op0=ALU.mult, op1=ALU.add), reads=[bLT], writes=[bLG])
                for cst in (0.25, 1.0 / 3, 0.5, 1.0):
                    p.op("dve", lambda e: e.tensor_tensor(out=LG[:], in0=LG[:], in1=LT[:], op=ALU.mult), reads=[bLG, bLT], writes=[bLG])
                    p.op("dve", lambda e, cst=cst: e.tensor_scalar(out=LG[:], in0=LG[:], scalar1=cst, scalar2=None, op0=ALU.add), reads=[bLG], writes=[bLG])
                p.op("dve", lambda e: e.tensor_tensor(out=NLG[:], in0=LG[:], in1=LT[:], op=ALU.mult), reads=[bLG, bLT], writes=[bLG])
                p.op("dve", lambda e: e.tensor_scalar(out=LG[:], in0=NLG[:], scalar1=-1.0, scalar2=None, op0=ALU.mult), reads=[bLG], writes=[bLG])

                def rope_post(bk, dst3, bdst, sg, blk):
                    if sg == 0:
                        c_ap = RTAB[:, 0, 8 * blk:8 * blk + 8].unsqueeze(2).broadcast_to([128, 8, 64])
                        s_ap = RTAB[:, 1, 8 * blk:8 * blk + 8].unsqueeze(2).broadcast_to([128, 8, 64])
                    else:
                        c_ap = RTAB[:, 0, 32:96].unsqueeze(1).broadcast_to([128, 8, 64])
                        s_ap = RTAB[:, 1, 32:96].unsqueeze(1).broadcast_to([128, 8, 64])
                    if RET_DBG == 21:
                        c_ap = IOTA[:].rearrange("p (r c) -> p r c", c=64)
                        s_ap = IOTA[:].rearrange("p (r c) -> p r c", c=64)
                    if RET_DBG == 22 and sg == 1:
                        c_ap = RTAB[:, 0, 0:8].unsqueeze(2).broadcast_to([128, 8, 64])
                        s_ap = RTAB[:, 1, 0:8].unsqueeze(2).broadcast_to([128, 8, 64])
                    p.op("act", lambda e: e.activation(out=QB[:], in_=PS[bk][:], func=AF.Copy), reads=[bPS[bk]], writes=[bQB])
                    b3 = next_bank()
                    mm_group(PS[b3][:], b3, [(CM[:, 2, :], QB[:])], reads=[bQB, bCM])
                    i1, i2 = nmk(), nmk()
                    v3 = lambda ap: ap.rearrange("p (r c) -> p r c", c=64)
                    p.op("dve", lambda e: e.tensor_tensor(out=v3(MK[i1][:]), in0=v3(PS[bk][:]), in1=c_ap, op=ALU.mult), reads=[bPS[bk], bRTAB], writes=[bMK[i1]])
                    p.op("dve", lambda e: e.tensor_tensor(out=v3(MK[i2][:]), in0=v3(PS[b3][:]), in1=s_ap, op=ALU.mult), reads=[bPS[b3], bRTAB], writes=[bMK[i2]])
                    p.op("dve", lambda e: e.tensor_tensor(out=dst3, in0=MK[i1][:], in1=MK[i2][:], op=ALU.add), reads=[bMK[i1], bMK[i2]], writes=[bdst])

                for h in range(RET_NH):
                    if RET_DBG == 1:
                        break
                    lgf, nlgf = LG[:, h:h + 1], NLG[:, h:h + 1]
                    lgb, nlgb = LG[:, 4 + h:5 + h], NLG[:, 4 + h:5 + h]
                    for which, (dstT, bdst, c0) in enumerate(((QTh, bQTh, 256 * h), (KTh, bKTh, 1024 + 256 * h))):
                        s_ = ring_load([(0, KC, 256, 256, win[:, :, c0:c0 + 256])])
                        wv = wview(s_, KC, 256)
                        for sg in range(2):
                            for blk, (s, w, ic) in enumerate(PLAIN_LAT):
                                bk = next_bank()
                                mm_group(PS[bk][:], bk, [(wv[:, kc, 128 * sg:128 * sg + 128], HT[:, kc, s:s + w]) for kc in range(KC)],
                                         reads=[bRING[s_]] + segs(bHT, s, s + w))
                                cbase = (s - LAT0) if which == 0 else s
                                rope_post(bk, dstT[:, sg, cbase:cbase + 512], bdst, sg, blk)
                            if which == 1:
                                (s, w, ic) = PLAIN_CTX[0]
                                bk = next_bank()
                                mm_group(PS[bk][:, 0:w], bk, [(wv[:, kc, 128 * sg:128 * sg + 128], HT[:, kc, s:s + w]) for kc in range(KC)],
                                         reads=[bRING[s_]] + segs(bHT, s, s + w))
                                p.op("act", lambda e, bk=bk, sg=sg, s=s, w=w: e.activation(out=KTh[:, sg, s:s + w], in_=PS[bk][:, 0:w], func=AF.Copy),
                                     reads=[bPS[bk]], writes=[bKTh])
                    if RET_DBG in (2, 21, 22):
                        break
                    sv = [ring_load([(0, KC, 256, 256, win[:, :, 2048 + 512 * h + 256 * i:2048 + 512 * h + 256 * i + 256])]) for i in range(2)]
                    for kt in range(18):
                        col = LAT0 + 128 * kt if kt < 16 else CTX0 + 128 * (kt - 16)
                        bk = next_bank()
                        for i in range(2):
                            wv = wview(sv[i], KC, 256)
                            mm_group(PS[bk][:, 256 * i:256 * i + 256], bk, [(HT[:, kc, col:col + 128], wv[:, kc, :]) for kc in range(KC)],
                                     reads=[bRING[sv[i]]] + segs(bHT, col, col + 128))
                        if kt % 2 == 0:
                            p.op("act", lambda e, kt=kt, bk=bk: e.activation(out=VH[:, kt, :], in_=PS[bk][:], func=AF.Copy), reads=[bPS[bk]], writes=[bVH])
                        else:
                            p.op("dve", lambda e, kt=kt, bk=bk: e.tensor_copy(out=VH[:, kt, :], in_=PS[bk][:]), reads=[bPS[bk]], writes=[bVH])

                    if RET_DBG == 3:
                        break
                    for qb, (s, w, ic) in enumerate(PLAIN_LAT):
                        q0 = s - LAT0
                        sg_ = [ring_load([(0, KC, 256, 256, win[:, :, 4096 + 512 * h + 256 * i:4096 + 512 * h + 256 * i + 256])]) for i in range(2)]
                        for m in range(4):
                            wv = wview(sg_[m // 2], KC, 256)
                            bk = rb()
                            mm_group(PS[bk][:], bk, [(wv[:, kc, 128 * (m % 2):128 * (m % 2) + 128], HT[:, kc, s:s + w]) for kc in range(KC)],
                                     reads=[bRING[sg_[m // 2]]] + segs(bHT, s, s + w))
                            p.op("act", lambda e, m=m, bk=bk: e.activation(out=GS[:, m, :], in_=PS[bk][:], func=AF.Silu), reads=[bPS[bk]], writes=[bGS])
                        pend = None
                        for kt in range(18):
                            kcol = LAT0 + 128 * kt if kt < 16 else CTX0 + 128 * (kt - 16)
                            bs = rb()
                            mm_group(PS[bs][:], bs, [(KTh[:, sg, kcol:kcol + 128], QTh[:, sg, q0:q0 + 512]) for sg in range(2)], reads=[bKTh, bQTh])
                            pi = cnt["pt"] % 3
                            cnt["pt"] += 1
                            if kt < 16:
                                off = 512 * qb - 128 * kt
                                if off >= 128 or off <= -512:
                                    sc, bsrc = (lgf, lgf) if off >= 128 else (nlgb, nlgb)
                                    bi = nbias()
                                    p.op("dve", lambda e, bi=bi, bsrc=bsrc, off=off: e.tensor_scalar(out=BIAS[:, bi:bi + 1], in0=bsrc, scalar1=float(off), scalar2=-LN16,
                                                                                                      op0=ALU.mult, op1=ALU.add), reads=[bLG], writes=[bBIAS[bi]])
                                    mi = nmk()
                                    p.op("act", lambda e, mi=mi, sc=sc, bi=bi: e.activation(out=MK[mi][:], in_=IOTA[:], func=AF.Exp, scale=sc, bias=BIAS[:, bi:bi + 1]),
                                         reads=[bIOTA, bLG, bBIAS[bi]], writes=[bMK[mi]])
                                    p.op("dve", lambda e, pi=pi, bs=bs, mi=mi: e.tensor_tensor(out=PT[pi][:], in0=PS[bs][:], in1=MK[mi][:], op=ALU.mult),
                                         reads=[bPS[bs], bMK[mi]], writes=[bPT[pi]])
                                else:
                                    m1, m2 = nmk(), nmk()
                                    bi = nbias()
                                    p.op("dve", lambda e, bi=bi: e.memset(BIAS[:, bi:bi + 1], -LN16), writes=[bBIAS[bi]])
                                    p.op("dve", lambda e, m1=m1, off=off: e.tensor_scalar(out=MK[m1][:], in0=IOTA[:], scalar1=float(off), scalar2=0.0, op0=ALU.add, op1=ALU.max),
                                         reads=[bIOTA], writes=[bMK[m1]])
                                    p.op("dve", lambda e, m1=m1, m2=m2, off=off: e.scalar_tensor_tensor(out=MK[m2][:], in0=IOTA[:], scalar=float(off), in1=MK[m1][:],
                                                                                                         op0=ALU.add, op1=ALU.subtract), reads=[bIOTA, bMK[m1]], writes=[bMK[m2]])
                                    p.op("act", lambda e, m1=m1, bi=bi: e.activation(out=MK[m1][:], in_=MK[m1][:], func=AF.Exp, scale=lgf, bias=BIAS[:, bi:bi + 1]),
                                         reads=[bMK[m1], bLG, bBIAS[bi]], writes=[bMK[m1]])
                                    p.op("act", lambda e, m2=m2: e.activation(out=MK[m2][:], in_=MK[m2][:], func=AF.Exp, scale=nlgb), reads=[bMK[m2], bLG], writes=[bMK[m2]])
                                    p.op("dve", lambda e, bs=bs, m1=m1: e.tensor_tensor(out=TMP[:], in0=PS[bs][:], in1=MK[m1][:], op=ALU.mult),
                                         reads=[bPS[bs], bMK[m1]], writes=[bTMP])
                                    p.op("dve", lambda e, pi=pi, m2=m2: e.tensor_tensor(out=PT[pi][:], in0=TMP[:], in1=MK[m2][:], op=ALU.mult),
                                         reads=[bTMP, bMK[m2]], writes=[bPT[pi]])
                            else:
                                a_ = kt - 16
                                m1, m2 = nmk(), nmk()
                                b1, b2 = nbias(), nbias()
                                o1 = float(512 * qb + 256 - 128 * a_)
                                o2 = float(2048 - 512 * qb + 128 * a_)
                                p.op("dve", lambda e, b1=b1, o1=o1: e.tensor_scalar(out=BIAS[:, b1:b1 + 1], in0=lgf, scalar1=o1, scalar2=-LN16, op0=ALU.mult, op1=ALU.add),
                                     reads=[bLG], writes=[bBIAS[b1]])
                                p.op("dve", lambda e, b2=b2, o2=o2: e.tensor_scalar(out=BIAS[:, b2:b2 + 1], in0=lgb, scalar1=o2, scalar2=-LN16, op0=ALU.mult, op1=ALU.add),
                                     reads=[bLG], writes=[bBIAS[b2]])
                                p.op("act", lambda e, m1=m1, b1=b1: e.activation(out=MK[m1][:], in_=IOTA[:], func=AF.Exp, scale=lgf, bias=BIAS[:, b1:b1 + 1]),
                                     reads=[bIOTA, bLG, bBIAS[b1]], writes=[bMK[m1]])
                                p.op("act", lambda e, m2=m2, b2=b2: e.activation(out=MK[m2][:], in_=IOTA[:], func=AF.Exp, scale=nlgb, bias=BIAS[:, b2:b2 + 1]),
                                     reads=[bIOTA, bLG, bBIAS[b2]], writes=[bMK[m2]])
                                p.op("dve", lambda e, m1=m1, m2=m2: e.tensor_tensor(out=MK[m1][:], in0=MK[m1][:], in1=MK[m2][:], op=ALU.add),
                                     reads=[bMK[m1], bMK[m2]], writes=[bMK[m1]])
                                p.op("dve", lambda e, pi=pi, bs=bs, m1=m1: e.tensor_tensor(out=PT[pi][:], in0=PS[bs][:], in1=MK[m1][:], op=ALU.mult),
                                     reads=[bPS[bs], bMK[m1]], writes=[bPT[pi]])
                            if pend is not None:
                                for dv in range(4):
                                    p.op("pe", lambda e, dv=dv, pend=pend: e.matmul(PS[dv][:], lhsT=VH[:, pend[0], 128 * dv:128 * dv + 128], rhs=PT[pend[1]][:],
                                                                                     start=(pend[0] == 0), stop=False),
                                         reads=[bVH, bPT[pend[1]]], writes=[bPS[dv]], inc=(dv == 3))
                            pend = (kt, pi)
                        for dv in range(4):
                            p.op("pe", lambda e, dv=dv, pend=pend: e.matmul(PS[dv][:], lhsT=VH[:, pend[0], 128 * dv:128 * dv + 128], rhs=PT[pend[1]][:],
                                                                             start=False, stop=True),
                                 reads=[bVH, bPT[pend[1]]], writes=[bPS[dv]], inc=True)
                        for dv in range(4):
                            p.op("act", lambda e, dv=dv: e.activation(out=SQY[:, dv, :], in_=PS[dv][:], func=AF.Square), reads=[bPS[dv]], writes=[bSQY])
                        bss = rb()
                        mm_group(PS[bss][:], bss, [(ONES[:], SQY[:, dv, :]) for dv in range(4)], reads=[bSQY, bONES])
                        p.op("act", lambda e, bss=bss: e.activation(out=RRr[:], in_=PS[bss][:], func=AF.Sqrt, bias=EPSV[:, 1:2], scale=1.0 / 512),
                             reads=[bPS[bss], bEPS], writes=[bRRr])
                        p.op("dve", lambda e: e.reciprocal(out=RRr[:], in_=RRr[:]), reads=[bRRr], writes=[bRRr])
                        for dv in range(4):
                            p.op("dve", lambda e, dv=dv: e.tensor_tensor(out=TMP[:], in0=PS[dv][:], in1=RRr[:], op=ALU.mult), reads=[bPS[dv], bRRr], writes=[bTMP])
                            p.op("dve", lambda e, dv=dv: e.tensor_tensor(out=Z[:, dv, :], in0=TMP[:], in1=GS[:, dv, :], op=ALU.mult), reads=[bTMP, bGS], writes=[bZ])
                        wo = ret_w_out[512 * h:512 * h + 512, :].rearrange("(k p) n -> p k n", p=128)
                        for pj in range(2):
                            s_ = ring_load([(0, 4, 512, 512, wo[:, :, 512 * pj:512 * pj + 512])])
                            wv = wview(s_, 4, 512)
                            for mt in range(4):
                                bk = rb()
                                mm_group(PS[bk][:], bk, [(wv[:, dv, 128 * mt:128 * mt + 128], Z[:, dv, :]) for dv in range(4)], reads=[bRING[s_], bZ])
                                resid_add(bk, 4 * pj + mt, s, w, 0, 16)
                p.barrier()

        for L in layers:
            kind = L % 3
            has_ctx = L <= 1
            ctx_in = L <= 2
            plain = PLAIN_LAT + (PLAIN_CTX if has_ctx else [])
            halo = HALO_LAT + (HALO_CTX if has_ctx else [])
            ada_phase(L)
            norm_phase(0, PLAIN_LAT + (PLAIN_CTX if ctx_in else []))
            if kind == 0:
                conv_phase(L, halo, plain)
            elif kind == 1:
                attn_phase(L)
            else:
                ret_phase(L)
            if not DBG_SKIP_FFN:
                norm_phase(1, plain)
                ffn_phase(L, halo, plain)

        with ExitStack() as ph:
            SQ = [sb(f"fSQ{i}", [128, KC, 128], BF16, ph) for i in range(2)]
            RR = [sb(f"fRR{i}", [128, 128], F32, ph) for i in range(2)]
            YT = [sb(f"fYT{i}", [128, KC, 128], F32, ph) for i in range(2)]
            OS = [sb(f"fOS{i}", [128, D], F32, ph) for i in range(2)]
            bSQ, bRR, bYT, bOS = [Buf(), Buf()], [Buf(), Buf()], [Buf(), Buf()], [Buf(), Buf()]
            GF = sb("GF32", [128, KC], F32, ph)
            bGF = Buf()
            p.op("dve", lambda e: e.tensor_scalar(out=GF[:], in0=VEC[:, O_FIN:O_FIN + 8], scalar1=32.0, scalar2=None, op0=ALU.mult),
                 reads=[bVEC], writes=[bGF])
            for t in range(16):
                s = LAT0 + 128 * t
                sq, rr, yt, os_ = SQ[t % 2], RR[t % 2], YT[t % 2], OS[t % 2]
                bsq, brr, byt, bos = bSQ[t % 2], bRR[t % 2], bYT[t % 2], bOS[t % 2]
                xb = segs(bXT, s, s + 128)
                p.op("act", lambda e, sq=sq, s=s: e.activation(out=sq[:], in_=XT[:, :, s:s + 128], func=AF.Square), reads=xb, writes=[bsq])
                bk = next_bank()
                mm_group(PS[bk][:, 0:128], bk, [(ONES[:], sq[:, kc, :]) for kc in range(KC)], reads=[bsq, bONES])
                p.op("act", lambda e, rr=rr, bk=bk: e.activation(out=rr[:], in_=PS[bk][:, 0:128], func=AF.Sqrt, bias=EPSV[:, 0:1], scale=1.0),
                     reads=[bPS[bk], bEPS], writes=[brr])
                p.op("dve", lambda e, rr=rr: e.reciprocal(out=rr[:], in_=rr[:]), reads=[brr], writes=[brr])
                p.op("dve", lambda e, yt=yt, rr=rr, s=s: e.tensor_tensor(out=yt[:], in0=XT[:, :, s:s + 128],
                                                                          in1=rr[:].unsqueeze(1).broadcast_to([128, KC, 128]), op=ALU.mult),
                     reads=xb + [brr], writes=[byt])
                p.op("dve", lambda e, yt=yt: e.tensor_tensor(out=yt[:], in0=yt[:], in1=GF[:].unsqueeze(2).broadcast_to([128, KC, 128]), op=ALU.mult),
                     reads=[byt, bGF], writes=[byt])
                for half in range(2):
                    bk = next_bank()
                    for q in range(4):
                        kc = 4 * half + q
                        p.op("pe", lambda e, bk=bk, q=q, kc=kc, yt=yt: e.transpose(out=PS[bk][:, 128 * q:128 * q + 128], in_=yt[:, kc, :], identity=IDF[:]),
                             reads=[byt, bIDF], writes=[bPS[bk]], inc=(q == 3))
                    if half == 0:
                        p.op("act", lambda e, os_=os_, bk=bk: e.activation(out=os_[:, 0:512], in_=PS[bk][:], func=AF.Copy), reads=[bPS[bk]], writes=[bos])
                    else:
                        p.op("dve", lambda e, os_=os_, bk=bk: e.tensor_copy(out=os_[:, 512:1024], in_=PS[bk][:]), reads=[bPS[bk]], writes=[bos])
                p.dma("sp", lambda e, os_=os_, t=t: e.dma_start(out=out_d[128 * t:128 * t + 128, :], in_=os_[:]), reads=[bos], owner=bos)
            p.barrier()
        print(f"[build] ops={p.n_op} waits={p.n_wait} cnt={p.cnt}")
    return nc


def _cols(v):
    v = np.asarray(v, np.float32).reshape(-1, 128)
    return np.ascontiguousarray(v.T)


def _rope_tables():
    rows = np.repeat(np.arange(32, dtype=np.float32), 64)
    cols = np.tile(np.arange(64, dtype=np.float32), 32)
    q = 16
    inv = (10000.0 ** (-np.arange(q, dtype=np.float32) / q)).astype(np.float32)
    ca = np.zeros((128, SEQ), np.float32)
    sa = np.zeros((128, SEQ), np.float32)
    for pp in range(128):
        d = pp % 64
        seg, half, i = d // 32, (d % 32) // 16, d % 16
        ang = (rows if seg == 0 else cols) * inv[i]
        ca[pp] = np.cos(ang)
        sa[pp] = np.sin(ang) * (-1.0 if half == 0 else 1.0)
    q = 64
    inv = (10000.0 ** (-np.arange(q, dtype=np.float32) / q)).astype(np.float32)
    cr = np.zeros((128, 96), np.float32)
    sr = np.zeros((128, 96), np.float32)
    for pp in range(128):
        half, i = pp // 64, pp % 64
        sgn = -1.0 if half == 0 else 1.0
        a_row = np.arange(32, dtype=np.float32) * inv[i]
        a_col = np.arange(64, dtype=np.float32) * inv[i]
        cr[pp, 0:32] = np.cos(a_row)
        cr[pp, 32:96] = np.cos(a_col)
        sr[pp, 0:32] = np.sin(a_row) * sgn
        sr[pp, 32:96] = np.sin(a_col) * sgn
    return np.stack([ca, sa]), np.stack([cr, sr])


def _pack(inputs, layers):
    f = lambda a: np.ascontiguousarray(np.asarray(a, np.float32))
    vec = np.zeros((128, NV), np.float32)
    for L in range(4):
        b = L * LW
        vec[:, b + O_ADAB:b + O_ADAB + 48] = _cols(inputs["ada_b"][L])
        vec[:, b + O_GMIX:b + O_GMIX + 8] = _cols(inputs["norm_mix_g"][L])
        vec[:, b + O_GFFN:b + O_GFFN + 8] = _cols(inputs["norm_ffn_g"][L])
        for d in range(3):
            vec[:, b + O_FK + 44 * d:b + O_FK + 44 * d + 44] = _cols(inputs["ffn_conv_k"][L][d])
        vec[:, b + O_FB:b + O_FB + 44] = _cols(inputs["ffn_conv_b"][L])
        if L % 3 == 0:
            for d in range(3):
                vec[:, b + O_MK + 8 * d:b + O_MK + 8 * d + 8] = _cols(inputs["conv_k"][L // 3][d])
    vec[:, O_FIN:O_FIN + 8] = _cols(inputs["final_norm_g"])
    vec[:, O_QG] = np.tile(np.asarray(inputs["attn_q_norm_g"][0], np.float32), 2)
    vec[:, O_KG] = np.tile(np.asarray(inputs["attn_k_norm_g"][0], np.float32), 2)
    vec[:, O_DEC:O_DEC + 8] = np.broadcast_to(np.asarray(inputs["ret_decay"][0], np.float32).reshape(1, 8), (128, 8))
    ra, rr = _rope_tables()
    iota = (np.arange(512, dtype=np.float32)[None, :] - np.arange(128, dtype=np.float32)[:, None])
    wq = f(inputs["attn_w_qkv"][0])
    pidx = np.arange(128)
    bd = (pidx[:, None] // 64 == pidx[None, :] // 64).astype(np.float32)
    pa = ((pidx[:, None] ^ 16) == pidx[None, :]).astype(np.float32)
    pr = ((pidx[:, None] ^ 64) == pidx[None, :]).astype(np.float32)
    cmat = np.ascontiguousarray(np.concatenate([bd, pa, pr], axis=1))
    shared = {
        "vec": vec, "ada_w": f(inputs["ada_w"]), "conv_w_in": f(inputs["conv_w_in"]), "conv_w_out": f(inputs["conv_w_out"]),
        "attn_wqkv": wq, "attn_w_out": f(inputs["attn_w_out"][0]), "ret_w_in": f(inputs["ret_w_in"][0]),
        "ret_w_out": f(inputs["ret_w_out"][0]), "ffn_w_up": f(inputs["ffn_w_up"]), "ffn_w_down": f(inputs["ffn_w_down"]),
        "rope_a": ra, "rope_r": rr, "iota_d": np.ascontiguousarray(iota), "cmat": cmat,
    }
    maps = []
    for b in range(8):
        cv = np.stack([_cols(inputs["c"][b]), _cols(inputs["c_ctx"])], axis=-1).reshape(128, 16)
        m = dict(shared)
        m["x"] = f(inputs["x"][b])
        m["ctx"] = f(inputs["ctx"][b])
        m["cvec"] = np.ascontiguousarray(cv)
        maps.append(m)
    return maps


_NC_CACHE = {}


def kernel(_layers=(0, 1, 2, 3), _cores=8, **inputs):
    key = tuple(_layers)
    if key not in _NC_CACHE:
        _NC_CACHE[key] = build(key)
    nc = _NC_CACHE[key]
    maps = _pack(inputs, key)[:_cores]
    res = run_bass_kernel_spmd(nc, maps, core_ids=list(range(_cores)))
    out = np.stack([np.asarray(r["out"], np.float32) for r in res.results], axis=0)
    return out
```

```python
import numpy as np
from contextlib import ExitStack
import concourse.bass as bass
import concourse.mybir as mybir
from concourse.bass_utils import run_bass_kernel_spmd

F32 = mybir.dt.float32
BF16 = mybir.dt.bfloat16
ALU = mybir.AluOpType
AF = mybir.ActivationFunctionType

ENGS = ("pe", "act", "dve", "pool", "sp")

D = 1024
KC = 8
SEQ = 2048
CTXL = 256
NCOL = 2308
LAT0 = 1
CTX0 = 2051
DFF = 2816
NPAIR = 22
EPS = 1e-6
SEG = [0, 513, 1025, 1537, 2050, NCOL]
PLAIN_LAT = [(1 + 512 * b, 512, 0) for b in range(4)]
PLAIN_CTX = [(CTX0, 256, 1)]
HALO_LAT = [(410 * b, min(412, 2050 - 410 * b), 0) for b in range(5)]
HALO_CTX = [(2050, 258, 1)]

LW = 264
O_ADAB, O_GMIX, O_GFFN, O_FK, O_FB, O_MK = 0, 48, 56, 64, 196, 240
O_FIN = 4 * LW
O_QG = O_FIN + 8
O_KG = O_QG + 1
O_DEC = O_KG + 1
NV = O_DEC + 8

RET_NH = 4
RET_DBG = 0
FFN_MUL_ENG = "pool"
DBG_SKIP_FFN = False
RSLOT = 2048
NSLOT = 5


class Buf:
    __slots__ = ("name", "w", "r", "dsem", "excl")

    def __init__(self, name="", excl=False):
        self.name = name
        self.w = None
        self.r = {}
        self.dsem = None
        self.excl = excl


class Prog:
    def __init__(self, nc, eng_sems, dma_sems):
        self.nc = nc
        self.eng = {"pe": nc.tensor, "act": nc.scalar, "dve": nc.vector, "pool": nc.gpsimd, "sp": nc.sync}
        self.cnt = {e: 0 for e in ENGS}
        self.semh = dict(eng_sems)
        self.dma_free = list(dma_sems)
        self.dcnt = {}
        self.seen = {e: {} for e in ENGS}
        self.n_op = 0
        self.n_wait = 0

    def _deps(self, reads, writes, eng=None):
        deps = {}
        for b in reads:
            t = b.w
            if t is not None and deps.get(t[0], 0) < t[1]:
                deps[t[0]] = t[1]
            if b.excl:
                for k, v in b.r.items():
                    if k != eng and deps.get(k, 0) < v:
                        deps[k] = v
        for b in writes:
            t = b.w
            if t is not None and deps.get(t[0], 0) < t[1]:
                deps[t[0]] = t[1]
            for k, v in b.r.items():
                if deps.get(k, 0) < v:
                    deps[k] = v
        return deps

    def _waits(self, eng, deps):
        seen = self.seen[eng]
        for k, v in deps.items():
            if k == "pe" and eng == "pe":
                continue
            if seen.get(k, 0) < v:
                seen[k] = v
                self.eng[eng].wait_ge(self.semh[k], v)
                self.n_wait += 1

    def _mark(self, tok, reads, writes):
        k, v = tok
        for b in writes:
            b.w = tok
            b.r = {}
        for b in reads:
            if b.r.get(k, 0) < v:
                b.r[k] = v

    def op(self, eng, fn, reads=(), writes=(), inc=True):
        self._waits(eng, self._deps(reads, writes, eng))
        tok = (eng, self.cnt[eng] + 1)
        ins = fn(self.eng[eng])
        if inc:
            self.cnt[eng] += 1
            ins.then_inc(self.semh[eng], 1)
        self.n_op += 1
        self._mark(tok, reads, writes)
        return tok

    def _dsem(self, b):
        if b.dsem is None:
            h = self.dma_free.pop()
            key = ("dma", len(self.dcnt))
            self.semh[key] = h
            self.dcnt[key] = 0
            b.dsem = key
        return b.dsem

    def dma(self, queue, fn, reads=(), writes=(), owner=None):
        owner = owner or (writes[0] if writes else reads[0])
        key = self._dsem(owner)
        self._waits(queue, self._deps(reads, writes))
        self.dcnt[key] += 16
        tok = (key, self.dcnt[key])
        fn(self.eng[queue]).then_inc(self.semh[key], 16)
        self._mark(tok, reads, writes)
        return tok

    def barrier(self):
        deps = {e: self.cnt[e] for e in ("pe", "act", "dve", "pool") if self.cnt[e] > 0}
        for k, v in self.dcnt.items():
            if v > 0:
                deps[k] = v
        for e in ENGS:
            d = dict(deps)
            self._waits(e, d)


def segs(bufs, s, e):
    out = []
    for i in range(len(SEG) - 1):
        if s < SEG[i + 1] and e > SEG[i]:
            out.append(bufs[i])
    return out


def build(layers=(0, 1, 2, 3)):
    nc = bass.Bass("TRN2", target_bir_lowering=False)

    def din(name, shape):
        return nc.dram_tensor(name, list(shape), F32, kind="ExternalInput").ap()

    x_d = din("x", [SEQ, D])
    ctx_d = din("ctx", [CTXL, D])
    cvec_d = din("cvec", [128, KC * 2])
    vec_d = din("vec", [128, NV])
    ada_w = din("ada_w", [4, D, 6 * D])
    conv_w_in = din("conv_w_in", [2, D, 3 * D])
    conv_w_out = din("conv_w_out", [2, D, D])
    attn_wqkv = din("attn_wqkv", [D, 1536])
    attn_w_out = din("attn_w_out", [D, D])
    ret_w_in = din("ret_w_in", [D, 6 * D])
    ret_w_out = din("ret_w_out", [2 * D, D])
    ffn_w_up = din("ffn_w_up", [4, D, 2 * DFF])
    ffn_w_down = din("ffn_w_down", [4, DFF, D])
    rope_a = din("rope_a", [2, 128, SEQ])
    rope_r = din("rope_r", [2, 128, 96])
    iota_d = din("iota_d", [128, 512])
    cmat_d = din("cmat", [128, 3 * 128])
    out_d = nc.dram_tensor("out", [SEQ, D], F32, kind="ExternalOutput").ap()

    with ExitStack() as es:
        uniq = [0]

        def sb(name, shape, dt, stack=es):
            uniq[0] += 1
            return stack.enter_context(nc.sbuf_tensor(f"{name}_{uniq[0]}", list(shape), dt))

        XT = sb("XT", [128, KC, NCOL], F32)
        HT = sb("HT", [128, KC, NCOL], BF16)
        RING = sb("RING", [128, NSLOT, RSLOT], BF16)
        VEC = sb("VEC", [128, NV], F32)
        CV = sb("CVEC", [128, KC, 2], F32)
        SC = sb("SC", [128, KC, 2], BF16)
        MODS = [sb(f"MOD{i}", [128, 48, 2], F32) for i in range(2)]
        AMS = [sb(f"AM{i}", [128, 2, KC, 2], F32) for i in range(2)]
        ONES = sb("ONES", [128, 128], BF16)
        IDF = sb("IDF", [128, 128], F32)
        EPSV = sb("EPSV", [128, 2], F32)
        CM = sb("CM", [128, 3, 128], BF16)
        PS = [es.enter_context(nc.psum_tensor(f"PS{i}", [128, 512], F32)) for i in range(8)]

        sems = {e: es.enter_context(nc.semaphore("s_" + e)) for e in ("pe", "act", "dve", "pool")}
        dsems = [es.enter_context(nc.semaphore(f"d{i}")) for i in range(60)]
        es.enter_context(nc.Block())
        p = Prog(nc, sems, dsems)

        bXT = [Buf(f"XT{i}") for i in range(5)]
        bHT = [Buf(f"HT{i}") for i in range(5)]
        bRING = [Buf(f"R{i}") for i in range(NSLOT)]
        bPS = [Buf(f"PS{i}", excl=True) for i in range(8)]
        bVEC, bCV, bSC, bONES, bIDF, bEPS, bCM = [Buf(n) for n in "VEC CV SC ONES IDF EPS CM".split()]
        bMODS = [Buf("MOD0"), Buf("MOD1")]
        bAMS = [Buf("AM0"), Buf("AM1")]
        cur = {}

        def set_layer(L):
            cur["MOD"], cur["AM"], cur["bMOD"], cur["bAM"] = MODS[L % 2], AMS[L % 2], bMODS[L % 2], bAMS[L % 2]
        st = {"slot": 0, "bank": 0, "reserved": set(), "ring_n": NSLOT, "side": 0}

        def next_bank():
            while True:
                b = st["bank"]
                st["bank"] = (b + 1) % 8
                if b not in st["reserved"]:
                    return b

        def ring_load(parts, side=False):
            if side:
                s = NSLOT - 2 + st["side"] % 2
                st["side"] += 1
            else:
                s = st["slot"]
                st["slot"] = (s + 1) % st["ring_n"]
            for (c0, kcn, n, tot, src) in parts:
                dst = RING[:, s, 0:kcn * tot].rearrange("p (k n) -> p k n", n=tot)[:, :, c0:c0 + n]
                p.dma("pool", lambda e, dst=dst, src=src: e.dma_start(out=dst, in_=src), writes=[bRING[s]])
            return s

        def wview(s, kcn, tot):
            return RING[:, s, 0:kcn * tot].rearrange("p (k n) -> p k n", n=tot)

        def mm_group(out_ap, bank, pairs, reads):
            n = len(pairs)
            for i, (l, r) in enumerate(pairs):
                p.op("pe", lambda e, l=l, r=r, i=i: e.matmul(out_ap, lhsT=l, rhs=r, start=(i == 0), stop=(i == n - 1)),
                     reads=reads, writes=[bPS[bank]], inc=(i == n - 1))

        p.dma("sp", lambda e: e.dma_start(out=VEC[:], in_=vec_d), writes=[bVEC])
        p.dma("sp", lambda e: e.dma_start(out=CV[:].rearrange("p k t -> p (k t)"), in_=cvec_d), writes=[bCV])
        p.dma("pool", lambda e: e.dma_start(out=CM[:].rearrange("p a b -> p (a b)"), in_=cmat_d), writes=[bCM])
        p.op("dve", lambda e: e.memset(ONES[:], 1.0), writes=[bONES])
        p.op("dve", lambda e: e.memset(EPSV[:, 0:1], 1024.0 * EPS), writes=[bEPS])
        p.op("dve", lambda e: e.memset(EPSV[:, 1:2], EPS), reads=[bEPS], writes=[bEPS])
        p.op("pool", lambda e: e.memset(IDF[:], 0.0), writes=[bIDF])
        p.op("pool", lambda e: e.affine_select(out=IDF[:], in_=IDF[:], pattern=[[-1, 128]], compare_op=ALU.not_equal,
                                               fill=1.0, base=0, channel_multiplier=1), reads=[bIDF], writes=[bIDF])
        p.op("pool", lambda e: e.memset(HT[:].rearrange("p k n -> p (k n)"), 0.0), writes=bHT)
        p.op("pool", lambda e: e.memset(XT[:].rearrange("p k n -> p (k n)"), 0.0), writes=bXT)
        p.op("act", lambda e: e.activation(out=SC[:], in_=CV[:], func=AF.Silu), reads=[bCV], writes=[bSC])

        with ExitStack() as ph:
            XS = [sb(f"XS{i}", [128, D], F32, ph) for i in range(2)]
            bXS = [Buf("XS0"), Buf("XS1")]
            for t in range(18):
                src = x_d[128 * t:128 * t + 128, :] if t < 16 else ctx_d[128 * (t - 16):128 * (t - 16) + 128, :]
                col = LAT0 + 128 * t if t < 16 else CTX0 + 128 * (t - 16)
                xs, bxs = XS[t % 2], bXS[t % 2]
                p.dma("sp", lambda e, xs=xs, src=src: e.dma_start(out=xs[:], in_=src), writes=[bxs])
                for half in range(2):
                    bk = next_bank()
                    for q in range(4):
                        kc = half * 4 + q
                        p.op("pe", lambda e, bk=bk, q=q, kc=kc, xs=xs: e.transpose(out=PS[bk][:, 128 * q:128 * q + 128],
                                                                                     in_=xs[:, 128 * kc:128 * kc + 128], identity=IDF[:]),
                             reads=[bxs, bIDF], writes=[bPS[bk]], inc=(q == 3))
                    eng = "act" if half == 0 else "dve"
                    dst = XT[:, half * 4:half * 4 + 4, col:col + 128]
                    srcp = PS[bk][:].rearrange("p (k n) -> p k n", n=128)
                    if eng == "act":
                        p.op("act", lambda e, dst=dst, srcp=srcp: e.activation(out=dst, in_=srcp, func=AF.Copy),
                             reads=[bPS[bk]], writes=segs(bXT, col, col + 128))
                    else:
                        p.op("dve", lambda e, dst=dst, srcp=srcp: e.tensor_copy(out=dst, in_=srcp),
                             reads=[bPS[bk]], writes=segs(bXT, col, col + 128))
            p.barrier()

        def ada_gen(L):
            MOD, AM, bMOD, bAM = MODS[L % 2], AMS[L % 2], bMODS[L % 2], bAMS[L % 2]
            bk = next_bank()
            st["reserved"].add(bk)
            psa = PS[bk]
            wsrc = ada_w[L].rearrange("(k p) n -> p k n", p=128)
            nxt_s = ring_load([(0, KC, 256, 256, wsrc[:, :, 0:256])], side=True)
            for pi in range(24):
                s = nxt_s
                if pi + 1 < 24:
                    nxt_s = ring_load([(0, KC, 256, 256, wsrc[:, :, 256 * (pi + 1):256 * (pi + 1) + 256])], side=True)
                wv = wview(s, KC, 256)
                for mt in range(2):
                    m = 2 * pi + mt
                    mm_group(psa[:, 2 * m:2 * m + 2], bk,
                             [(wv[:, kc, 128 * mt:128 * mt + 128], SC[:, kc, :]) for kc in range(KC)],
                             reads=[bRING[s], bSC])
                yield
            p.op("dve", lambda e: e.tensor_tensor(out=MOD[:], in0=psa[:, 0:96].rearrange("p (m t) -> p m t", t=2),
                                                  in1=VEC[:, L * LW + O_ADAB:L * LW + O_ADAB + 48].unsqueeze(2).broadcast_to([128, 48, 2]),
                                                  op=ALU.add), reads=[bPS[bk], bVEC], writes=[bMOD])
            st["reserved"].discard(bk)
            for which, (osc, og) in enumerate(((8, O_GMIX), (32, O_GFFN))):
                p.op("dve", lambda e, which=which, osc=osc: e.tensor_scalar(out=AM[:, which], in0=MOD[:, osc:osc + 8, :], scalar1=1.0, scalar2=32.0,
                                                                              op0=ALU.add, op1=ALU.mult), reads=[bMOD, bAM], writes=[bAM])
                p.op("dve", lambda e, which=which, og=og: e.tensor_tensor(out=AM[:, which], in0=AM[:, which],
                                                                            in1=VEC[:, L * LW + og:L * LW + og + 8].unsqueeze(2).broadcast_to([128, 8, 2]),
                                                                            op=ALU.mult), reads=[bAM, bVEC], writes=[bAM])

        def norm_phase(which, blocks):
            osh = 0 if which == 0 else 24
            with ExitStack() as ph:
                SQ = [sb(f"SQ{i}", [128, KC, 512], BF16, ph) for i in range(2)]
                RR = [sb(f"RR{i}", [128, 512], F32, ph) for i in range(2)]
                TM = [sb(f"TM{i}", [128, 4, 512], F32, ph) for i in range(2)]
                bSQ = [Buf(), Buf()]
                bRR = [Buf(), Buf()]
                bTM = [Buf(), Buf()]
                for bi, (s, w, ic) in enumerate(blocks):
                    sq, bsq, rr, brr = SQ[bi % 2], bSQ[bi % 2], RR[bi % 2], bRR[bi % 2]
                    xb = segs(bXT, s, s + w)
                    hb = segs(bHT, s, s + w)
                    p.op("act", lambda e, sq=sq, s=s, w=w: e.activation(out=sq[:, :, 0:w], in_=XT[:, :, s:s + w], func=AF.Square),
                         reads=xb, writes=[bsq])
                    bk = next_bank()
                    mm_group(PS[bk][:, 0:w], bk, [(ONES[:], sq[:, kc, 0:w]) for kc in range(KC)], reads=[bsq, bONES])
                    p.op("act", lambda e, rr=rr, bk=bk, w=w: e.activation(out=rr[:, 0:w], in_=PS[bk][:, 0:w], func=AF.Sqrt, bias=EPSV[:, 0:1], scale=1.0),
                         reads=[bPS[bk], bEPS], writes=[brr])
                    p.op("dve", lambda e, rr=rr, w=w: e.reciprocal(out=rr[:, 0:w], in_=rr[:, 0:w]), reads=[brr], writes=[brr])
                    for half in range(2):
                        tm, btm = TM[half], bTM[half]
                        p.op("dve", lambda e, tm=tm, half=half, s=s, w=w, rr=rr: e.tensor_tensor(
                            out=tm[:, :, 0:w], in0=XT[:, 4 * half:4 * half + 4, s:s + w],
                            in1=rr[:, 0:w].unsqueeze(1).broadcast_to([128, 4, w]), op=ALU.mult),
                            reads=xb + [brr], writes=[btm])
                        for q in range(4):
                            kc = 4 * half + q
                            a_ap = cur["AM"][:, which, kc, ic:ic + 1]
                            b_ap = cur["MOD"][:, osh + kc, ic:ic + 1]
                            if q % 2 == 0:
                                p.op("dve", lambda e, tm=tm, q=q, kc=kc, s=s, w=w, a_ap=a_ap, b_ap=b_ap: e.tensor_scalar(
                                    out=HT[:, kc, s:s + w], in0=tm[:, q, 0:w], scalar1=a_ap, scalar2=b_ap, op0=ALU.mult, op1=ALU.add),
                                    reads=[btm, cur["bAM"], cur["bMOD"]], writes=hb)
                            else:
                                p.op("act", lambda e, tm=tm, q=q, kc=kc, s=s, w=w, a_ap=a_ap, b_ap=b_ap: e.activation(
                                    out=HT[:, kc, s:s + w], in_=tm[:, q, 0:w], func=AF.Identity, scale=a_ap, bias=b_ap),
                                    reads=[btm, cur["bAM"], cur["bMOD"]], writes=hb)
                p.barrier()

        def resid_add(bk, m, s, w, ic, gcol):
            g_ap = cur["MOD"][:, gcol + m, ic:ic + 1]
            xb = segs(bXT, s, s + w)
            p.op("dve", lambda e: e.scalar_tensor_tensor(out=XT[:, m, s:s + w], in0=PS[bk][:, 0:w], scalar=g_ap,
                                                         in1=XT[:, m, s:s + w], op0=ALU.mult, op1=ALU.add),
                 reads=[bPS[bk], cur["bMOD"]] + xb, writes=xb)

        def ffn_phase(L, halo, plain, side):
            vb = L * LW
            wup = ffn_w_up[L].rearrange("(k p) n -> p k n", p=128)
            groups = [list(range(0, 6)), list(range(6, 12)), list(range(12, 17)), list(range(17, 22))]
            ncols = 256
            loads = []
            for grp in groups:
                for i in grp:
                    loads.append([(0, KC, 128, 256, wup[:, :, 128 * i:128 * i + 128]),
                                  (128, KC, 128, 256, wup[:, :, DFF + 128 * i:DFF + 128 * i + 128])])
                gp, i0 = len(grp), grp[0]
                wdn = ffn_w_down[L][128 * i0:128 * (i0 + gp), :].rearrange("(k p) n -> p k n", p=128)
                for pj in range(1024 // ncols):
                    loads.append([(0, gp, ncols, ncols, wdn[:, :, ncols * pj:ncols * pj + ncols])])
            slots = {}
            PF = 2
            st["ring_n"], st["slot"] = NSLOT - 2, 0

            def get(idx):
                for k in range(idx, min(idx + PF + 1, len(loads))):
                    if k not in slots:
                        slots[k] = ring_load(loads[k])
                return slots[idx]

            with ExitStack() as ph:
                FT = sb("FT", [128, 6, NCOL], BF16, ph)
                bFT = [Buf(f"FT{i}") for i in range(5)]
                VA = [sb(f"VA{i}", [128, 412], F32, ph) for i in range(3)]
                GA = [sb(f"GA{i}", [128, 412], F32, ph) for i in range(3)]
                SG = [sb(f"SG{i}", [128, 412], F32, ph) for i in range(3)]
                bVA, bGA, bSG = [Buf() for _ in range(3)], [Buf() for _ in range(3)], [Buf() for _ in range(3)]
                it = 0
                li = 0
                pend_s = [None]

                def kcol(d, ch):
                    return VEC[:, vb + O_FK + d * 44 + ch:vb + O_FK + d * 44 + ch + 1]

                def stage_s():
                    if pend_s[0] is None:
                        return
                    (sg, ga, va, bsg, bga, bva, n, il, s) = pend_s[0]
                    pend_s[0] = None
                    p.op("act", lambda e: e.activation(out=sg[:, 0:n], in_=ga[:, 0:n], func=AF.Silu), reads=[bga], writes=[bsg])
                    p.op(FFN_MUL_ENG, lambda e: e.tensor_tensor(out=FT[:, il, s + 1:s + 1 + n], in0=va[:, 0:n], in1=sg[:, 0:n], op=ALU.mult),
                         reads=[bva, bsg], writes=segs(bFT, s + 1, s + 1 + n))

                for grp in groups:
                    gp = len(grp)
                    for il, i in enumerate(grp):
                        s_ = get(li)
                        li += 1
                        wv = wview(s_, KC, 256)
                        for (s, w, ic) in halo:
                            ba, bg = next_bank(), next_bank()
                            hb = segs(bHT, s, s + w)
                            mm_group(PS[bg][:, 0:w], bg, [(wv[:, kc, 128:256], HT[:, kc, s:s + w]) for kc in range(KC)], reads=[bRING[s_]] + hb)
                            mm_group(PS[ba][:, 0:w], ba, [(wv[:, kc, 0:128], HT[:, kc, s:s + w]) for kc in range(KC)], reads=[bRING[s_]] + hb)
                            va, ga, sg = VA[it % 3], GA[it % 3], SG[it % 3]
                            bva, bga, bsg = bVA[it % 3], bGA[it % 3], bSG[it % 3]
                            it += 1
                            n = w - 2
                            rows = ((ga, bga, bg, NPAIR + i), (va, bva, ba, i))
                            for (acc, bacc, bkk, ch) in rows:
                                p.op("act", lambda e, acc=acc, bkk=bkk, ch=ch: e.activation(
                                    out=acc[:, 0:n], in_=PS[bkk][:, 1:1 + n], func=AF.Identity, scale=kcol(1, ch),
                                    bias=VEC[:, vb + O_FB + ch:vb + O_FB + ch + 1]), reads=[bPS[bkk], bVEC], writes=[bacc])
                            stage_s()
                            for d in (0, 2):
                                for (acc, bacc, bkk, ch) in rows:
                                    p.op("dve", lambda e, acc=acc, bkk=bkk, ch=ch, d=d: e.scalar_tensor_tensor(
                                        out=acc[:, 0:n], in0=PS[bkk][:, d:d + n], scalar=kcol(d, ch), in1=acc[:, 0:n],
                                        op0=ALU.mult, op1=ALU.add), reads=[bPS[bkk], bVEC, bacc], writes=[bacc])
                            pend_s[0] = (sg, ga, va, bsg, bga, bva, n, il, s)
                        next(side, None)
                    stage_s()
                    for pj in range(1024 // ncols):
                        s_ = get(li)
                        li += 1
                        wv = wview(s_, gp, ncols)
                        for mt in range(ncols // 128):
                            m = pj * (ncols // 128) + mt
                            for (s, w, ic) in plain:
                                bk = next_bank()
                                mm_group(PS[bk][:, 0:w], bk, [(wv[:, k, 128 * mt:128 * mt + 128], FT[:, k, s:s + w]) for k in range(gp)],
                                         reads=[bRING[s_]] + segs(bFT, s, s + w))
                                resid_add(bk, m, s, w, ic, 40)
                    next(side, None)
                for _ in side:
                    pass
                p.barrier()
                st["ring_n"], st["slot"] = NSLOT, 0

        def conv_phase(L, halo, plain):
            j = L // 3
            vb = L * LW
            win = conv_w_in[j].rearrange("(k p) n -> p k n", p=128)
            wout = conv_w_out[j].rearrange("(k p) n -> p k n", p=128)
            with ExitStack() as ph:
                MT = sb("MT", [128, KC, NCOL], BF16, ph)
                bMT = [Buf(f"MT{i}") for i in range(5)]
                CS = [sb(f"CS{i}", [128, 412], F32, ph) for i in range(2)]
                CVV = [sb(f"CVV{i}", [128, 412], F32, ph) for i in range(2)]
                TT = [sb(f"TT{i}", [128, 412], F32, ph) for i in range(2)]
                bCS, bCVV, bTT = [Buf(), Buf()], [Buf(), Buf()], [Buf(), Buf()]
                it = 0
                for jc in range(KC):
                    s1 = ring_load([(0, KC, 128, 256, win[:, :, D + 128 * jc:D + 128 * jc + 128]),
                                    (128, KC, 128, 256, win[:, :, 2 * D + 128 * jc:2 * D + 128 * jc + 128])])
                    s2 = ring_load([(0, KC, 128, 128, win[:, :, 128 * jc:128 * jc + 128])])
                    w1 = wview(s1, KC, 256)
                    w2 = wview(s2, KC, 128)
                    for (s, w, ic) in halo:
                        bc, bv, bb = next_bank(), next_bank(), next_bank()
                        hb = segs(bHT, s, s + w)
                        mm_group(PS[bc][:, 0:w], bc, [(w1[:, kc, 0:128], HT[:, kc, s:s + w]) for kc in range(KC)], reads=[bRING[s1]] + hb)
                        mm_group(PS[bv][:, 0:w], bv, [(w1[:, kc, 128:256], HT[:, kc, s:s + w]) for kc in range(KC)], reads=[bRING[s1]] + hb)
                        mm_group(PS[bb][:, 0:w], bb, [(w2[:, kc, 0:128], HT[:, kc, s:s + w]) for kc in range(KC)], reads=[bRING[s2]] + hb)
                        cs, cvv, tt = CS[it % 2], CVV[it % 2], TT[it % 2]
                        bcs, bcvv, btt = bCS[it % 2], bCVV[it % 2], bTT[it % 2]
                        it += 1
                        n = w - 2

                        def kcol(d):
                            return VEC[:, vb + O_MK + d * 8 + jc:vb + O_MK + d * 8 + jc + 1]
                        p.op("act", lambda e, cs=cs, bc=bc, w=w: e.activation(out=cs[:, 0:w], in_=PS[bc][:, 0:w], func=AF.Copy),
                             reads=[bPS[bc]], writes=[bcs])
                        p.op("dve", lambda e, cvv=cvv, cs=cs, bv=bv, w=w: e.tensor_tensor(out=cvv[:, 0:w], in0=cs[:, 0:w], in1=PS[bv][:, 0:w], op=ALU.mult),
                             reads=[bcs, bPS[bv]], writes=[bcvv])
                        p.op("act", lambda e, tt=tt, cvv=cvv, n=n: e.activation(out=tt[:, 0:n], in_=cvv[:, 1:1 + n], func=AF.Copy, scale=kcol(1)),
                             reads=[bcvv, bVEC], writes=[btt])
                        for d in (0, 2):
                            p.op("dve", lambda e, tt=tt, cvv=cvv, n=n, d=d: e.scalar_tensor_tensor(
                                out=tt[:, 0:n], in0=cvv[:, d:d + n], scalar=kcol(d), in1=tt[:, 0:n], op0=ALU.mult, op1=ALU.add),
                                reads=[bcvv, bVEC, btt], writes=[btt])
                        p.op("dve", lambda e, tt=tt, bb=bb, n=n, s=s: e.tensor_tensor(
                            out=MT[:, jc, s + 1:s + 1 + n], in0=tt[:, 0:n], in1=PS[bb][:, 1:1 + n], op=ALU.mult),
                            reads=[btt, bPS[bb]], writes=segs(bMT, s + 1, s + 1 + n))
                for pj in range(4):
                    s_ = ring_load([(0, KC, 256, 256, wout[:, :, 256 * pj:256 * pj + 256])])
                    wv = wview(s_, KC, 256)
                    for mt in range(2):
                        m = 2 * pj + mt
                        for (s, w, ic) in plain:
                            bk = next_bank()
                            mm_group(PS[bk][:, 0:w], bk, [(wv[:, kc, 128 * mt:128 * mt + 128], MT[:, kc, s:s + w]) for kc in range(KC)],
                                     reads=[bRING[s_]] + segs(bMT, s, s + w))
                            resid_add(bk, m, s, w, ic, 16)
                p.barrier()


        def attn_phase(L):
            wq_all = attn_wqkv.rearrange("(k p) n -> p k n", p=128)
            plain_all = PLAIN_LAT + PLAIN_CTX
            with ExitStack() as ph:
                KT = sb("KT", [128, 2, NCOL], BF16, ph)
                VA = sb("VA", [128, 18, 4, 128], BF16, ph)
                QT = [sb(f"QT{i}", [128, NCOL], BF16, ph) for i in range(2)]
                OT = [sb(f"OT{i}", [128, NCOL], BF16, ph) for i in range(2)]
                TAB = [sb(f"TAB{i}", [128, 2, 512], F32, ph) for i in range(2)]
                NPT = 6
                PT = [sb(f"PT{i}", [128, 512], BF16, ph) for i in range(NPT)]
                SQ = sb("aSQ", [128, 512], BF16, ph)
                QG = sb("aQG", [128, 512], BF16, ph)
                RR = sb("aRR", [128, 512], F32, ph)
                T1 = sb("aT1", [128, 512], F32, ph)
                T2 = sb("aT2", [128, 512], F32, ph)
                RC = [sb(f"aRC{i}", [128, 512], F32, ph) for i in range(2)]
                bKT = [Buf(), Buf()]
                bVA = Buf()
                bOT = [Buf(), Buf()]
                bQT = [Buf(), Buf()]
                bTAB = [Buf(), Buf()]
                bPT = [Buf() for _ in range(NPT)]
                bSQ, bQG, bRR, bT1, bT2 = Buf(), Buf(), Buf(), Buf(), Buf()
                bRC = [Buf(), Buf()]
                cnt = {"tab": 0, "pt": 0, "rc": 0}

                p.op("pool", lambda e: e.memset(VA[:].rearrange("p a b c -> p (a b c)"), 1.0), writes=[bVA])

                def reserve(b):
                    st["reserved"].add(b)

                def release(b):
                    st["reserved"].discard(b)

                def qk_gen(wv, s_, dst_of, bdst, gcol):
                    g_ap = VEC[:, gcol:gcol + 1]
                    for (s, w, ic) in plain_all:
                        rope = (ic == 0)
                        dst_ap = dst_of(s, w)
                        bk = next_bank()
                        reserve(bk)
                        mm_group(PS[bk][:, 0:w], bk, [(wv[:, kc, :], HT[:, kc, s:s + w]) for kc in range(KC)], reads=[bRING[s_]] + segs(bHT, s, s + w))
                        if rope:
                            ti = cnt["tab"] % 2
                            cnt["tab"] += 1
                            tab, btab = TAB[ti], bTAB[ti]
                            t0 = s - LAT0
                            p.dma("sp", lambda e: e.dma_start(out=tab[:, 0, :], in_=rope_a[0][:, t0:t0 + 512]), writes=[btab])
                            p.dma("sp", lambda e: e.dma_start(out=tab[:, 1, :], in_=rope_a[1][:, t0:t0 + 512]), writes=[btab])
                        yield
                        p.op("act", lambda e: e.activation(out=SQ[:, 0:w], in_=PS[bk][:, 0:w], func=AF.Square), reads=[bPS[bk]], writes=[bSQ])
                        if rope:
                            p.op("act", lambda e: e.activation(out=QG[:, 0:w], in_=PS[bk][:, 0:w], func=AF.Copy, scale=g_ap), reads=[bPS[bk], bVEC], writes=[bQG])
                        yield
                        b2 = next_bank()
                        reserve(b2)
                        mm_group(PS[b2][:, 0:w], b2, [(CM[:, 0, :], SQ[:, 0:w])], reads=[bSQ, bCM])
                        yield
                        p.op("act", lambda e: e.activation(out=RR[:, 0:w], in_=PS[b2][:, 0:w], func=AF.Ln, bias=EPSV[:, 1:2], scale=1.0 / 64),
                             reads=[bPS[b2], bEPS], writes=[bRR])
                        if rope:
                            mm_group(PS[b2][:, 0:w], b2, [(CM[:, 1, :], QG[:, 0:w])], reads=[bQG, bCM])
                        yield
                        p.op("act", lambda e: e.activation(out=RR[:, 0:w], in_=RR[:, 0:w], func=AF.Exp, scale=-0.5), reads=[bRR], writes=[bRR])
                        if not rope:
                            p.op("dve", lambda e: e.scalar_tensor_tensor(out=dst_ap, in0=PS[bk][:, 0:w], scalar=g_ap, in1=RR[:, 0:w],
                                                                         op0=ALU.mult, op1=ALU.mult), reads=[bPS[bk], bVEC, bRR], writes=[bdst])
                        else:
                            p.op("dve", lambda e: e.scalar_tensor_tensor(out=T1[:, 0:w], in0=PS[bk][:, 0:w], scalar=g_ap, in1=tab[:, 0, 0:w],
                                                                         op0=ALU.mult, op1=ALU.mult), reads=[bPS[bk], bVEC, btab], writes=[bT1])
                            p.op("dve", lambda e: e.tensor_tensor(out=T2[:, 0:w], in0=PS[b2][:, 0:w], in1=tab[:, 1, 0:w], op=ALU.mult),
                                 reads=[bPS[b2], btab], writes=[bT2])
                            yield
                            p.op("dve", lambda e: e.tensor_tensor(out=T1[:, 0:w], in0=T1[:, 0:w], in1=T2[:, 0:w], op=ALU.add), reads=[bT1, bT2], writes=[bT1])
                            p.op("dve", lambda e: e.tensor_tensor(out=dst_ap, in0=T1[:, 0:w], in1=RR[:, 0:w], op=ALU.mult), reads=[bT1, bRR], writes=[bdst])
                        release(bk)
                        release(b2)
                        yield

                for g2 in range(2):
                    s_ = ring_load([(0, KC, 128, 128, wq_all[:, :, 1024 + 128 * g2:1024 + 128 * g2 + 128])])
                    for _ in qk_gen(wview(s_, KC, 128), s_, lambda s, w, g2=g2: KT[:, g2, s:s + w], bKT[g2], O_KG):
                        pass
                s_ = ring_load([(0, KC, 256, 256, wq_all[:, :, 1280:1536])])
                wv = wview(s_, KC, 256)
                for kt in range(18):
                    col = LAT0 + 128 * kt if kt < 16 else CTX0 + 128 * (kt - 16)
                    bk = next_bank()
                    mm_group(PS[bk][:, 0:256], bk, [(HT[:, kc, col:col + 128], wv[:, kc, :]) for kc in range(KC)],
                             reads=[bRING[s_]] + segs(bHT, col, col + 128))
                    p.op("act", lambda e, kt=kt, bk=bk: e.activation(out=VA[:, kt, :, 0:64], in_=PS[bk][:, 0:256].rearrange("p (h d) -> p h d", d=64), func=AF.Copy),
                         reads=[bPS[bk]], writes=[bVA])

                def heads_of(j):
                    g2, r = j // 4, j % 4
                    return g2, 8 * g2 + r, 8 * g2 + 4 + r

                def q_gen(j):
                    g2, ha, hb = heads_of(j)
                    s_ = ring_load([(0, KC, 64, 128, wq_all[:, :, 64 * ha:64 * ha + 64]),
                                    (64, KC, 64, 128, wq_all[:, :, 64 * hb:64 * hb + 64])])
                    yield from qk_gen(wview(s_, KC, 128), s_, lambda s, w: QT[j % 2][:, s:s + w], bQT[j % 2], O_QG)

                def outproj_gen(j):
                    g2, ha, hb = heads_of(j)
                    ot, bot = OT[j % 2], bOT[j % 2]
                    s_ = st["slot"]
                    st["slot"] = (s_ + 1) % NSLOT
                    for hh, hd in enumerate((ha, hb)):
                        p.dma("pool", lambda e, hh=hh, hd=hd: e.dma_start(out=RING[64 * hh:64 * hh + 64, s_, 0:1024], in_=attn_w_out[64 * hd:64 * hd + 64, :]),
                              writes=[bRING[s_]])
                    yield
                    for m in range(8):
                        for (s, w, ic) in plain_all:
                            bk = next_bank()
                            mm_group(PS[bk][:, 0:w], bk, [(RING[:, s_, 128 * m:128 * m + 128], ot[:, s:s + w])], reads=[bRING[s_], bot])
                            resid_add(bk, m, s, w, ic, 16)
                            yield

                def attend(j, sides):
                    g2, ha, hb = heads_of(j)
                    qt, bqt = QT[j % 2], bQT[j % 2]
                    ot, bot = OT[j % 2], bOT[j % 2]
                    rr_i = [0]

                    def pull():
                        for _ in range(len(sides)):
                            g = sides[rr_i[0] % len(sides)]
                            rr_i[0] += 1
                            try:
                                next(g)
                                return
                            except StopIteration:
                                continue

                    def pv(item, nk):
                        (hh, pi, kt_, idx_, oa, w) = item
                        p.op("pe", lambda e: e.matmul(PS[oa[hh]][:, 0:w], lhsT=VA[:, kt_, 2 * g2 + hh, :], rhs=PT[pi][:, 0:w],
                                                      start=(idx_ == 0), stop=(idx_ == nk - 1)),
                             reads=[bVA, bPT[pi]], writes=[bPS[oa[hh]]], inc=True)

                    for (s, w, ic) in plain_all:
                        kts = list(range(18)) if ic == 0 else [16, 17]
                        oa = [next_bank(), next_bank()]
                        reserve(oa[0])
                        reserve(oa[1])
                        pend = None
                        for idx, kt in enumerate(kts):
                            kcol = LAT0 + 128 * kt if kt < 16 else CTX0 + 128 * (kt - 16)
                            cur_ = []
                            for hh in range(2):
                                bs = next_bank()
                                lo = 64 * hh
                                mm_group(PS[bs][:, 0:w], bs, [(KT[lo:lo + 64, g2, kcol:kcol + 128], qt[lo:lo + 64, s:s + w])], reads=[bKT[g2], bqt])
                                pi = cnt["pt"] % NPT
                                cnt["pt"] += 1
                                p.op("act", lambda e, pi=pi, bs=bs: e.activation(out=PT[pi][:, 0:w], in_=PS[bs][:, 0:w], func=AF.Exp, scale=0.125),
                                     reads=[bPS[bs]], writes=[bPT[pi]])
                                cur_.append((hh, pi, kt, idx, oa, w))
                            if pend is not None:
                                for item in pend:
                                    pv(item, len(kts))
                            pend = cur_
                            pull()
                        for item in pend:
                            pv(item, len(kts))
                        for hh in range(2):
                            ri = cnt["rc"] % 2
                            cnt["rc"] += 1
                            rc, brc = RC[ri], bRC[ri]
                            p.op("dve", lambda e, rc=rc, hh=hh: e.reciprocal(out=rc[0:64, 0:w], in_=PS[oa[hh]][64:128, 0:w]), reads=[bPS[oa[hh]]], writes=[brc])
                            p.op("dve", lambda e, rc=rc, hh=hh: e.tensor_tensor(out=ot[64 * hh:64 * hh + 64, s:s + w], in0=PS[oa[hh]][0:64, 0:w],
                                                                                in1=rc[0:64, 0:w], op=ALU.mult), reads=[bPS[oa[hh]], brc], writes=[bot])
                        release(oa[0])
                        release(oa[1])
                    for g in sides:
                        for _ in g:
                            pass

                for _ in q_gen(0):
                    pass
                prev_out = None
                for j in range(8):
                    sides = []
                    if j + 1 < 8:
                        sides.append(q_gen(j + 1))
                    if prev_out is not None:
                        sides.append(prev_out)
                    attend(j, sides)
                    prev_out = outproj_gen(j)
                for _ in prev_out:
                    pass
                p.barrier()

        def ret_phase(L):
            import math
            LN16 = math.log(16.0)
            win = ret_w_in.rearrange("(k p) n -> p k n", p=128)
            with ExitStack() as ph:
                QTh = sb("QTh", [128, 2, SEQ], BF16, ph)
                KTh = sb("KTh", [128, 2, NCOL], BF16, ph)
                VH = sb("VH", [128, 18, 512], BF16, ph)
                GS = sb("GS", [128, 4, 512], BF16, ph)
                SQY = sb("SQY", [128, 4, 512], BF16, ph)
                Z = sb("Z", [128, 4, 512], BF16, ph)
                TMP = sb("rTMP", [128, 512], F32, ph)
                PT = [sb(f"rPT{i}", [128, 512], BF16, ph) for i in range(3)]
                MK = [sb(f"rMK{i}", [128, 512], F32, ph) for i in range(3)]
                QB = sb("rQB", [128, 512], BF16, ph)
                IOTA = sb("IOTA", [128, 512], F32, ph)
                RTAB = sb("RTAB", [128, 2, 96], F32, ph)
                RRr = sb("rRR", [128, 512], F32, ph)
                LG = sb("LG", [128, 8], F32, ph)
                NLG = sb("NLG", [128, 8], F32, ph)
                LT = sb("LT", [128, 8], F32, ph)
                BIAS = sb("BIAS", [128, 8], F32, ph)
                bQTh, bKTh, bVH, bGS, bSQY, bZ, bTMP, bQB, bIOTA, bRTAB, bRRr, bLG, bLT = [Buf() for _ in range(13)]
                bPT = [Buf() for _ in range(3)]
                bMK = [Buf() for _ in range(3)]
                bBIAS = [Buf() for _ in range(8)]
                cnt = {"pt": 0, "mk": 0, "bias": 0, "rb": 0}

                def rb():
                    b = 4 + cnt["rb"] % 4
                    cnt["rb"] += 1
                    return b

                def nmk():
                    i = cnt["mk"] % 3
                    cnt["mk"] += 1
                    return i

                def nbias():
                    i = cnt["bias"] % 8
                    cnt["bias"] += 1
                    return i

                p.dma("sp", lambda e: e.dma_start(out=IOTA[:], in_=iota_d), writes=[bIOTA])
                p.dma("sp", lambda e: e.dma_start(out=RTAB[:, 0, :], in_=rope_r[0]), writes=[bRTAB])
                p.dma("sp", lambda e: e.dma_start(out=RTAB[:, 1, :], in_=rope_r[1]), writes=[bRTAB])
                p.op("act", lambda e: e.activation(out=LT[:], in_=VEC[:, O_DEC:O_DEC + 8], func=AF.Exp, scale=-math.log(2.0)), reads=[bVEC], writes=[bLT])
                p.op("dve", lambda e: e.tensor_scalar(out=LG[:], in0=LT[:], scalar1=1.0 / 6, scalar2=0.2, op0=ALU.mult, op1=ALU.add), reads=[bLT], writes=[bLG])
                for cst in (0.25, 1.0 / 3, 0.5, 1.0):
                    p.op("dve", lambda e: e.tensor_tensor(out=LG[:], in0=LG[:], in1=LT[:], op=ALU.mult), reads=[bLG, bLT], writes=[bLG])
                    p.op("dve", lambda e, cst=cst: e.tensor_scalar(out=LG[:], in0=LG[:], scalar1=cst, scalar2=None, op0=ALU.add), reads=[bLG], writes=[bLG])
                p.op("dve", lambda e: e.tensor_tensor(out=NLG[:], in0=LG[:], in1=LT[:], op=ALU.mult), reads=[bLG, bLT], writes=[bLG])
                p.op("dve", lambda e: e.tensor_scalar(out=LG[:], in0=NLG[:], scalar1=-1.0, scalar2=None, op0=ALU.mult), reads=[bLG], writes=[bLG])

                def rope_post(bk, dst3, bdst, sg, blk):
                    if sg == 0:
                        c_ap = RTAB[:, 0, 8 * blk:8 * blk + 8].unsqueeze(2).broadcast_to([128, 8, 64])
                        s_ap = RTAB[:, 1, 8 * blk:8 * blk + 8].unsqueeze(2).broadcast_to([128, 8, 64])
                    else:
                        c_ap = RTAB[:, 0, 32:96].unsqueeze(1).broadcast_to([128, 8, 64])
                        s_ap = RTAB[:, 1, 32:96].unsqueeze(1).broadcast_to([128, 8, 64])
                    if RET_DBG == 21:
                        c_ap = IOTA[:].rearrange("p (r c) -> p r c", c=64)
                        s_ap = IOTA[:].rearrange("p (r c) -> p r c", c=64)
                    if RET_DBG == 22 and sg == 1:
                        c_ap = RTAB[:, 0, 0:8].unsqueeze(2).broadcast_to([128, 8, 64])
                        s_ap = RTAB[:, 1, 0:8].unsqueeze(2).broadcast_to([128, 8, 64])
                    p.op("act", lambda e: e.activation(out=QB[:], in_=PS[bk][:], func=AF.Copy), reads=[bPS[bk]], writes=[bQB])
                    b3 = next_bank()
                    mm_group(PS[b3][:], b3, [(CM[:, 2, :], QB[:])], reads=[bQB, bCM])
                    i1, i2 = nmk(), nmk()
                    v3 = lambda ap: ap.rearrange("p (r c) -> p r c", c=64)
                    p.op("dve", lambda e: e.tensor_tensor(out=v3(MK[i1][:]), in0=v3(PS[bk][:]), in1=c_ap, op=ALU.mult), reads=[bPS[bk], bRTAB], writes=[bMK[i1]])
                    p.op("dve", lambda e: e.tensor_tensor(out=v3(MK[i2][:]), in0=v3(PS[b3][:]), in1=s_ap, op=ALU.mult), reads=[bPS[b3], bRTAB], writes=[bMK[i2]])
                    p.op("dve", lambda e: e.tensor_tensor(out=dst3, in0=MK[i1][:], in1=MK[i2][:], op=ALU.add), reads=[bMK[i1], bMK[i2]], writes=[bdst])

                for h in range(RET_NH):
                    if RET_DBG == 1:
                        break
                    lgf, nlgf = LG[:, h:h + 1], NLG[:, h:h + 1]
                    lgb, nlgb = LG[:, 4 + h:5 + h], NLG[:, 4 + h:5 + h]
                    for which, (dstT, bdst, c0) in enumerate(((QTh, bQTh, 256 * h), (KTh, bKTh, 1024 + 256 * h))):
                        s_ = ring_load([(0, KC, 256, 256, win[:, :, c0:c0 + 256])])
                        wv = wview(s_, KC, 256)
                        for sg in range(2):
                            for blk, (s, w, ic) in enumerate(PLAIN_LAT):
                                bk = next_bank()
                                mm_group(PS[bk][:], bk, [(wv[:, kc, 128 * sg:128 * sg + 128], HT[:, kc, s:s + w]) for kc in range(KC)],
                                         reads=[bRING[s_]] + segs(bHT, s, s + w))
                                cbase = (s - LAT0) if which == 0 else s
                                rope_post(bk, dstT[:, sg, cbase:cbase + 512], bdst, sg, blk)
                            if which == 1:
                                (s, w, ic) = PLAIN_CTX[0]
                                bk = next_bank()
                                mm_group(PS[bk][:, 0:w], bk, [(wv[:, kc, 128 * sg:128 * sg + 128], HT[:, kc, s:s + w]) for kc in range(KC)],
                                         reads=[bRING[s_]] + segs(bHT, s, s + w))
                                p.op("act", lambda e, bk=bk, sg=sg, s=s, w=w: e.activation(out=KTh[:, sg, s:s + w], in_=PS[bk][:, 0:w], func=AF.Copy),
                                     reads=[bPS[bk]], writes=[bKTh])
                    if RET_DBG in (2, 21, 22):
                        break
                    sv = [ring_load([(0, KC, 256, 256, win[:, :, 2048 + 512 * h + 256 * i:2048 + 512 * h + 256 * i + 256])]) for i in range(2)]
                    for kt in range(18):
                        col = LAT0 + 128 * kt if kt < 16 else CTX0 + 128 * (kt - 16)
                        bk = next_bank()
                        for i in range(2):
                            wv = wview(sv[i], KC, 256)
                            mm_group(PS[bk][:, 256 * i:256 * i + 256], bk, [(HT[:, kc, col:col + 128], wv[:, kc, :]) for kc in range(KC)],
                                     reads=[bRING[sv[i]]] + segs(bHT, col, col + 128))
                        if kt % 2 == 0:
                            p.op("act", lambda e, kt=kt, bk=bk: e.activation(out=VH[:, kt, :], in_=PS[bk][:], func=AF.Copy), reads=[bPS[bk]], writes=[bVH])
                        else:
                            p.op("dve", lambda e, kt=kt, bk=bk: e.tensor_copy(out=VH[:, kt, :], in_=PS[bk][:]), reads=[bPS[bk]], writes=[bVH])

                    if RET_DBG == 3:
                        break
                    for qb, (s, w, ic) in enumerate(PLAIN_LAT):
                        q0 = s - LAT0
                        sg_ = [ring_load([(0, KC, 256, 256, win[:, :, 4096 + 512 * h + 256 * i:4096 + 512 * h + 256 * i + 256])]) for i in range(2)]
                        for m in range(4):
                            wv = wview(sg_[m // 2], KC, 256)
                            bk = rb()
                            mm_group(PS[bk][:], bk, [(wv[:, kc, 128 * (m % 2):128 * (m % 2) + 128], HT[:, kc, s:s + w]) for kc in range(KC)],
                                     reads=[bRING[sg_[m // 2]]] + segs(bHT, s, s + w))
                            p.op("act", lambda e, m=m, bk=bk: e.activation(out=GS[:, m, :], in_=PS[bk][:], func=AF.Silu), reads=[bPS[bk]], writes=[bGS])
                        pend = None
                        for kt in range(18):
                            kcol = LAT0 + 128 * kt if kt < 16 else CTX0 + 128 * (kt - 16)
                            bs = rb()
                            mm_group(PS[bs][:], bs, [(KTh[:, sg, kcol:kcol + 128], QTh[:, sg, q0:q0 + 512]) for sg in range(2)], reads=[bKTh, bQTh])
                            pi = cnt["pt"] % 3
                            cnt["pt"] += 1
                            if kt < 16:
                                off = 512 * qb - 128 * kt
                                if off >= 128 or off <= -512:
                                    sc, bsrc = (lgf, lgf) if off >= 128 else (nlgb, nlgb)
                                    bi = nbias()
                                    p.op("dve", lambda e, bi=bi, bsrc=bsrc, off=off: e.tensor_scalar(out=BIAS[:, bi:bi + 1], in0=bsrc, scalar1=float(off), scalar2=-LN16,
                                                                                                      op0=ALU.mult, op1=ALU.add), reads=[bLG], writes=[bBIAS[bi]])
                                    mi = nmk()
                                    p.op("act", lambda e, mi=mi, sc=sc, bi=bi: e.activation(out=MK[mi][:], in_=IOTA[:], func=AF.Exp, scale=sc, bias=BIAS[:, bi:bi + 1]),
                                         reads=[bIOTA, bLG, bBIAS[bi]], writes=[bMK[mi]])
                                    p.op("dve", lambda e, pi=pi, bs=bs, mi=mi: e.tensor_tensor(out=PT[pi][:], in0=PS[bs][:], in1=MK[mi][:], op=ALU.mult),
                                         reads=[bPS[bs], bMK[mi]], writes=[bPT[pi]])
                                else:
                                    m1, m2 = nmk(), nmk()
                                    bi = nbias()
                                    p.op("dve", lambda e, bi=bi: e.memset(BIAS[:, bi:bi + 1], -LN16), writes=[bBIAS[bi]])
                                    p.op("dve", lambda e, m1=m1, off=off: e.tensor_scalar(out=MK[m1][:], in0=IOTA[:], scalar1=float(off), scalar2=0.0, op0=ALU.add, op1=ALU.max),
                                         reads=[bIOTA], writes=[bMK[m1]])
                                    p.op("dve", lambda e, m1=m1, m2=m2, off=off: e.scalar_tensor_tensor(out=MK[m2][:], in0=IOTA[:], scalar=float(off), in1=MK[m1][:],
                                                                                                         op0=ALU.add, op1=ALU.subtract), reads=[bIOTA, bMK[m1]], writes=[bMK[m2]])
                                    p.op("act", lambda e, m1=m1, bi=bi: e.activation(out=MK[m1][:], in_=MK[m1][:], func=AF.Exp, scale=lgf, bias=BIAS[:, bi:bi + 1]),
                                         reads=[bMK[m1], bLG, bBIAS[bi]], writes=[bMK[m1]])
                                    p.op("act", lambda e, m2=m2: e.activation(out=MK[m2][:], in_=MK[m2][:], func=AF.Exp, scale=nlgb), reads=[bMK[m2], bLG], writes=[bMK[m2]])
                                    p.op("dve", lambda e, bs=bs, m1=m1: e.tensor_tensor(out=TMP[:], in0=PS[bs][:], in1=MK[m1][:], op=ALU.mult),
                                         reads=[bPS[bs], bMK[m1]], writes=[bTMP])
                                    p.op("dve", lambda e, pi=pi, m2=m2: e.tensor_tensor(out=PT[pi][:], in0=TMP[:], in1=MK[m2][:], op=ALU.mult),
                                         reads=[bTMP, bMK[m2]], writes=[bPT[pi]])
                            else:
                                a_ = kt - 16
                                m1, m2 = nmk(), nmk()
                                b1, b2 = nbias(), nbias()
                                o1 = float(512 * qb + 256 - 128 * a_)
                                o2 = float(2048 - 512 * qb + 128 * a_)
                                p.op("dve", lambda e, b1=b1, o1=o1: e.tensor_scalar(out=BIAS[:, b1:b1 + 1], in0=lgf, scalar1=o1, scalar2=-LN16, op0=ALU.mult, op1=ALU.add),
                                     reads=[bLG], writes=[bBIAS[b1]])
                                p.op("dve", lambda e, b2=b2, o2=o2: e.tensor_scalar(out=BIAS[:, b2:b2 + 1], in0=lgb, scalar1=o2, scalar2=-LN16, op0=ALU.mult, op1=ALU.add),
                                     reads=[bLG], writes=[bBIAS[b2]])
                                p.op("act", lambda e, m1=m1, b1=b1: e.activation(out=MK[m1][:], in_=IOTA[:], func=AF.Exp, scale=lgf, bias=BIAS[:, b1:b1 + 1]),
                                     reads=[bIOTA, bLG, bBIAS[b1]], writes=[bMK[m1]])
                                p.op("act", lambda e, m2=m2, b2=b2: e.activation(out=MK[m2][:], in_=IOTA[:], func=AF.Exp, scale=nlgb, bias=BIAS[:, b2:b2 + 1]),
                                     reads=[bIOTA, bLG, bBIAS[b2]], writes=[bMK[m2]])
                                p.op("dve", lambda e, m1=m1, m2=m2: e.tensor_tensor(out=MK[m1][:], in0=MK[m1][:], in1=MK[m2][:], op=ALU.add),
                                     reads=[bMK[m1], bMK[m2]], writes=[bMK[m1]])
                                p.op("dve", lambda e, pi=pi, bs=bs, m1=m1: e.tensor_tensor(out=PT[pi][:], in0=PS[bs][:], in1=MK[m1][:], op=ALU.mult),
                                     reads=[bPS[bs], bMK[m1]], writes=[bPT[pi]])
                            if pend is not None:
                                for dv in range(4):
                                    p.op("pe", lambda e, dv=dv, pend=pend: e.matmul(PS[dv][:], lhsT=VH[:, pend[0], 128 * dv:128 * dv + 128], rhs=PT[pend[1]][:],
                                                                                     start=(pend[0] == 0), stop=False),
                                         reads=[bVH, bPT[pend[1]]], writes=[bPS[dv]], inc=(dv == 3))
                            pend = (kt, pi)
                        for dv in range(4):
                            p.op("pe", lambda e, dv=dv, pend=pend: e.matmul(PS[dv][:], lhsT=VH[:, pend[0], 128 * dv:128 * dv + 128], rhs=PT[pend[1]][:],
                                                                             start=False, stop=True),
                                 reads=[bVH, bPT[pend[1]]], writes=[bPS[dv]], inc=True)
                        for dv in range(4):
                            p.op("act", lambda e, dv=dv: e.activation(out=SQY[:, dv, :], in_=PS[dv][:], func=AF.Square), reads=[bPS[dv]], writes=[bSQY])
                        bss = rb()
                        mm_group(PS[bss][:], bss, [(ONES[:], SQY[:, dv, :]) for dv in range(4)], reads=[bSQY, bONES])
                        p.op("act", lambda e, bss=bss: e.activation(out=RRr[:], in_=PS[bss][:], func=AF.Sqrt, bias=EPSV[:, 1:2], scale=1.0 / 512),
                             reads=[bPS[bss], bEPS], writes=[bRRr])
                        p.op("dve", lambda e: e.reciprocal(out=RRr[:], in_=RRr[:]), reads=[bRRr], writes=[bRRr])
                        for dv in range(4):
                            p.op("dve", lambda e, dv=dv: e.tensor_tensor(out=TMP[:], in0=PS[dv][:], in1=RRr[:], op=ALU.mult), reads=[bPS[dv], bRRr], writes=[bTMP])
                            p.op("dve", lambda e, dv=dv: e.tensor_tensor(out=Z[:, dv, :], in0=TMP[:], in1=GS[:, dv, :], op=ALU.mult), reads=[bTMP, bGS], writes=[bZ])
                        wo = ret_w_out[512 * h:512 * h + 512, :].rearrange("(k p) n -> p k n", p=128)
                        for pj in range(2):
                            s_ = ring_load([(0, 4, 512, 512, wo[:, :, 512 * pj:512 * pj + 512])])
                            wv = wview(s_, 4, 512)
                            for mt in range(4):
                                bk = rb()
                                mm_group(PS[bk][:], bk, [(wv[:, dv, 128 * mt:128 * mt + 128], Z[:, dv, :]) for dv in range(4)], reads=[bRING[s_], bZ])
                                resid_add(bk, 4 * pj + mt, s, w, 0, 16)
                p.barrier()

        for L in layers:
            kind = L % 3
            has_ctx = L <= 1
            ctx_in = L <= 2
            plain = PLAIN_LAT + (PLAIN_CTX if has_ctx else [])
            halo = HALO_LAT + (HALO_CTX if has_ctx else [])
            if L == layers[0]:
                for _ in ada_gen(L):
                    pass
            set_layer(L)
            nxt = layers[layers.index(L) + 1] if layers.index(L) + 1 < len(layers) else None
            norm_phase(0, PLAIN_LAT + (PLAIN_CTX if ctx_in else []))
            if kind == 0:
                conv_phase(L, halo, plain)
            elif kind == 1:
                attn_phase(L)
            else:
                ret_phase(L)
            if not DBG_SKIP_FFN:
                norm_phase(1, plain)
                ffn_phase(L, halo, plain, ada_gen(nxt) if nxt is not None else iter(()))
            elif nxt is not None:
                for _ in ada_gen(nxt):
                    pass

        with ExitStack() as ph:
            SQ = [sb(f"fSQ{i}", [128, KC, 128], BF16, ph) for i in range(2)]
            RR = [sb(f"fRR{i}", [128, 128], F32, ph) for i in range(2)]
            YT = [sb(f"fYT{i}", [128, KC, 128], F32, ph) for i in range(2)]
            OS = [sb(f"fOS{i}", [128, D], F32, ph) for i in range(2)]
            bSQ, bRR, bYT, bOS = [Buf(), Buf()], [Buf(), Buf()], [Buf(), Buf()], [Buf(), Buf()]
            GF = sb("GF32", [128, KC], F32, ph)
            bGF = Buf()
            p.op("dve", lambda e: e.tensor_scalar(out=GF[:], in0=VEC[:, O_FIN:O_FIN + 8], scalar1=32.0, scalar2=None, op0=ALU.mult),
                 reads=[bVEC], writes=[bGF])
            for t in range(16):
                s = LAT0 + 128 * t
                sq, rr, yt, os_ = SQ[t % 2], RR[t % 2], YT[t % 2], OS[t % 2]
                bsq, brr, byt, bos = bSQ[t % 2], bRR[t % 2], bYT[t % 2], bOS[t % 2]
                xb = segs(bXT, s, s + 128)
                p.op("act", lambda e, sq=sq, s=s: e.activation(out=sq[:], in_=XT[:, :, s:s + 128], func=AF.Square), reads=xb, writes=[bsq])
                bk = next_bank()
                mm_group(PS[bk][:, 0:128], bk, [(ONES[:], sq[:, kc, :]) for kc in range(KC)], reads=[bsq, bONES])
                p.op("act", lambda e, rr=rr, bk=bk: e.activation(out=rr[:], in_=PS[bk][:, 0:128], func=AF.Sqrt, bias=EPSV[:, 0:1], scale=1.0),
                     reads=[bPS[bk], bEPS], writes=[brr])
                p.op("dve", lambda e, rr=rr: e.reciprocal(out=rr[:], in_=rr[:]), reads=[brr], writes=[brr])
                p.op("dve", lambda e, yt=yt, rr=rr, s=s: e.tensor_tensor(out=yt[:], in0=XT[:, :, s:s + 128],
                                                                          in1=rr[:].unsqueeze(1).broadcast_to([128, KC, 128]), op=ALU.mult),
                     reads=xb + [brr], writes=[byt])
                p.op("dve", lambda e, yt=yt: e.tensor_tensor(out=yt[:], in0=yt[:], in1=GF[:].unsqueeze(2).broadcast_to([128, KC, 128]), op=ALU.mult),
                     reads=[byt, bGF], writes=[byt])
                for half in range(2):
                    bk = next_bank()
                    for q in range(4):
                        kc = 4 * half + q
                        p.op("pe", lambda e, bk=bk, q=q, kc=kc, yt=yt: e.transpose(out=PS[bk][:, 128 * q:128 * q + 128], in_=yt[:, kc, :], identity=IDF[:]),
                             reads=[byt, bIDF], writes=[bPS[bk]], inc=(q == 3))
                    if half == 0:
                        p.op("act", lambda e, os_=os_, bk=bk: e.activation(out=os_[:, 0:512], in_=PS[bk][:], func=AF.Copy), reads=[bPS[bk]], writes=[bos])
                    else:
                        p.op("dve", lambda e, os_=os_, bk=bk: e.tensor_copy(out=os_[:, 512:1024], in_=PS[bk][:]), reads=[bPS[bk]], writes=[bos])
                p.dma("sp", lambda e, os_=os_, t=t: e.dma_start(out=out_d[128 * t:128 * t + 128, :], in_=os_[:]), reads=[bos], owner=bos)
            p.barrier()
        print(f"[build] ops={p.n_op} waits={p.n_wait} cnt={p.cnt}")
    return nc


def _cols(v):
    v = np.asarray(v, np.float32).reshape(-1, 128)
    return np.ascontiguousarray(v.T)


def _rope_tables():
    rows = np.repeat(np.arange(32, dtype=np.float32), 64)
    cols = np.tile(np.arange(64, dtype=np.float32), 32)
    q = 16
    inv = (10000.0 ** (-np.arange(q, dtype=np.float32) / q)).astype(np.float32)
    ca = np.zeros((128, SEQ), np.float32)
    sa = np.zeros((128, SEQ), np.float32)
    for pp in range(128):
        d = pp % 64
        seg, half, i = d // 32, (d % 32) // 16, d % 16
        ang = (rows if seg == 0 else cols) * inv[i]
        ca[pp] = np.cos(ang)
        sa[pp] = np.sin(ang) * (-1.0 if half == 0 else 1.0)
    q = 64
    inv = (10000.0 ** (-np.arange(q, dtype=np.float32) / q)).astype(np.float32)
    cr = np.zeros((128, 96), np.float32)
    sr = np.zeros((128, 96), np.float32)
    for pp in range(128):
        half, i = pp // 64, pp % 64
        sgn = -1.0 if half == 0 else 1.0
        a_row = np.arange(32, dtype=np.float32) * inv[i]
        a_col = np.arange(64, dtype=np.float32) * inv[i]
        cr[pp, 0:32] = np.cos(a_row)
        cr[pp, 32:96] = np.cos(a_col)
        sr[pp, 0:32] = np.sin(a_row) * sgn
        sr[pp, 32:96] = np.sin(a_col) * sgn
    return np.stack([ca, sa]), np.stack([cr, sr])


def _pack(inputs, layers):
    f = lambda a: np.ascontiguousarray(np.asarray(a, np.float32))
    vec = np.zeros((128, NV), np.float32)
    for L in range(4):
        b = L * LW
        vec[:, b + O_ADAB:b + O_ADAB + 48] = _cols(inputs["ada_b"][L])
        vec[:, b + O_GMIX:b + O_GMIX + 8] = _cols(inputs["norm_mix_g"][L])
        vec[:, b + O_GFFN:b + O_GFFN + 8] = _cols(inputs["norm_ffn_g"][L])
        for d in range(3):
            vec[:, b + O_FK + 44 * d:b + O_FK + 44 * d + 44] = _cols(inputs["ffn_conv_k"][L][d])
        vec[:, b + O_FB:b + O_FB + 44] = _cols(inputs["ffn_conv_b"][L])
        if L % 3 == 0:
            for d in range(3):
                vec[:, b + O_MK + 8 * d:b + O_MK + 8 * d + 8] = _cols(inputs["conv_k"][L // 3][d])
    vec[:, O_FIN:O_FIN + 8] = _cols(inputs["final_norm_g"])
    vec[:, O_QG] = np.tile(np.asarray(inputs["attn_q_norm_g"][0], np.float32), 2)
    vec[:, O_KG] = np.tile(np.asarray(inputs["attn_k_norm_g"][0], np.float32), 2)
    vec[:, O_DEC:O_DEC + 8] = np.broadcast_to(np.asarray(inputs["ret_decay"][0], np.float32).reshape(1, 8), (128, 8))
    ra, rr = _rope_tables()
    iota = (np.arange(512, dtype=np.float32)[None, :] - np.arange(128, dtype=np.float32)[:, None])
    wq = f(inputs["attn_w_qkv"][0])
    pidx = np.arange(128)
    bd = (pidx[:, None] // 64 == pidx[None, :] // 64).astype(np.float32)
    pa = ((pidx[:, None] ^ 16) == pidx[None, :]).astype(np.float32)
    pr = ((pidx[:, None] ^ 64) == pidx[None, :]).astype(np.float32)
    cmat = np.ascontiguousarray(np.concatenate([bd, pa, pr], axis=1))
    shared = {
        "vec": vec, "ada_w": f(inputs["ada_w"]), "conv_w_in": f(inputs["conv_w_in"]), "conv_w_out": f(inputs["conv_w_out"]),
        "attn_wqkv": wq, "attn_w_out": f(inputs["attn_w_out"][0]), "ret_w_in": f(inputs["ret_w_in"][0]),
        "ret_w_out": f(inputs["ret_w_out"][0]), "ffn_w_up": f(inputs["ffn_w_up"]), "ffn_w_down": f(inputs["ffn_w_down"]),
        "rope_a": ra, "rope_r": rr, "iota_d": np.ascontiguousarray(iota), "cmat": cmat,
    }
    maps = []
    for b in range(8):
        cv = np.stack([_cols(inputs["c"][b]), _cols(inputs["c_ctx"])], axis=-1).reshape(128, 16)
        m = dict(shared)
        m["x"] = f(inputs["x"][b])
        m["ctx"] = f(inputs["ctx"][b])
        m["cvec"] = np.ascontiguousarray(cv)
        maps.append(m)
    return maps


_NC_CACHE = {}


def kernel(_layers=(0, 1, 2, 3), _cores=8, **inputs):
    key = tuple(_layers)
    if key not in _NC_CACHE:
        _NC_CACHE[key] = build(key)
    nc = _NC_CACHE[key]
    maps = _pack(inputs, key)[:_cores]
    res = run_bass_kernel_spmd(nc, maps, core_ids=list(range(_cores)))
    out = np.stack([np.asarray(r["out"], np.float32) for r in res.results], axis=0)
    return out
```

```python
import numpy as np
from contextlib import ExitStack
import concourse.bass as bass
import concourse.mybir as mybir
from concourse.bass_utils import run_bass_kernel_spmd

F32 = mybir.dt.float32
BF16 = mybir.dt.bfloat16
ALU = mybir.AluOpType
AF = mybir.ActivationFunctionType

ENGS = ("pe", "act", "dve", "pool", "sp")

D = 1024
KC = 8
SEQ = 2048
CTXL = 256
NCOL = 2308
LAT0 = 1
CTX0 = 2051
DFF = 2816
NPAIR = 22
EPS = 1e-6
SEG = [0, 513, 1025, 1537, 2050, NCOL]
PLAIN_LAT = [(1 + 512 * b, 512, 0) for b in range(4)]
PLAIN_CTX = [(CTX0, 256, 1)]
HALO_LAT = [(410 * b, min(412, 2050 - 410 * b), 0) for b in range(5)]
HALO_CTX = [(2050, 258, 1)]

LW = 264
O_ADAB, O_GMIX, O_GFFN, O_FK, O_FB, O_MK = 0, 48, 56, 64, 196, 240
O_FIN = 4 * LW
O_QG = O_FIN + 8
O_KG = O_QG + 1
O_DEC = O_KG + 1
NV = O_DEC + 8

RET_NH = 4
RET_DBG = 0
FFN_MUL_ENG = "pool"
DBG_SKIP_FFN = False
RSLOT = 2048
NSLOT = 5


class Buf:
    __slots__ = ("name", "w", "r", "dsem", "excl")

    def __init__(self, name="", excl=False):
        self.name = name
        self.w = None
        self.r = {}
        self.dsem = None
        self.excl = excl


class Prog:
    def __init__(self, nc, eng_sems, dma_sems):
        self.nc = nc
        self.eng = {"pe": nc.tensor, "act": nc.scalar, "dve": nc.vector, "pool": nc.gpsimd, "sp": nc.sync}
        self.cnt = {e: 0 for e in ENGS}
        self.semh = dict(eng_sems)
        self.dma_free = list(dma_sems)
        self.dcnt = {}
        self.seen = {e: {} for e in ENGS}
        self.n_op = 0
        self.n_wait = 0

    def _deps(self, reads, writes, eng=None):
        deps = {}
        for b in reads:
            t = b.w
            if t is not None and deps.get(t[0], 0) < t[1]:
                deps[t[0]] = t[1]
            if b.excl:
                for k, v in b.r.items():
                    if k != eng and deps.get(k, 0) < v:
                        deps[k] = v
        for b in writes:
            t = b.w
            if t is not None and deps.get(t[0], 0) < t[1]:
                deps[t[0]] = t[1]
            for k, v in b.r.items():
                if deps.get(k, 0) < v:
                    deps[k] = v
        return deps

    def _waits(self, eng, deps):
        seen = self.seen[eng]
        for k, v in deps.items():
            if k == "pe" and eng == "pe":
                continue
            if seen.get(k, 0) < v:
                seen[k] = v
                self.eng[eng].wait_ge(self.semh[k], v)
                self.n_wait += 1

    def _mark(self, tok, reads, writes):
        k, v = tok
        for b in writes:
            b.w = tok
            b.r = {}
        for b in reads:
            if b.r.get(k, 0) < v:
                b.r[k] = v

    def op(self, eng, fn, reads=(), writes=(), inc=True):
        self._waits(eng, self._deps(reads, writes, eng))
        tok = (eng, self.cnt[eng] + 1)
        ins = fn(self.eng[eng])
        if inc:
            self.cnt[eng] += 1
            ins.then_inc(self.semh[eng], 1)
        self.n_op += 1
        self._mark(tok, reads, writes)
        return tok

    def _dsem(self, b):
        if b.dsem is None:
            h = self.dma_free.pop()
            key = ("dma", len(self.dcnt))
            self.semh[key] = h
            self.dcnt[key] = 0
            b.dsem = key
        return b.dsem

    def dma(self, queue, fn, reads=(), writes=(), owner=None):
        owner = owner or (writes[0] if writes else reads[0])
        key = self._dsem(owner)
        self._waits(queue, self._deps(reads, writes))
        self.dcnt[key] += 16
        tok = (key, self.dcnt[key])
        fn(self.eng[queue]).then_inc(self.semh[key], 16)
        self._mark(tok, reads, writes)
        return tok

    def barrier(self):
        deps = {e: self.cnt[e] for e in ("pe", "act", "dve", "pool") if self.cnt[e] > 0}
        for k, v in self.dcnt.items():
            if v > 0:
                deps[k] = v
        for e in ENGS:
            d = dict(deps)
            self._waits(e, d)


def segs(bufs, s, e):
    out = []
    for i in range(len(SEG) - 1):
        if s < SEG[i + 1] and e > SEG[i]:
            out.append(bufs[i])
    return out


def build(layers=(0, 1, 2, 3)):
    nc = bass.Bass("TRN2", target_bir_lowering=False)

    def din(name, shape):
        return nc.dram_tensor(name, list(shape), F32, kind="ExternalInput").ap()

    x_d = din("x", [SEQ, D])
    ctx_d = din("ctx", [CTXL, D])
    cvec_d = din("cvec", [128, KC * 2])
    vec_d = din("vec", [128, NV])
    ada_w = din("ada_w", [4, D, 6 * D])
    conv_w_in = din("conv_w_in", [2, D, 3 * D])
    conv_w_out = din("conv_w_out", [2, D, D])
    attn_wqkv = din("attn_wqkv", [D, 1536])
    attn_w_out = din("attn_w_out", [D, D])
    ret_w_in = din("ret_w_in", [D, 6 * D])
    ret_w_out = din("ret_w_out", [2 * D, D])
    ffn_w_up = din("ffn_w_up", [4, D, 2 * DFF])
    ffn_w_down = din("ffn_w_down", [4, DFF, D])
    rope_a = din("rope_a", [2, 128, SEQ])
    rope_r = din("rope_r", [2, 128, 96])
    iota_d = din("iota_d", [128, 512])
    cmat_d = din("cmat", [128, 3 * 128])
    out_d = nc.dram_tensor("out", [SEQ, D], F32, kind="ExternalOutput").ap()

    with ExitStack() as es:
        uniq = [0]

        def sb(name, shape, dt, stack=es):
            uniq[0] += 1
            return stack.enter_context(nc.sbuf_tensor(f"{name}_{uniq[0]}", list(shape), dt))

        XT = sb("XT", [128, KC, NCOL], F32)
        HT = sb("HT", [128, KC, NCOL], BF16)
        RING = sb("RING", [128, NSLOT, RSLOT], BF16)
        VEC = sb("VEC", [128, NV], F32)
        CV = sb("CVEC", [128, KC, 2], F32)
        SC = sb("SC", [128, KC, 2], BF16)
        MODS = [sb(f"MOD{i}", [128, 48, 2], F32) for i in range(2)]
        AMS = [sb(f"AM{i}", [128, 2, KC, 2], F32) for i in range(2)]
        ONES = sb("ONES", [128, 128], BF16)
        IDF = sb("IDF", [128, 128], F32)
        EPSV = sb("EPSV", [128, 2], F32)
        CM = sb("CM", [128, 3, 128], BF16)
        PS = [es.enter_context(nc.psum_tensor(f"PS{i}", [128, 512], F32)) for i in range(8)]

        sems = {e: es.enter_context(nc.semaphore("s_" + e)) for e in ("pe", "act", "dve", "pool")}
        dsems = [es.enter_context(nc.semaphore(f"d{i}")) for i in range(60)]
        es.enter_context(nc.Block())
        p = Prog(nc, sems, dsems)

        bXT = [Buf(f"XT{i}") for i in range(5)]
        bHT = [Buf(f"HT{i}") for i in range(5)]
        bRING = [Buf(f"R{i}") for i in range(NSLOT)]
        bPS = [Buf(f"PS{i}", excl=True) for i in range(8)]
        bVEC, bCV, bSC, bONES, bIDF, bEPS, bCM = [Buf(n) for n in "VEC CV SC ONES IDF EPS CM".split()]
        bMODS = [Buf("MOD0"), Buf("MOD1")]
        bAMS = [Buf("AM0"), Buf("AM1")]
        cur = {}

        def set_layer(L):
            cur["MOD"], cur["AM"], cur["bMOD"], cur["bAM"] = MODS[L % 2], AMS[L % 2], bMODS[L % 2], bAMS[L % 2]
        st = {"slot": 0, "bank": 0, "reserved": set(), "ring_n": NSLOT, "side": 0}

        def next_bank():
            while True:
                b = st["bank"]
                st["bank"] = (b + 1) % 8
                if b not in st["reserved"]:
                    return b

        def ring_load(parts, side=False):
            if side:
                s = NSLOT - 2 + st["side"] % 2
                st["side"] += 1
            else:
                s = st["slot"]
                st["slot"] = (s + 1) % st["ring_n"]
            for (c0, kcn, n, tot, src) in parts:
                dst = RING[:, s, 0:kcn * tot].rearrange("p (k n) -> p k n", n=tot)[:, :, c0:c0 + n]
                p.dma("pool", lambda e, dst=dst, src=src: e.dma_start(out=dst, in_=src), writes=[bRING[s]])
            return s

        def wview(s, kcn, tot):
            return RING[:, s, 0:kcn * tot].rearrange("p (k n) -> p k n", n=tot)

        def mm_group(out_ap, bank, pairs, reads):
            n = len(pairs)
            for i, (l, r) in enumerate(pairs):
                p.op("pe", lambda e, l=l, r=r, i=i: e.matmul(out_ap, lhsT=l, rhs=r, start=(i == 0), stop=(i == n - 1)),
                     reads=reads, writes=[bPS[bank]], inc=(i == n - 1))

        p.dma("sp", lambda e: e.dma_start(out=VEC[:], in_=vec_d), writes=[bVEC])
        p.dma("sp", lambda e: e.dma_start(out=CV[:].rearrange("p k t -> p (k t)"), in_=cvec_d), writes=[bCV])
        p.dma("pool", lambda e: e.dma_start(out=CM[:].rearrange("p a b -> p (a b)"), in_=cmat_d), writes=[bCM])
        p.op("dve", lambda e: e.memset(ONES[:], 1.0), writes=[bONES])
        p.op("dve", lambda e: e.memset(EPSV[:, 0:1], 1024.0 * EPS), writes=[bEPS])
        p.op("dve", lambda e: e.memset(EPSV[:, 1:2], EPS), reads=[bEPS], writes=[bEPS])
        p.op("pool", lambda e: e.memset(IDF[:], 0.0), writes=[bIDF])
        p.op("pool", lambda e: e.affine_select(out=IDF[:], in_=IDF[:], pattern=[[-1, 128]], compare_op=ALU.not_equal,
                                               fill=1.0, base=0, channel_multiplier=1), reads=[bIDF], writes=[bIDF])
        p.op("pool", lambda e: e.memset(HT[:].rearrange("p k n -> p (k n)"), 0.0), writes=bHT)
        p.op("pool", lambda e: e.memset(XT[:].rearrange("p k n -> p (k n)"), 0.0), writes=bXT)
        p.op("act", lambda e: e.activation(out=SC[:], in_=CV[:], func=AF.Silu), reads=[bCV], writes=[bSC])

        with ExitStack() as ph:
            XS = [sb(f"XS{i}", [128, D], F32, ph) for i in range(2)]
            bXS = [Buf("XS0"), Buf("XS1")]
            for t in range(18):
                src = x_d[128 * t:128 * t + 128, :] if t < 16 else ctx_d[128 * (t - 16):128 * (t - 16) + 128, :]
                col = LAT0 + 128 * t if t < 16 else CTX0 + 128 * (t - 16)
                xs, bxs = XS[t % 2], bXS[t % 2]
                p.dma("sp", lambda e, xs=xs, src=src: e.dma_start(out=xs[:], in_=src), writes=[bxs])
                for half in range(2):
                    bk = next_bank()
                    for q in range(4):
                        kc = half * 4 + q
                        p.op("pe", lambda e, bk=bk, q=q, kc=kc, xs=xs: e.transpose(out=PS[bk][:, 128 * q:128 * q + 128],
                                                                                     in_=xs[:, 128 * kc:128 * kc + 128], identity=IDF[:]),
                             reads=[bxs, bIDF], writes=[bPS[bk]], inc=(q == 3))
                    eng = "act" if half == 0 else "dve"
                    dst = XT[:, half * 4:half * 4 + 4, col:col + 128]
                    srcp = PS[bk][:].rearrange("p (k n) -> p k n", n=128)
                    if eng == "act":
                        p.op("act", lambda e, dst=dst, srcp=srcp: e.activation(out=dst, in_=srcp, func=AF.Copy),
                             reads=[bPS[bk]], writes=segs(bXT, col, col + 128))
                    else:
                        p.op("dve", lambda e, dst=dst, srcp=srcp: e.tensor_copy(out=dst, in_=srcp),
                             reads=[bPS[bk]], writes=segs(bXT, col, col + 128))
            p.barrier()

        def ada_gen(L):
            MOD, AM, bMOD, bAM = MODS[L % 2], AMS[L % 2], bMODS[L % 2], bAMS[L % 2]
            bk = next_bank()
            st["reserved"].add(bk)
            psa = PS[bk]
            wsrc = ada_w[L].rearrange("(k p) n -> p k n", p=128)
            nxt_s = ring_load([(0, KC, 256, 256, wsrc[:, :, 0:256])], side=True)
            for pi in range(24):
                s = nxt_s
                if pi + 1 < 24:
                    nxt_s = ring_load([(0, KC, 256, 256, wsrc[:, :, 256 * (pi + 1):256 * (pi + 1) + 256])], side=True)
                wv = wview(s, KC, 256)
                for mt in range(2):
                    m = 2 * pi + mt
                    mm_group(psa[:, 2 * m:2 * m + 2], bk,
                             [(wv[:, kc, 128 * mt:128 * mt + 128], SC[:, kc, :]) for kc in range(KC)],
                             reads=[bRING[s], bSC])
                yield
            p.op("dve", lambda e: e.tensor_tensor(out=MOD[:], in0=psa[:, 0:96].rearrange("p (m t) -> p m t", t=2),
                                                  in1=VEC[:, L * LW + O_ADAB:L * LW + O_ADAB + 48].unsqueeze(2).broadcast_to([128, 48, 2]),
                                                  op=ALU.add), reads=[bPS[bk], bVEC], writes=[bMOD])
            st["reserved"].discard(bk)
            for which, (osc, og) in enumerate(((8, O_GMIX), (32, O_GFFN))):
                p.op("dve", lambda e, which=which, osc=osc: e.tensor_scalar(out=AM[:, which], in0=MOD[:, osc:osc + 8, :], scalar1=1.0, scalar2=32.0,
                                                                              op0=ALU.add, op1=ALU.mult), reads=[bMOD, bAM], writes=[bAM])
                p.op("dve", lambda e, which=which, og=og: e.tensor_tensor(out=AM[:, which], in0=AM[:, which],
                                                                            in1=VEC[:, L * LW + og:L * LW + og + 8].unsqueeze(2).broadcast_to([128, 8, 2]),
                                                                            op=ALU.mult), reads=[bAM, bVEC], writes=[bAM])

        def norm_phase(which, blocks):
            osh = 0 if which == 0 else 24
            with ExitStack() as ph:
                SQ = [sb(f"SQ{i}", [128, KC, 512], BF16, ph) for i in range(2)]
                RR = [sb(f"RR{i}", [128, 512], F32, ph) for i in range(2)]
                TM = [sb(f"TM{i}", [128, 4, 512], F32, ph) for i in range(2)]
                bSQ = [Buf(), Buf()]
                bRR = [Buf(), Buf()]
                bTM = [Buf(), Buf()]
                for bi, (s, w, ic) in enumerate(blocks):
                    sq, bsq, rr, brr = SQ[bi % 2], bSQ[bi % 2], RR[bi % 2], bRR[bi % 2]
                    xb = segs(bXT, s, s + w)
                    hb = segs(bHT, s, s + w)
                    p.op("act", lambda e, sq=sq, s=s, w=w: e.activation(out=sq[:, :, 0:w], in_=XT[:, :, s:s + w], func=AF.Square),
                         reads=xb, writes=[bsq])
                    bk = next_bank()
                    mm_group(PS[bk][:, 0:w], bk, [(ONES[:], sq[:, kc, 0:w]) for kc in range(KC)], reads=[bsq, bONES])
                    p.op("act", lambda e, rr=rr, bk=bk, w=w: e.activation(out=rr[:, 0:w], in_=PS[bk][:, 0:w], func=AF.Sqrt, bias=EPSV[:, 0:1], scale=1.0),
                         reads=[bPS[bk], bEPS], writes=[brr])
                    p.op("dve", lambda e, rr=rr, w=w: e.reciprocal(out=rr[:, 0:w], in_=rr[:, 0:w]), reads=[brr], writes=[brr])
                    for half in range(2):
                        tm, btm = TM[half], bTM[half]
                        p.op("dve", lambda e, tm=tm, half=half, s=s, w=w, rr=rr: e.tensor_tensor(
                            out=tm[:, :, 0:w], in0=XT[:, 4 * half:4 * half + 4, s:s + w],
                            in1=rr[:, 0:w].unsqueeze(1).broadcast_to([128, 4, w]), op=ALU.mult),
                            reads=xb + [brr], writes=[btm])
                        for q in range(4):
                            kc = 4 * half + q
                            a_ap = cur["AM"][:, which, kc, ic:ic + 1]
                            b_ap = cur["MOD"][:, osh + kc, ic:ic + 1]
                            if q % 2 == 0:
                                p.op("dve", lambda e, tm=tm, q=q, kc=kc, s=s, w=w, a_ap=a_ap, b_ap=b_ap: e.tensor_scalar(
                                    out=HT[:, kc, s:s + w], in0=tm[:, q, 0:w], scalar1=a_ap, scalar2=b_ap, op0=ALU.mult, op1=ALU.add),
                                    reads=[btm, cur["bAM"], cur["bMOD"]], writes=hb)
                            else:
                                p.op("act", lambda e, tm=tm, q=q, kc=kc, s=s, w=w, a_ap=a_ap, b_ap=b_ap: e.activation(
                                    out=HT[:, kc, s:s + w], in_=tm[:, q, 0:w], func=AF.Identity, scale=a_ap, bias=b_ap),
                                    reads=[btm, cur["bAM"], cur["bMOD"]], writes=hb)
                p.barrier()

        def resid_add(bk, m, s, w, ic, gcol):
            g_ap = cur["MOD"][:, gcol + m, ic:ic + 1]
            xb = segs(bXT, s, s + w)
            p.op("dve", lambda e: e.scalar_tensor_tensor(out=XT[:, m, s:s + w], in0=PS[bk][:, 0:w], scalar=g_ap,
                                                         in1=XT[:, m, s:s + w], op0=ALU.mult, op1=ALU.add),
                 reads=[bPS[bk], cur["bMOD"]] + xb, writes=xb)

        def ffn_phase(L, halo, plain, side):
            vb = L * LW
            wup = ffn_w_up[L].rearrange("(k p) n -> p k n", p=128)
            groups = [list(range(0, 6)), list(range(6, 12)), list(range(12, 17)), list(range(17, 22))]
            ncols = 256
            loads = []
            for grp in groups:
                for i in grp:
                    loads.append([(0, KC, 128, 256, wup[:, :, 128 * i:128 * i + 128]),
                                  (128, KC, 128, 256, wup[:, :, DFF + 128 * i:DFF + 128 * i + 128])])
                gp, i0 = len(grp), grp[0]
                wdn = ffn_w_down[L][128 * i0:128 * (i0 + gp), :].rearrange("(k p) n -> p k n", p=128)
                for pj in range(1024 // ncols):
                    loads.append([(0, gp, ncols, ncols, wdn[:, :, ncols * pj:ncols * pj + ncols])])
            slots = {}
            PF = 2
            st["ring_n"], st["slot"] = NSLOT - 2, 0

            def get(idx):
                for k in range(idx, min(idx + PF + 1, len(loads))):
                    if k not in slots:
                        slots[k] = ring_load(loads[k])
                return slots[idx]

            with ExitStack() as ph:
                FT = sb("FT", [128, 6, NCOL], BF16, ph)
                bFT = [Buf(f"FT{i}") for i in range(5)]
                VA = [sb(f"VA{i}", [128, 412], F32, ph) for i in range(3)]
                GA = [sb(f"GA{i}", [128, 412], F32, ph) for i in range(3)]
                SG = [sb(f"SG{i}", [128, 412], F32, ph) for i in range(3)]
                bVA, bGA, bSG = [Buf() for _ in range(3)], [Buf() for _ in range(3)], [Buf() for _ in range(3)]
                it = 0
                li = 0
                pend_s = [None]

                def kcol(d, ch):
                    return VEC[:, vb + O_FK + d * 44 + ch:vb + O_FK + d * 44 + ch + 1]

                def stage_s():
                    if pend_s[0] is None:
                        return
                    (sg, ga, va, bsg, bga, bva, n, il, s) = pend_s[0]
                    pend_s[0] = None
                    p.op("act", lambda e: e.activation(out=sg[:, 0:n], in_=ga[:, 0:n], func=AF.Silu), reads=[bga], writes=[bsg])
                    p.op(FFN_MUL_ENG, lambda e: e.tensor_tensor(out=FT[:, il, s + 1:s + 1 + n], in0=va[:, 0:n], in1=sg[:, 0:n], op=ALU.mult),
                         reads=[bva, bsg], writes=segs(bFT, s + 1, s + 1 + n))

                for grp in groups:
                    gp = len(grp)
                    for il, i in enumerate(grp):
                        s_ = get(li)
                        li += 1
                        wv = wview(s_, KC, 256)
                        for (s, w, ic) in halo:
                            ba, bg = next_bank(), next_bank()
                            hb = segs(bHT, s, s + w)
                            mm_group(PS[bg][:, 0:w], bg, [(wv[:, kc, 128:256], HT[:, kc, s:s + w]) for kc in range(KC)], reads=[bRING[s_]] + hb)
                            mm_group(PS[ba][:, 0:w], ba, [(wv[:, kc, 0:128], HT[:, kc, s:s + w]) for kc in range(KC)], reads=[bRING[s_]] + hb)
                            va, ga, sg = VA[it % 3], GA[it % 3], SG[it % 3]
                            bva, bga, bsg = bVA[it % 3], bGA[it % 3], bSG[it % 3]
                            it += 1
                            n = w - 2
                            rows = ((ga, bga, bg, NPAIR + i), (va, bva, ba, i))
                            for (acc, bacc, bkk, ch) in rows:
                                p.op("act", lambda e, acc=acc, bkk=bkk, ch=ch: e.activation(
                                    out=acc[:, 0:n], in_=PS[bkk][:, 1:1 + n], func=AF.Identity, scale=kcol(1, ch),
                                    bias=VEC[:, vb + O_FB + ch:vb + O_FB + ch + 1]), reads=[bPS[bkk], bVEC], writes=[bacc])
                            stage_s()
                            for d in (0, 2):
                                for (acc, bacc, bkk, ch) in rows:
                                    p.op("dve", lambda e, acc=acc, bkk=bkk, ch=ch, d=d: e.scalar_tensor_tensor(
                                        out=acc[:, 0:n], in0=PS[bkk][:, d:d + n], scalar=kcol(d, ch), in1=acc[:, 0:n],
                                        op0=ALU.mult, op1=ALU.add), reads=[bPS[bkk], bVEC, bacc], writes=[bacc])
                            pend_s[0] = (sg, ga, va, bsg, bga, bva, n, il, s)
                        next(side, None)
                    stage_s()
                    for pj in range(1024 // ncols):
                        s_ = get(li)
                        li += 1
                        wv = wview(s_, gp, ncols)
                        for mt in range(ncols // 128):
                            m = pj * (ncols // 128) + mt
                            for (s, w, ic) in plain:
                                bk = next_bank()
                                mm_group(PS[bk][:, 0:w], bk, [(wv[:, k, 128 * mt:128 * mt + 128], FT[:, k, s:s + w]) for k in range(gp)],
                                         reads=[bRING[s_]] + segs(bFT, s, s + w))
                                resid_add(bk, m, s, w, ic, 40)
                    next(side, None)
                for _ in side:
                    pass
                p.barrier()
                st["ring_n"], st["slot"] = NSLOT, 0

        def conv_phase(L, halo, plain):
            j = L // 3
            vb = L * LW
            win = conv_w_in[j].rearrange("(k p) n -> p k n", p=128)
            wout = conv_w_out[j].rearrange("(k p) n -> p k n", p=128)
            with ExitStack() as ph:
                MT = sb("MT", [128, KC, NCOL], BF16, ph)
                bMT = [Buf(f"MT{i}") for i in range(5)]
                CS = [sb(f"CS{i}", [128, 412], F32, ph) for i in range(2)]
                CVV = [sb(f"CVV{i}", [128, 412], F32, ph) for i in range(2)]
                TT = [sb(f"TT{i}", [128, 412], F32, ph) for i in range(2)]
                bCS, bCVV, bTT = [Buf(), Buf()], [Buf(), Buf()], [Buf(), Buf()]
                it = 0
                for jc in range(KC):
                    s1 = ring_load([(0, KC, 128, 256, win[:, :, D + 128 * jc:D + 128 * jc + 128]),
                                    (128, KC, 128, 256, win[:, :, 2 * D + 128 * jc:2 * D + 128 * jc + 128])])
                    s2 = ring_load([(0, KC, 128, 128, win[:, :, 128 * jc:128 * jc + 128])])
                    w1 = wview(s1, KC, 256)
                    w2 = wview(s2, KC, 128)
                    for (s, w, ic) in halo:
                        bc, bv, bb = next_bank(), next_bank(), next_bank()
                        hb = segs(bHT, s, s + w)
                        mm_group(PS[bc][:, 0:w], bc, [(w1[:, kc, 0:128], HT[:, kc, s:s + w]) for kc in range(KC)], reads=[bRING[s1]] + hb)
                        mm_group(PS[bv][:, 0:w], bv, [(w1[:, kc, 128:256], HT[:, kc, s:s + w]) for kc in range(KC)], reads=[bRING[s1]] + hb)
                        mm_group(PS[bb][:, 0:w], bb, [(w2[:, kc, 0:128], HT[:, kc, s:s + w]) for kc in range(KC)], reads=[bRING[s2]] + hb)
                        cs, cvv, tt = CS[it % 2], CVV[it % 2], TT[it % 2]
                        bcs, bcvv, btt = bCS[it % 2], bCVV[it % 2], bTT[it % 2]
                        it += 1
                        n = w - 2

                        def kcol(d):
                            return VEC[:, vb + O_MK + d * 8 + jc:vb + O_MK + d * 8 + jc + 1]
                        p.op("act", lambda e, cs=cs, bc=bc, w=w: e.activation(out=cs[:, 0:w], in_=PS[bc][:, 0:w], func=AF.Copy),
                             reads=[bPS[bc]], writes=[bcs])
                        p.op("dve", lambda e, cvv=cvv, cs=cs, bv=bv, w=w: e.tensor_tensor(out=cvv[:, 0:w], in0=cs[:, 0:w], in1=PS[bv][:, 0:w], op=ALU.mult),
                             reads=[bcs, bPS[bv]], writes=[bcvv])
                        p.op("act", lambda e, tt=tt, cvv=cvv, n=n: e.activation(out=tt[:, 0:n], in_=cvv[:, 1:1 + n], func=AF.Copy, scale=kcol(1)),
                             reads=[bcvv, bVEC], writes=[btt])
                        for d in (0, 2):
                            p.op("dve", lambda e, tt=tt, cvv=cvv, n=n, d=d: e.scalar_tensor_tensor(
                                out=tt[:, 0:n], in0=cvv[:, d:d + n], scalar=kcol(d), in1=tt[:, 0:n], op0=ALU.mult, op1=ALU.add),
                                reads=[bcvv, bVEC, btt], writes=[btt])
                        p.op("dve", lambda e, tt=tt, bb=bb, n=n, s=s: e.tensor_tensor(
                            out=MT[:, jc, s + 1:s + 1 + n], in0=tt[:, 0:n], in1=PS[bb][:, 1:1 + n], op=ALU.mult),
                            reads=[btt, bPS[bb]], writes=segs(bMT, s + 1, s + 1 + n))
                for pj in range(4):
                    s_ = ring_load([(0, KC, 256, 256, wout[:, :, 256 * pj:256 * pj + 256])])
                    wv = wview(s_, KC, 256)
                    for mt in range(2):
                        m = 2 * pj + mt
                        for (s, w, ic) in plain:
                            bk = next_bank()
                            mm_group(PS[bk][:, 0:w], bk, [(wv[:, kc, 128 * mt:128 * mt + 128], MT[:, kc, s:s + w]) for kc in range(KC)],
                                     reads=[bRING[s_]] + segs(bMT, s, s + w))
                            resid_add(bk, m, s, w, ic, 16)
                p.barrier()


        def attn_phase(L):
            wq_all = attn_wqkv.rearrange("(k p) n -> p k n", p=128)
            plain_all = PLAIN_LAT + PLAIN_CTX
            with ExitStack() as ph:
                KT = sb("KT", [128, 2, NCOL], BF16, ph)
                VA = sb("VA", [128, 18, 4, 128], BF16, ph)
                QT = [sb(f"QT{i}", [128, NCOL], BF16, ph) for i in range(2)]
                OT = [sb(f"OT{i}", [128, NCOL], BF16, ph) for i in range(2)]
                TAB = [sb(f"TAB{i}", [128, 2, 512], F32, ph) for i in range(2)]
                NPT = 6
                PT = [sb(f"PT{i}", [128, 512], BF16, ph) for i in range(NPT)]
                SQ = sb("aSQ", [128, 512], BF16, ph)
                QG = sb("aQG", [128, 512], BF16, ph)
                RR = sb("aRR", [128, 512], F32, ph)
                T1 = sb("aT1", [128, 512], F32, ph)
                T2 = sb("aT2", [128, 512], F32, ph)
                RC = [sb(f"aRC{i}", [128, 512], F32, ph) for i in range(2)]
                bKT = [Buf(), Buf()]
                bVA = Buf()
                bOT = [Buf(), Buf()]
                bQT = [Buf(), Buf()]
                bTAB = [Buf(), Buf()]
                bPT = [Buf() for _ in range(NPT)]
                bSQ, bQG, bRR, bT1, bT2 = Buf(), Buf(), Buf(), Buf(), Buf()
                bRC = [Buf(), Buf()]
                cnt = {"tab": 0, "pt": 0, "rc": 0}

                p.op("pool", lambda e: e.memset(VA[:].rearrange("p a b c -> p (a b c)"), 1.0), writes=[bVA])

                def reserve(b):
                    st["reserved"].add(b)

                def release(b):
                    st["reserved"].discard(b)

                def qk_gen(wv, s_, dst_of, bdst, gcol):
                    g_ap = VEC[:, gcol:gcol + 1]
                    for (s, w, ic) in plain_all:
                        rope = (ic == 0)
                        dst_ap = dst_of(s, w)
                        bk = next_bank()
                        reserve(bk)
                        mm_group(PS[bk][:, 0:w], bk, [(wv[:, kc, :], HT[:, kc, s:s + w]) for kc in range(KC)], reads=[bRING[s_]] + segs(bHT, s, s + w))
                        if rope:
                            ti = cnt["tab"] % 2
                            cnt["tab"] += 1
                            tab, btab = TAB[ti], bTAB[ti]
                            t0 = s - LAT0
                            p.dma("sp", lambda e: e.dma_start(out=tab[:, 0, :], in_=rope_a[0][:, t0:t0 + 512]), writes=[btab])
                            p.dma("sp", lambda e: e.dma_start(out=tab[:, 1, :], in_=rope_a[1][:, t0:t0 + 512]), writes=[btab])
                        yield
                        p.op("act", lambda e: e.activation(out=SQ[:, 0:w], in_=PS[bk][:, 0:w], func=AF.Square), reads=[bPS[bk]], writes=[bSQ])
                        if rope:
                            p.op("act", lambda e: e.activation(out=QG[:, 0:w], in_=PS[bk][:, 0:w], func=AF.Copy, scale=g_ap), reads=[bPS[bk], bVEC], writes=[bQG])
                        yield
                        b2 = next_bank()
                        reserve(b2)
                        mm_group(PS[b2][:, 0:w], b2, [(CM[:, 0, :], SQ[:, 0:w])], reads=[bSQ, bCM])
                        yield
                        p.op("act", lambda e: e.activation(out=RR[:, 0:w], in_=PS[b2][:, 0:w], func=AF.Ln, bias=EPSV[:, 1:2], scale=1.0 / 64),
                             reads=[bPS[b2], bEPS], writes=[bRR])
                        if rope:
                            mm_group(PS[b2][:, 0:w], b2, [(CM[:, 1, :], QG[:, 0:w])], reads=[bQG, bCM])
                        yield
                        p.op("act", lambda e: e.activation(out=RR[:, 0:w], in_=RR[:, 0:w], func=AF.Exp, scale=-0.5), reads=[bRR], writes=[bRR])
                        if not rope:
                            p.op("dve", lambda e: e.scalar_tensor_tensor(out=dst_ap, in0=PS[bk][:, 0:w], scalar=g_ap, in1=RR[:, 0:w],
                                                                         op0=ALU.mult, op1=ALU.mult), reads=[bPS[bk], bVEC, bRR], writes=[bdst])
                        else:
                            p.op("dve", lambda e: e.scalar_tensor_tensor(out=T1[:, 0:w], in0=PS[bk][:, 0:w], scalar=g_ap, in1=tab[:, 0, 0:w],
                                                                         op0=ALU.mult, op1=ALU.mult), reads=[bPS[bk], bVEC, btab], writes=[bT1])
                            p.op("dve", lambda e: e.tensor_tensor(out=T2[:, 0:w], in0=PS[b2][:, 0:w], in1=tab[:, 1, 0:w], op=ALU.mult),
                                 reads=[bPS[b2], btab], writes=[bT2])
                            yield
                            p.op("dve", lambda e: e.tensor_tensor(out=T1[:, 0:w], in0=T1[:, 0:w], in1=T2[:, 0:w], op=ALU.add), reads=[bT1, bT2], writes=[bT1])
                            p.op("dve", lambda e: e.tensor_tensor(out=dst_ap, in0=T1[:, 0:w], in1=RR[:, 0:w], op=ALU.mult), reads=[bT1, bRR], writes=[bdst])
                        release(bk)
                        release(b2)
                        yield

                for g2 in range(2):
                    s_ = ring_load([(0, KC, 128, 128, wq_all[:, :, 1024 + 128 * g2:1024 + 128 * g2 + 128])])
                    for _ in qk_gen(wview(s_, KC, 128), s_, lambda s, w, g2=g2: KT[:, g2, s:s + w], bKT[g2], O_KG):
                        pass
                s_ = ring_load([(0, KC, 256, 256, wq_all[:, :, 1280:1536])])
                wv = wview(s_, KC, 256)
                for kt in range(18):
                    col = LAT0 + 128 * kt if kt < 16 else CTX0 + 128 * (kt - 16)
                    bk = next_bank()
                    mm_group(PS[bk][:, 0:256], bk, [(HT[:, kc, col:col + 128], wv[:, kc, :]) for kc in range(KC)],
                             reads=[bRING[s_]] + segs(bHT, col, col + 128))
                    p.op("act", lambda e, kt=kt, bk=bk: e.activation(out=VA[:, kt, :, 0:64], in_=PS[bk][:, 0:256].rearrange("p (h d) -> p h d", d=64), func=AF.Copy),
                         reads=[bPS[bk]], writes=[bVA])

                def heads_of(j):
                    g2, r = j // 4, j % 4
                    return g2, 8 * g2 + r, 8 * g2 + 4 + r

                def q_gen(j):
                    g2, ha, hb = heads_of(j)
                    s_ = ring_load([(0, KC, 64, 128, wq_all[:, :, 64 * ha:64 * ha + 64]),
                                    (64, KC, 64, 128, wq_all[:, :, 64 * hb:64 * hb + 64])])
                    yield from qk_gen(wview(s_, KC, 128), s_, lambda s, w: QT[j % 2][:, s:s + w], bQT[j % 2], O_QG)

                def outproj_gen(j):
                    g2, ha, hb = heads_of(j)
                    ot, bot = OT[j % 2], bOT[j % 2]
                    s_ = st["slot"]
                    st["slot"] = (s_ + 1) % NSLOT
                    for hh, hd in enumerate((ha, hb)):
                        p.dma("pool", lambda e, hh=hh, hd=hd: e.dma_start(out=RING[64 * hh:64 * hh + 64, s_, 0:1024], in_=attn_w_out[64 * hd:64 * hd + 64, :]),
                              writes=[bRING[s_]])
                    yield
                    for m in range(8):
                        for (s, w, ic) in plain_all:
                            bk = next_bank()
                            mm_group(PS[bk][:, 0:w], bk, [(RING[:, s_, 128 * m:128 * m + 128], ot[:, s:s + w])], reads=[bRING[s_], bot])
                            resid_add(bk, m, s, w, ic, 16)
                            yield

                def attend(j, sides):
                    g2, ha, hb = heads_of(j)
                    qt, bqt = QT[j % 2], bQT[j % 2]
                    ot, bot = OT[j % 2], bOT[j % 2]
                    rr_i = [0]

                    def pull():
                        for _ in range(len(sides)):
                            g = sides[rr_i[0] % len(sides)]
                            rr_i[0] += 1
                            try:
                                next(g)
                                return
                            except StopIteration:
                                continue

                    def pv(item, nk):
                        (hh, pi, kt_, idx_, oa, w) = item
                        p.op("pe", lambda e: e.matmul(PS[oa[hh]][:, 0:w], lhsT=VA[:, kt_, 2 * g2 + hh, :], rhs=PT[pi][:, 0:w],
                                                      start=(idx_ == 0), stop=(idx_ == nk - 1)),
                             reads=[bVA, bPT[pi]], writes=[bPS[oa[hh]]], inc=True)

                    for (s, w, ic) in plain_all:
                        kts = list(range(18)) if ic == 0 else [16, 17]
                        oa = [next_bank(), next_bank()]
                        reserve(oa[0])
                        reserve(oa[1])
                        pend = None
                        for idx, kt in enumerate(kts):
                            kcol = LAT0 + 128 * kt if kt < 16 else CTX0 + 128 * (kt - 16)
                            cur_ = []
                            for hh in range(2):
                                bs = next_bank()
                                lo = 64 * hh
                                mm_group(PS[bs][:, 0:w], bs, [(KT[lo:lo + 64, g2, kcol:kcol + 128], qt[lo:lo + 64, s:s + w])], reads=[bKT[g2], bqt])
                                pi = cnt["pt"] % NPT
                                cnt["pt"] += 1
                                p.op("act", lambda e, pi=pi, bs=bs: e.activation(out=PT[pi][:, 0:w], in_=PS[bs][:, 0:w], func=AF.Exp, scale=0.125),
                                     reads=[bPS[bs]], writes=[bPT[pi]])
                                cur_.append((hh, pi, kt, idx, oa, w))
                            if pend is not None:
                                for item in pend:
                                    pv(item, len(kts))
                            pend = cur_
                            pull()
                        for item in pend:
                            pv(item, len(kts))
                        for hh in range(2):
                            ri = cnt["rc"] % 2
                            cnt["rc"] += 1
                            rc, brc = RC[ri], bRC[ri]
                            p.op("dve", lambda e, rc=rc, hh=hh: e.reciprocal(out=rc[0:64, 0:w], in_=PS[oa[hh]][64:128, 0:w]), reads=[bPS[oa[hh]]], writes=[brc])
                            p.op("dve", lambda e, rc=rc, hh=hh: e.tensor_tensor(out=ot[64 * hh:64 * hh + 64, s:s + w], in0=PS[oa[hh]][0:64, 0:w],
                                                                                in1=rc[0:64, 0:w], op=ALU.mult), reads=[bPS[oa[hh]], brc], writes=[bot])
                        release(oa[0])
                        release(oa[1])
                    for g in sides:
                        for _ in g:
                            pass

                for _ in q_gen(0):
                    pass
                prev_out = None
                for j in range(8):
                    sides = []
                    if j + 1 < 8:
                        sides.append(q_gen(j + 1))
                    if prev_out is not None:
                        sides.append(prev_out)
                    attend(j, sides)
                    prev_out = outproj_gen(j)
                for _ in prev_out:
                    pass
                p.barrier()

        def ret_phase(L):
            import math
            LN16 = math.log(16.0)
            win = ret_w_in.rearrange("(k p) n -> p k n", p=128)
            with ExitStack() as ph:
                QTh = sb("QTh", [128, 2, SEQ], BF16, ph)
                KTh = sb("KTh", [128, 2, NCOL], BF16, ph)
                VH = sb("VH", [128, 18, 512], BF16, ph)
                GS = [sb(f"GS{i}", [128, 4, 512], BF16, ph) for i in range(2)]
                SQY = sb("SQY", [128, 4, 512], BF16, ph)
                Z = SQY
                TMP = sb("rTMP", [128, 512], F32, ph)
                NPTR, NMK = 3, 4
                PT = [sb(f"rPT{i}", [128, 512], BF16, ph) for i in range(NPTR)]
                MK = [sb(f"rMK{i}", [128, 512], F32, ph) for i in range(NMK)]
                QB = sb("rQB", [128, 512], BF16, ph)
                IOTA = sb("IOTA", [128, 512], F32, ph)
                RTAB = sb("RTAB", [128, 2, 96], F32, ph)
                RRr = sb("rRR", [128, 512], F32, ph)
                LG = sb("LG", [128, 8], F32, ph)
                NLG = sb("NLG", [128, 8], F32, ph)
                LT = sb("LT", [128, 8], F32, ph)
                BIAS = sb("BIAS", [128, 8], F32, ph)
                bQTh, bKTh, bVH, bSQY, bTMP, bQB, bIOTA, bRTAB, bRRr, bLG, bLT = [Buf() for _ in range(11)]
                bZ = bSQY
                bGS = [Buf(), Buf()]
                bPT = [Buf() for _ in range(NPTR)]
                bMK = [Buf() for _ in range(NMK)]
                bBIAS = [Buf() for _ in range(8)]
                cnt = {"pt": 0, "mk": 0, "bias": 0, "rb": 0}

                def rb():
                    b = 4 + cnt["rb"] % 4
                    cnt["rb"] += 1
                    return b

                def nmk():
                    i = cnt["mk"] % NMK
                    cnt["mk"] += 1
                    return i

                def nbias():
                    i = cnt["bias"] % 8
                    cnt["bias"] += 1
                    return i

                p.dma("sp", lambda e: e.dma_start(out=IOTA[:], in_=iota_d), writes=[bIOTA])
                p.dma("sp", lambda e: e.dma_start(out=RTAB[:, 0, :], in_=rope_r[0]), writes=[bRTAB])
                p.dma("sp", lambda e: e.dma_start(out=RTAB[:, 1, :], in_=rope_r[1]), writes=[bRTAB])
                p.op("act", lambda e: e.activation(out=LT[:], in_=VEC[:, O_DEC:O_DEC + 8], func=AF.Exp, scale=-math.log(2.0)), reads=[bVEC], writes=[bLT])
                p.op("dve", lambda e: e.tensor_scalar(out=LG[:], in0=LT[:], scalar1=1.0 / 6, scalar2=0.2, op0=ALU.mult, op1=ALU.add), reads=[bLT], writes=[bLG])
                for cst in (0.25, 1.0 / 3, 0.5, 1.0):
                    p.op("dve", lambda e: e.tensor_tensor(out=LG[:], in0=LG[:], in1=LT[:], op=ALU.mult), reads=[bLG, bLT], writes=[bLG])
                    p.op("dve", lambda e, cst=cst: e.tensor_scalar(out=LG[:], in0=LG[:], scalar1=cst, scalar2=None, op0=ALU.add), reads=[bLG], writes=[bLG])
                p.op("dve", lambda e: e.tensor_tensor(out=NLG[:], in0=LG[:], in1=LT[:], op=ALU.mult), reads=[bLG, bLT], writes=[bLG])
                p.op("dve", lambda e: e.tensor_scalar(out=LG[:], in0=NLG[:], scalar1=-1.0, scalar2=None, op0=ALU.mult), reads=[bLG], writes=[bLG])

                def rope_post(bk, dst3, bdst, sg, blk):
                    if sg == 0:
                        c_ap = RTAB[:, 0, 8 * blk:8 * blk + 8].unsqueeze(2).broadcast_to([128, 8, 64])
                        s_ap = RTAB[:, 1, 8 * blk:8 * blk + 8].unsqueeze(2).broadcast_to([128, 8, 64])
                    else:
                        c_ap = RTAB[:, 0, 32:96].unsqueeze(1).broadcast_to([128, 8, 64])
                        s_ap = RTAB[:, 1, 32:96].unsqueeze(1).broadcast_to([128, 8, 64])
                    if RET_DBG == 21:
                        c_ap = IOTA[:].rearrange("p (r c) -> p r c", c=64)
                        s_ap = IOTA[:].rearrange("p (r c) -> p r c", c=64)
                    if RET_DBG == 22 and sg == 1:
                        c_ap = RTAB[:, 0, 0:8].unsqueeze(2).broadcast_to([128, 8, 64])
                        s_ap = RTAB[:, 1, 0:8].unsqueeze(2).broadcast_to([128, 8, 64])
                    p.op("act", lambda e: e.activation(out=QB[:], in_=PS[bk][:], func=AF.Copy), reads=[bPS[bk]], writes=[bQB])
                    b3 = next_bank()
                    mm_group(PS[b3][:], b3, [(CM[:, 2, :], QB[:])], reads=[bQB, bCM])
                    i1, i2 = nmk(), nmk()
                    v3 = lambda ap: ap.rearrange("p (r c) -> p r c", c=64)
                    p.op("dve", lambda e: e.tensor_tensor(out=v3(MK[i1][:]), in0=v3(PS[bk][:]), in1=c_ap, op=ALU.mult), reads=[bPS[bk], bRTAB], writes=[bMK[i1]])
                    p.op("dve", lambda e: e.tensor_tensor(out=v3(MK[i2][:]), in0=v3(PS[b3][:]), in1=s_ap, op=ALU.mult), reads=[bPS[b3], bRTAB], writes=[bMK[i2]])
                    p.op("dve", lambda e: e.tensor_tensor(out=dst3, in0=MK[i1][:], in1=MK[i2][:], op=ALU.add), reads=[bMK[i1], bMK[i2]], writes=[bdst])

                for h in range(RET_NH):
                    if RET_DBG == 1:
                        break
                    lgf, nlgf = LG[:, h:h + 1], NLG[:, h:h + 1]
                    lgb, nlgb = LG[:, 4 + h:5 + h], NLG[:, 4 + h:5 + h]
                    for which, (dstT, bdst, c0) in enumerate(((QTh, bQTh, 256 * h), (KTh, bKTh, 1024 + 256 * h))):
                        s_ = ring_load([(0, KC, 256, 256, win[:, :, c0:c0 + 256])])
                        wv = wview(s_, KC, 256)
                        for sg in range(2):
                            for blk, (s, w, ic) in enumerate(PLAIN_LAT):
                                bk = next_bank()
                                mm_group(PS[bk][:], bk, [(wv[:, kc, 128 * sg:128 * sg + 128], HT[:, kc, s:s + w]) for kc in range(KC)],
                                         reads=[bRING[s_]] + segs(bHT, s, s + w))
                                cbase = (s - LAT0) if which == 0 else s
                                rope_post(bk, dstT[:, sg, cbase:cbase + 512], bdst, sg, blk)
                            if which == 1:
                                (s, w, ic) = PLAIN_CTX[0]
                                bk = next_bank()
                                mm_group(PS[bk][:, 0:w], bk, [(wv[:, kc, 128 * sg:128 * sg + 128], HT[:, kc, s:s + w]) for kc in range(KC)],
                                         reads=[bRING[s_]] + segs(bHT, s, s + w))
                                p.op("act", lambda e, bk=bk, sg=sg, s=s, w=w: e.activation(out=KTh[:, sg, s:s + w], in_=PS[bk][:, 0:w], func=AF.Copy),
                                     reads=[bPS[bk]], writes=[bKTh])
                    if RET_DBG in (2, 21, 22):
                        break
                    sv = [ring_load([(0, KC, 256, 256, win[:, :, 2048 + 512 * h + 256 * i:2048 + 512 * h + 256 * i + 256])]) for i in range(2)]
                    for kt in range(18):
                        col = LAT0 + 128 * kt if kt < 16 else CTX0 + 128 * (kt - 16)
                        bk = next_bank()
                        for i in range(2):
                            wv = wview(sv[i], KC, 256)
                            mm_group(PS[bk][:, 256 * i:256 * i + 256], bk, [(HT[:, kc, col:col + 128], wv[:, kc, :]) for kc in range(KC)],
                                     reads=[bRING[sv[i]]] + segs(bHT, col, col + 128))
                        if kt % 2 == 0:
                            p.op("act", lambda e, kt=kt, bk=bk: e.activation(out=VH[:, kt, :], in_=PS[bk][:], func=AF.Copy), reads=[bPS[bk]], writes=[bVH])
                        else:
                            p.op("dve", lambda e, kt=kt, bk=bk: e.tensor_copy(out=VH[:, kt, :], in_=PS[bk][:]), reads=[bPS[bk]], writes=[bVH])

                    if RET_DBG == 3:
                        break
                    def gen_mask(qb, kt):
                        if kt < 16:
                            off = 512 * qb - 128 * kt
                            if off >= 128 or off <= -512:
                                sc = lgf if off >= 128 else nlgb
                                bi = nbias()
                                p.op("dve", lambda e: e.tensor_scalar(out=BIAS[:, bi:bi + 1], in0=sc, scalar1=float(off), scalar2=-LN16,
                                                                      op0=ALU.mult, op1=ALU.add), reads=[bLG], writes=[bBIAS[bi]])
                                mi = nmk()
                                p.op("act", lambda e: e.activation(out=MK[mi][:], in_=IOTA[:], func=AF.Exp, scale=sc, bias=BIAS[:, bi:bi + 1]),
                                     reads=[bIOTA, bLG, bBIAS[bi]], writes=[bMK[mi]])
                                return [mi]
                            m1, m2 = nmk(), nmk()
                            bi = nbias()
                            p.op("dve", lambda e: e.memset(BIAS[:, bi:bi + 1], -LN16), writes=[bBIAS[bi]])
                            p.op("dve", lambda e: e.tensor_scalar(out=MK[m1][:], in0=IOTA[:], scalar1=float(off), scalar2=0.0, op0=ALU.add, op1=ALU.max),
                                 reads=[bIOTA], writes=[bMK[m1]])
                            p.op("dve", lambda e: e.scalar_tensor_tensor(out=MK[m2][:], in0=IOTA[:], scalar=float(off), in1=MK[m1][:],
                                                                         op0=ALU.add, op1=ALU.subtract), reads=[bIOTA, bMK[m1]], writes=[bMK[m2]])
                            p.op("act", lambda e: e.activation(out=MK[m1][:], in_=MK[m1][:], func=AF.Exp, scale=lgf, bias=BIAS[:, bi:bi + 1]),
                                 reads=[bMK[m1], bLG, bBIAS[bi]], writes=[bMK[m1]])
                            p.op("act", lambda e: e.activation(out=MK[m2][:], in_=MK[m2][:], func=AF.Exp, scale=nlgb), reads=[bMK[m2], bLG], writes=[bMK[m2]])
                            return [m1, m2]
                        a_ = kt - 16
                        m1, m2 = nmk(), nmk()
                        b1, b2 = nbias(), nbias()
                        o1 = float(512 * qb + 256 - 128 * a_)
                        o2 = float(2048 - 512 * qb + 128 * a_)
                        p.op("dve", lambda e: e.tensor_scalar(out=BIAS[:, b1:b1 + 1], in0=lgf, scalar1=o1, scalar2=-LN16, op0=ALU.mult, op1=ALU.add),
                             reads=[bLG], writes=[bBIAS[b1]])
                        p.op("dve", lambda e: e.tensor_scalar(out=BIAS[:, b2:b2 + 1], in0=lgb, scalar1=o2, scalar2=-LN16, op0=ALU.mult, op1=ALU.add),
                             reads=[bLG], writes=[bBIAS[b2]])
                        p.op("act", lambda e: e.activation(out=MK[m1][:], in_=IOTA[:], func=AF.Exp, scale=lgf, bias=BIAS[:, b1:b1 + 1]),
                             reads=[bIOTA, bLG, bBIAS[b1]], writes=[bMK[m1]])
                        p.op("act", lambda e: e.activation(out=MK[m2][:], in_=IOTA[:], func=AF.Exp, scale=nlgb, bias=BIAS[:, b2:b2 + 1]),
                             reads=[bIOTA, bLG, bBIAS[b2]], writes=[bMK[m2]])
                        p.op("dve", lambda e: e.tensor_tensor(out=MK[m1][:], in0=MK[m1][:], in1=MK[m2][:], op=ALU.add),
                             reads=[bMK[m1], bMK[m2]], writes=[bMK[m1]])
                        return [m1]

                    def st_mm(qb, kt):
                        q0 = 512 * qb
                        kcol = LAT0 + 128 * kt if kt < 16 else CTX0 + 128 * (kt - 16)
                        bs = rb()
                        mm_group(PS[bs][:], bs, [(KTh[:, sg, kcol:kcol + 128], QTh[:, sg, q0:q0 + 512]) for sg in range(2)], reads=[bKTh, bQTh])
                        return bs

                    def apply_mask(bs, mks):
                        pi = cnt["pt"] % NPTR
                        cnt["pt"] += 1
                        if len(mks) == 1:
                            p.op("dve", lambda e: e.tensor_tensor(out=PT[pi][:], in0=PS[bs][:], in1=MK[mks[0]][:], op=ALU.mult),
                                 reads=[bPS[bs], bMK[mks[0]]], writes=[bPT[pi]])
                        else:
                            p.op("dve", lambda e: e.tensor_tensor(out=TMP[:], in0=PS[bs][:], in1=MK[mks[0]][:], op=ALU.mult),
                                 reads=[bPS[bs], bMK[mks[0]]], writes=[bTMP])
                            p.op("dve", lambda e: e.tensor_tensor(out=PT[pi][:], in0=TMP[:], in1=MK[mks[1]][:], op=ALU.mult),
                                 reads=[bTMP, bMK[mks[1]]], writes=[bPT[pi]])
                        return pi

                    def pv_mm(kt, pi):
                        for dv in range(4):
                            p.op("pe", lambda e, dv=dv: e.matmul(PS[dv][:], lhsT=VH[:, kt, 128 * dv:128 * dv + 128], rhs=PT[pi][:],
                                                                 start=(kt == 0), stop=(kt == 17)),
                                 reads=[bVH, bPT[pi]], writes=[bPS[dv]], inc=(dv == 3))

                    def g_proj(qb):
                        (s, w, ic) = PLAIN_LAT[qb]
                        sg_ = [ring_load([(0, KC, 256, 256, win[:, :, 4096 + 512 * h + 256 * i:4096 + 512 * h + 256 * i + 256])]) for i in range(2)]
                        for m in range(4):
                            wv = wview(sg_[m // 2], KC, 256)
                            bk = rb()
                            mm_group(PS[bk][:], bk, [(wv[:, kc, 128 * (m % 2):128 * (m % 2) + 128], HT[:, kc, s:s + w]) for kc in range(KC)],
                                     reads=[bRING[sg_[m // 2]]] + segs(bHT, s, s + w))
                            p.op("act", lambda e, m=m, bk=bk: e.activation(out=GS[qb % 2][:, m, :], in_=PS[bk][:], func=AF.Silu), reads=[bPS[bk]], writes=[bGS[qb % 2]])

                    g_proj(0)
                    for qb, (s, w, ic) in enumerate(PLAIN_LAT):
                        LOOK = 1
                        masks = {}
                        for k in range(min(LOOK, 18)):
                            masks[k] = gen_mask(qb, k)
                        stb = {0: st_mm(qb, 0)}
                        pts = {}
                        for i in range(18):
                            if i + LOOK < 18:
                                masks[i + LOOK] = gen_mask(qb, i + LOOK)
                            if i + 1 < 18:
                                stb[i + 1] = st_mm(qb, i + 1)
                            pts[i] = apply_mask(stb[i], masks[i])
                            if i >= 1:
                                pv_mm(i - 1, pts[i - 1])
                        pv_mm(17, pts[17])
                        for dv in range(4):
                            p.op("act", lambda e, dv=dv: e.activation(out=SQY[:, dv, :], in_=PS[dv][:], func=AF.Square), reads=[bPS[dv]], writes=[bSQY])
                        bss = rb()
                        mm_group(PS[bss][:], bss, [(ONES[:], SQY[:, dv, :]) for dv in range(4)], reads=[bSQY, bONES])
                        p.op("act", lambda e, bss=bss: e.activation(out=RRr[:], in_=PS[bss][:], func=AF.Ln, bias=EPSV[:, 1:2], scale=1.0 / 512),
                             reads=[bPS[bss], bEPS], writes=[bRRr])
                        if qb + 1 < 4:
                            g_proj(qb + 1)
                        p.op("act", lambda e: e.activation(out=RRr[:], in_=RRr[:], func=AF.Exp, scale=-0.5), reads=[bRRr], writes=[bRRr])
                        for dv in range(4):
                            p.op("dve", lambda e, dv=dv: e.tensor_tensor(out=TMP[:], in0=PS[dv][:], in1=RRr[:], op=ALU.mult), reads=[bPS[dv], bRRr], writes=[bTMP])
                            p.op("dve", lambda e, dv=dv: e.tensor_tensor(out=Z[:, dv, :], in0=TMP[:], in1=GS[qb % 2][:, dv, :], op=ALU.mult), reads=[bTMP, bGS[qb % 2]], writes=[bZ])
                        wo = ret_w_out[512 * h:512 * h + 512, :].rearrange("(k p) n -> p k n", p=128)
                        for pj in range(2):
                            s_ = ring_load([(0, 4, 512, 512, wo[:, :, 512 * pj:512 * pj + 512])])
                            wv = wview(s_, 4, 512)
                            for mt in range(4):
                                bk = rb()
                                mm_group(PS[bk][:], bk, [(wv[:, dv, 128 * mt:128 * mt + 128], Z[:, dv, :]) for dv in range(4)], reads=[bRING[s_], bZ])
                                resid_add(bk, 4 * pj + mt, s, w, 0, 16)
                p.barrier()

        for L in layers:
            kind = L % 3
            has_ctx = L <= 1
            ctx_in = L <= 2
            plain = PLAIN_LAT + (PLAIN_CTX if has_ctx else [])
            halo = HALO_LAT + (HALO_CTX if has_ctx else [])
            if L == layers[0]:
                for _ in ada_gen(L):
                    pass
            set_layer(L)
            nxt = layers[layers.index(L) + 1] if layers.index(L) + 1 < len(layers) else None
            norm_phase(0, PLAIN_LAT + (PLAIN_CTX if ctx_in else []))
            if kind == 0:
                conv_phase(L, halo, plain)
            elif kind == 1:
                attn_phase(L)
            else:
                ret_phase(L)
            if not DBG_SKIP_FFN:
                norm_phase(1, plain)
                ffn_phase(L, halo, plain, ada_gen(nxt) if nxt is not None else iter(()))
            elif nxt is not None:
                for _ in ada_gen(nxt):
                    pass

        with ExitStack() as ph:
            SQ = [sb(f"fSQ{i}", [128, KC, 128], BF16, ph) for i in range(2)]
            RR = [sb(f"fRR{i}", [128, 128], F32, ph) for i in range(2)]
            YT = [sb(f"fYT{i}", [128, KC, 128], F32, ph) for i in range(2)]
            OS = [sb(f"fOS{i}", [128, D], F32, ph) for i in range(2)]
            bSQ, bRR, bYT, bOS = [Buf(), Buf()], [Buf(), Buf()], [Buf(), Buf()], [Buf(), Buf()]
            GF = sb("GF32", [128, KC], F32, ph)
            bGF = Buf()
            p.op("dve", lambda e: e.tensor_scalar(out=GF[:], in0=VEC[:, O_FIN:O_FIN + 8], scalar1=32.0, scalar2=None, op0=ALU.mult),
                 reads=[bVEC], writes=[bGF])
            for t in range(16):
                s = LAT0 + 128 * t
                sq, rr, yt, os_ = SQ[t % 2], RR[t % 2], YT[t % 2], OS[t % 2]
                bsq, brr, byt, bos = bSQ[t % 2], bRR[t % 2], bYT[t % 2], bOS[t % 2]
                xb = segs(bXT, s, s + 128)
                p.op("act", lambda e, sq=sq, s=s: e.activation(out=sq[:], in_=XT[:, :, s:s + 128], func=AF.Square), reads=xb, writes=[bsq])
                bk = next_bank()
                mm_group(PS[bk][:, 0:128], bk, [(ONES[:], sq[:, kc, :]) for kc in range(KC)], reads=[bsq, bONES])
                p.op("act", lambda e, rr=rr, bk=bk: e.activation(out=rr[:], in_=PS[bk][:, 0:128], func=AF.Sqrt, bias=EPSV[:, 0:1], scale=1.0),
                     reads=[bPS[bk], bEPS], writes=[brr])
                p.op("dve", lambda e, rr=rr: e.reciprocal(out=rr[:], in_=rr[:]), reads=[brr], writes=[brr])
                p.op("dve", lambda e, yt=yt, rr=rr, s=s: e.tensor_tensor(out=yt[:], in0=XT[:, :, s:s + 128],
                                                                          in1=rr[:].unsqueeze(1).broadcast_to([128, KC, 128]), op=ALU.mult),
                     reads=xb + [brr], writes=[byt])
                p.op("dve", lambda e, yt=yt: e.tensor_tensor(out=yt[:], in0=yt[:], in1=GF[:].unsqueeze(2).broadcast_to([128, KC, 128]), op=ALU.mult),
                     reads=[byt, bGF], writes=[byt])
                for half in range(2):
                    bk = next_bank()
                    for q in range(4):
                        kc = 4 * half + q
                        p.op("pe", lambda e, bk=bk, q=q, kc=kc, yt=yt: e.transpose(out=PS[bk][:, 128 * q:128 * q + 128], in_=yt[:, kc, :], identity=IDF[:]),
                             reads=[byt, bIDF], writes=[bPS[bk]], inc=(q == 3))
                    if half == 0:
                        p.op("act", lambda e, os_=os_, bk=bk: e.activation(out=os_[:, 0:512], in_=PS[bk][:], func=AF.Copy), reads=[bPS[bk]], writes=[bos])
                    else:
                        p.op("dve", lambda e, os_=os_, bk=bk: e.tensor_copy(out=os_[:, 512:1024], in_=PS[bk][:]), reads=[bPS[bk]], writes=[bos])
                p.dma("sp", lambda e, os_=os_, t=t: e.dma_start(out=out_d[128 * t:128 * t + 128, :], in_=os_[:]), reads=[bos], owner=bos)
            p.barrier()
        print(f"[build] ops={p.n_op} waits={p.n_wait} cnt={p.cnt}")
    return nc


def _cols(v):
    v = np.asarray(v, np.float32).reshape(-1, 128)
    return np.ascontiguousarray(v.T)


def _rope_tables():
    rows = np.repeat(np.arange(32, dtype=np.float32), 64)
    cols = np.tile(np.arange(64, dtype=np.float32), 32)
    q = 16
    inv = (10000.0 ** (-np.arange(q, dtype=np.float32) / q)).astype(np.float32)
    ca = np.zeros((128, SEQ), np.float32)
    sa = np.zeros((128, SEQ), np.float32)
    for pp in range(128):
        d = pp % 64
        seg, half, i = d // 32, (d % 32) // 16, d % 16
        ang = (rows if seg == 0 else cols) * inv[i]
        ca[pp] = np.cos(ang)
        sa[pp] = np.sin(ang) * (-1.0 if half == 0 else 1.0)
    q = 64
    inv = (10000.0 ** (-np.arange(q, dtype=np.float32) / q)).astype(np.float32)
    cr = np.zeros((128, 96), np.float32)
    sr = np.zeros((128, 96), np.float32)
    for pp in range(128):
        half, i = pp // 64, pp % 64
        sgn = -1.0 if half == 0 else 1.0
        a_row = np.arange(32, dtype=np.float32) * inv[i]
        a_col = np.arange(64, dtype=np.float32) * inv[i]
        cr[pp, 0:32] = np.cos(a_row)
        cr[pp, 32:96] = np.cos(a_col)
        sr[pp, 0:32] = np.sin(a_row) * sgn
        sr[pp, 32:96] = np.sin(a_col) * sgn
    return np.stack([ca, sa]), np.stack([cr, sr])


def _pack(inputs, layers):
    f = lambda a: np.ascontiguousarray(np.asarray(a, np.float32))
    vec = np.zeros((128, NV), np.float32)
    for L in range(4):
        b = L * LW
        vec[:, b + O_ADAB:b + O_ADAB + 48] = _cols(inputs["ada_b"][L])
        vec[:, b + O_GMIX:b + O_GMIX + 8] = _cols(inputs["norm_mix_g"][L])
        vec[:, b + O_GFFN:b + O_GFFN + 8] = _cols(inputs["norm_ffn_g"][L])
        for d in range(3):
            vec[:, b + O_FK + 44 * d:b + O_FK + 44 * d + 44] = _cols(inputs["ffn_conv_k"][L][d])
        vec[:, b + O_FB:b + O_FB + 44] = _cols(inputs["ffn_conv_b"][L])
        if L % 3 == 0:
            for d in range(3):
                vec[:, b + O_MK + 8 * d:b + O_MK + 8 * d + 8] = _cols(inputs["conv_k"][L // 3][d])
    vec[:, O_FIN:O_FIN + 8] = _cols(inputs["final_norm_g"])
    vec[:, O_QG] = np.tile(np.asarray(inputs["attn_q_norm_g"][0], np.float32), 2)
    vec[:, O_KG] = np.tile(np.asarray(inputs["attn_k_norm_g"][0], np.float32), 2)
    vec[:, O_DEC:O_DEC + 8] = np.broadcast_to(np.asarray(inputs["ret_decay"][0], np.float32).reshape(1, 8), (128, 8))
    ra, rr = _rope_tables()
    iota = (np.arange(512, dtype=np.float32)[None, :] - np.arange(128, dtype=np.float32)[:, None])
    wq = f(inputs["attn_w_qkv"][0])
    pidx = np.arange(128)
    bd = (pidx[:, None] // 64 == pidx[None, :] // 64).astype(np.float32)
    pa = ((pidx[:, None] ^ 16) == pidx[None, :]).astype(np.float32)
    pr = ((pidx[:, None] ^ 64) == pidx[None, :]).astype(np.float32)
    cmat = np.ascontiguousarray(np.concatenate([bd, pa, pr], axis=1))
    shared = {
        "vec": vec, "ada_w": f(inputs["ada_w"]), "conv_w_in": f(inputs["conv_w_in"]), "conv_w_out": f(inputs["conv_w_out"]),
        "attn_wqkv": wq, "attn_w_out": f(inputs["attn_w_out"][0]), "ret_w_in": f(inputs["ret_w_in"][0]),
        "ret_w_out": f(inputs["ret_w_out"][0]), "ffn_w_up": f(inputs["ffn_w_up"]), "ffn_w_down": f(inputs["ffn_w_down"]),
        "rope_a": ra, "rope_r": rr, "iota_d": np.ascontiguousarray(iota), "cmat": cmat,
    }
    maps = []
    for b in range(8):
        cv = np.stack([_cols(inputs["c"][b]), _cols(inputs["c_ctx"])], axis=-1).reshape(128, 16)
        m = dict(shared)
        m["x"] = f(inputs["x"][b])
        m["ctx"] = f(inputs["ctx"][b])
        m["cvec"] = np.ascontiguousarray(cv)
        maps.append(m)
    return maps


_NC_CACHE = {}


def kernel(_layers=(0, 1, 2, 3), _cores=8, **inputs):
    key = tuple(_layers)
    if key not in _NC_CACHE:
        _NC_CACHE[key] = build(key)
    nc = _NC_CACHE[key]
    maps = _pack(inputs, key)[:_cores]
    res = run_bass_kernel_spmd(nc, maps, core_ids=list(range(_cores)))
    out = np.stack([np.asarray(r["out"], np.float32) for r in res.results], axis=0)
    return out
```

```python
import numpy as np
from contextlib import ExitStack
import concourse.bass as bass
import concourse.mybir as mybir
from concourse.bass_utils import run_bass_kernel_spmd

F32 = mybir.dt.float32
BF16 = mybir.dt.bfloat16
ALU = mybir.AluOpType
AF = mybir.ActivationFunctionType

ENGS = ("pe", "act", "dve", "pool", "sp")

D = 1024
KC = 8
SEQ = 2048
CTXL = 256
NCOL = 2308
LAT0 = 1
CTX0 = 2051
DFF = 2816
NPAIR = 22
EPS = 1e-6
SEG = [0, 513, 1025, 1537, 2050, NCOL]
PLAIN_LAT = [(1 + 512 * b, 512, 0) for b in range(4)]
PLAIN_CTX = [(CTX0, 256, 1)]
HALO_LAT = [(410 * b, min(412, 2050 - 410 * b), 0) for b in range(5)]
HALO_CTX = [(2050, 258, 1)]

LW = 264
O_ADAB, O_GMIX, O_GFFN, O_FK, O_FB, O_MK = 0, 48, 56, 64, 196, 240
O_FIN = 4 * LW
O_QG = O_FIN + 8
O_KG = O_QG + 1
O_DEC = O_KG + 1
NV = O_DEC + 8

RET_NH = 4
RET_DBG = 0
FFN_MUL_ENG = "pool"
DBG_SKIP_FFN = False
RSLOT = 2048
NSLOT = 5


class Buf:
    __slots__ = ("name", "w", "r", "dsem", "excl")

    def __init__(self, name="", excl=False):
        self.name = name
        self.w = None
        self.r = {}
        self.dsem = None
        self.excl = excl


class Prog:
    def __init__(self, nc, eng_sems, dma_sems):
        self.nc = nc
        self.eng = {"pe": nc.tensor, "act": nc.scalar, "dve": nc.vector, "pool": nc.gpsimd, "sp": nc.sync}
        self.cnt = {e: 0 for e in ENGS}
        self.semh = dict(eng_sems)
        self.dma_free = list(dma_sems)
        self.dcnt = {}
        self.seen = {e: {} for e in ENGS}
        self.n_op = 0
        self.n_wait = 0

    def _deps(self, reads, writes, eng=None):
        deps = {}
        for b in reads:
            t = b.w
            if t is not None and deps.get(t[0], 0) < t[1]:
                deps[t[0]] = t[1]
            if b.excl:
                for k, v in b.r.items():
                    if k != eng and deps.get(k, 0) < v:
                        deps[k] = v
        for b in writes:
            t = b.w
            if t is not None and deps.get(t[0], 0) < t[1]:
                deps[t[0]] = t[1]
            for k, v in b.r.items():
                if deps.get(k, 0) < v:
                    deps[k] = v
        return deps

    def _waits(self, eng, deps):
        seen = self.seen[eng]
        for k, v in deps.items():
            if k == "pe" and eng == "pe":
                continue
            if seen.get(k, 0) < v:
                seen[k] = v
                self.eng[eng].wait_ge(self.semh[k], v)
                self.n_wait += 1

    def _mark(self, tok, reads, writes):
        k, v = tok
        for b in writes:
            b.w = tok
            b.r = {}
        for b in reads:
            if b.r.get(k, 0) < v:
                b.r[k] = v

    def op(self, eng, fn, reads=(), writes=(), inc=True):
        self._waits(eng, self._deps(reads, writes, eng))
        tok = (eng, self.cnt[eng] + 1)
        ins = fn(self.eng[eng])
        if inc:
            self.cnt[eng] += 1
            ins.then_inc(self.semh[eng], 1)
        self.n_op += 1
        self._mark(tok, reads, writes)
        return tok

    def _dsem(self, b):
        if b.dsem is None:
            h = self.dma_free.pop()
            key = ("dma", len(self.dcnt))
            self.semh[key] = h
            self.dcnt[key] = 0
            b.dsem = key
        return b.dsem

    def dma(self, queue, fn, reads=(), writes=(), owner=None):
        owner = owner or (writes[0] if writes else reads[0])
        key = self._dsem(owner)
        self._waits(queue, self._deps(reads, writes))
        self.dcnt[key] += 16
        tok = (key, self.dcnt[key])
        fn(self.eng[queue]).then_inc(self.semh[key], 16)
        self._mark(tok, reads, writes)
        return tok

    def barrier(self):
        deps = {e: self.cnt[e] for e in ("pe", "act", "dve", "pool") if self.cnt[e] > 0}
        for k, v in self.dcnt.items():
            if v > 0:
                deps[k] = v
        for e in ENGS:
            d = dict(deps)
            self._waits(e, d)


def segs(bufs, s, e, part=None):
    out = []
    for i in range(len(SEG) - 1):
        if s < SEG[i + 1] and e > SEG[i]:
            b = bufs[i]
            if isinstance(b, tuple):
                out.extend(b if part is None else (b[part],))
            else:
                out.append(b)
    return out


def build(layers=(0, 1, 2, 3)):
    nc = bass.Bass("TRN2", target_bir_lowering=False)

    def din(name, shape):
        return nc.dram_tensor(name, list(shape), F32, kind="ExternalInput").ap()

    x_d = din("x", [SEQ, D])
    ctx_d = din("ctx", [CTXL, D])
    cvec_d = din("cvec", [128, KC * 2])
    vec_d = din("vec", [128, NV])
    ada_w = din("ada_w", [4, D, 6 * D])
    conv_w_in = din("conv_w_in", [2, D, 3 * D])
    conv_w_out = din("conv_w_out", [2, D, D])
    attn_wqkv = din("attn_wqkv", [D, 1536])
    attn_w_out = din("attn_w_out", [D, D])
    ret_w_in = din("ret_w_in", [D, 6 * D])
    ret_w_out = din("ret_w_out", [2 * D, D])
    ffn_w_up = din("ffn_w_up", [4, D, 2 * DFF])
    ffn_w_down = din("ffn_w_down", [4, DFF, D])
    rope_a = din("rope_a", [2, 128, SEQ])
    rope_r = din("rope_r", [2, 128, 96])
    iota_d = din("iota_d", [128, 512])
    cmat_d = din("cmat", [128, 3 * 128])
    out_d = nc.dram_tensor("out", [SEQ, D], F32, kind="ExternalOutput").ap()

    with ExitStack() as es:
        uniq = [0]

        def sb(name, shape, dt, stack=es):
            uniq[0] += 1
            return stack.enter_context(nc.sbuf_tensor(f"{name}_{uniq[0]}", list(shape), dt))

        XT = sb("XT", [128, KC, NCOL], F32)
        HT = sb("HT", [128, KC, NCOL], BF16)
        RING = sb("RING", [128, NSLOT, RSLOT], BF16)
        VEC = sb("VEC", [128, NV], F32)
        CV = sb("CVEC", [128, KC, 2], F32)
        SC = sb("SC", [128, KC, 2], BF16)
        MODS = [sb(f"MOD{i}", [128, 48, 2], F32) for i in range(2)]
        AMS = [sb(f"AM{i}", [128, 2, KC, 2], F32) for i in range(2)]
        ONES = sb("ONES", [128, 128], BF16)
        IDF = sb("IDF", [128, 128], F32)
        EPSV = sb("EPSV", [128, 2], F32)
        CM = sb("CM", [128, 3, 128], BF16)
        PS = [es.enter_context(nc.psum_tensor(f"PS{i}", [128, 512], F32)) for i in range(8)]

        sems = {e: es.enter_context(nc.semaphore("s_" + e)) for e in ("pe", "act", "dve", "pool")}
        dsems = [es.enter_context(nc.semaphore(f"d{i}")) for i in range(60)]
        es.enter_context(nc.Block())
        p = Prog(nc, sems, dsems)

        bXT = [Buf(f"XT{i}") for i in range(5)]
        bHT = [(Buf(f"HT{i}a"), Buf(f"HT{i}b")) for i in range(5)]
        bRING = [Buf(f"R{i}") for i in range(NSLOT)]
        bPS = [Buf(f"PS{i}", excl=True) for i in range(8)]
        bVEC, bCV, bSC, bONES, bIDF, bEPS, bCM = [Buf(n) for n in "VEC CV SC ONES IDF EPS CM".split()]
        bMODS = [Buf("MOD0"), Buf("MOD1")]
        bAMS = [Buf("AM0"), Buf("AM1")]
        cur = {}

        def set_layer(L):
            cur["MOD"], cur["AM"], cur["bMOD"], cur["bAM"] = MODS[L % 2], AMS[L % 2], bMODS[L % 2], bAMS[L % 2]
        st = {"slot": 0, "bank": 0, "reserved": set(), "ring_n": NSLOT, "side": 0}

        def next_bank():
            while True:
                b = st["bank"]
                st["bank"] = (b + 1) % 8
                if b not in st["reserved"]:
                    return b

        def ring_load(parts, side=False):
            if side:
                s = NSLOT - 2 + st["side"] % 2
                st["side"] += 1
            else:
                s = st["slot"]
                st["slot"] = (s + 1) % st["ring_n"]
            for (c0, kcn, n, tot, src) in parts:
                dst = RING[:, s, 0:kcn * tot].rearrange("p (k n) -> p k n", n=tot)[:, :, c0:c0 + n]
                p.dma("pool", lambda e, dst=dst, src=src: e.dma_start(out=dst, in_=src), writes=[bRING[s]])
            return s

        def wview(s, kcn, tot):
            return RING[:, s, 0:kcn * tot].rearrange("p (k n) -> p k n", n=tot)

        def mm_group(out_ap, bank, pairs, reads):
            n = len(pairs)
            for i, (l, r) in enumerate(pairs):
                p.op("pe", lambda e, l=l, r=r, i=i: e.matmul(out_ap, lhsT=l, rhs=r, start=(i == 0), stop=(i == n - 1)),
                     reads=reads, writes=[bPS[bank]], inc=(i == n - 1))

        p.dma("sp", lambda e: e.dma_start(out=VEC[:], in_=vec_d), writes=[bVEC])
        p.dma("sp", lambda e: e.dma_start(out=CV[:].rearrange("p k t -> p (k t)"), in_=cvec_d), writes=[bCV])
        p.dma("pool", lambda e: e.dma_start(out=CM[:].rearrange("p a b -> p (a b)"), in_=cmat_d), writes=[bCM])
        p.op("dve", lambda e: e.memset(ONES[:], 1.0), writes=[bONES])
        p.op("dve", lambda e: e.memset(EPSV[:, 0:1], 1024.0 * EPS), writes=[bEPS])
        p.op("dve", lambda e: e.memset(EPSV[:, 1:2], EPS), reads=[bEPS], writes=[bEPS])
        p.op("pool", lambda e: e.memset(IDF[:], 0.0), writes=[bIDF])
        p.op("pool", lambda e: e.affine_select(out=IDF[:], in_=IDF[:], pattern=[[-1, 128]], compare_op=ALU.not_equal,
                                               fill=1.0, base=0, channel_multiplier=1), reads=[bIDF], writes=[bIDF])
        p.op("pool", lambda e: e.memset(HT[:].rearrange("p k n -> p (k n)"), 0.0), writes=[b for t in bHT for b in t])
        p.op("pool", lambda e: e.memset(XT[:].rearrange("p k n -> p (k n)"), 0.0), writes=bXT)
        p.op("act", lambda e: e.activation(out=SC[:], in_=CV[:], func=AF.Silu), reads=[bCV], writes=[bSC])

        with ExitStack() as ph:
            XS = [sb(f"XS{i}", [128, D], F32, ph) for i in range(2)]
            bXS = [Buf("XS0"), Buf("XS1")]
            for t in range(18):
                src = x_d[128 * t:128 * t + 128, :] if t < 16 else ctx_d[128 * (t - 16):128 * (t - 16) + 128, :]
                col = LAT0 + 128 * t if t < 16 else CTX0 + 128 * (t - 16)
                xs, bxs = XS[t % 2], bXS[t % 2]
                p.dma("sp", lambda e, xs=xs, src=src: e.dma_start(out=xs[:], in_=src), writes=[bxs])
                for half in range(2):
                    bk = next_bank()
                    for q in range(4):
                        kc = half * 4 + q
                        p.op("pe", lambda e, bk=bk, q=q, kc=kc, xs=xs: e.transpose(out=PS[bk][:, 128 * q:128 * q + 128],
                                                                                     in_=xs[:, 128 * kc:128 * kc + 128], identity=IDF[:]),
                             reads=[bxs, bIDF], writes=[bPS[bk]], inc=(q == 3))
                    eng = "act" if half == 0 else "dve"
                    dst = XT[:, half * 4:half * 4 + 4, col:col + 128]
                    srcp = PS[bk][:].rearrange("p (k n) -> p k n", n=128)
                    if eng == "act":
                        p.op("act", lambda e, dst=dst, srcp=srcp: e.activation(out=dst, in_=srcp, func=AF.Copy),
                             reads=[bPS[bk]], writes=segs(bXT, col, col + 128))
                    else:
                        p.op("dve", lambda e, dst=dst, srcp=srcp: e.tensor_copy(out=dst, in_=srcp),
                             reads=[bPS[bk]], writes=segs(bXT, col, col + 128))
            p.barrier()

        def ada_gen(L):
            MOD, AM, bMOD, bAM = MODS[L % 2], AMS[L % 2], bMODS[L % 2], bAMS[L % 2]
            bk = next_bank()
            st["reserved"].add(bk)
            psa = PS[bk]
            wsrc = ada_w[L].rearrange("(k p) n -> p k n", p=128)
            nxt_s = ring_load([(0, KC, 256, 256, wsrc[:, :, 0:256])], side=True)
            for pi in range(24):
                s = nxt_s
                if pi + 1 < 24:
                    nxt_s = ring_load([(0, KC, 256, 256, wsrc[:, :, 256 * (pi + 1):256 * (pi + 1) + 256])], side=True)
                wv = wview(s, KC, 256)
                for mt in range(2):
                    m = 2 * pi + mt
                    mm_group(psa[:, 2 * m:2 * m + 2], bk,
                             [(wv[:, kc, 128 * mt:128 * mt + 128], SC[:, kc, :]) for kc in range(KC)],
                             reads=[bRING[s], bSC])
                yield
            p.op("dve", lambda e: e.tensor_tensor(out=MOD[:], in0=psa[:, 0:96].rearrange("p (m t) -> p m t", t=2),
                                                  in1=VEC[:, L * LW + O_ADAB:L * LW + O_ADAB + 48].unsqueeze(2).broadcast_to([128, 48, 2]),
                                                  op=ALU.add), reads=[bPS[bk], bVEC], writes=[bMOD])
            st["reserved"].discard(bk)
            for which, (osc, og) in enumerate(((8, O_GMIX), (32, O_GFFN))):
                p.op("dve", lambda e, which=which, osc=osc: e.tensor_scalar(out=AM[:, which], in0=MOD[:, osc:osc + 8, :], scalar1=1.0, scalar2=32.0,
                                                                              op0=ALU.add, op1=ALU.mult), reads=[bMOD, bAM], writes=[bAM])
                p.op("dve", lambda e, which=which, og=og: e.tensor_tensor(out=AM[:, which], in0=AM[:, which],
                                                                            in1=VEC[:, L * LW + og:L * LW + og + 8].unsqueeze(2).broadcast_to([128, 8, 2]),
                                                                            op=ALU.mult), reads=[bAM, bVEC], writes=[bAM])

        def norm_phase(which, blocks):
            osh = 0 if which == 0 else 24
            with ExitStack() as ph:
                SQ = [sb(f"SQ{i}", [128, KC, 512], BF16, ph) for i in range(2)]
                RR = [sb(f"RR{i}", [128, 512], F32, ph) for i in range(2)]
                TM = [sb(f"TM{i}", [128, 4, 512], F32, ph) for i in range(2)]
                bSQa = [Buf(), Buf()]
                bSQb = [Buf(), Buf()]
                bRR = [Buf(), Buf()]
                bTM = [Buf(), Buf()]

                def stats(bi):
                    (s, w, ic) = blocks[bi]
                    sq, rr, brr = SQ[bi % 2], RR[bi % 2], bRR[bi % 2]
                    xb = segs(bXT, s, s + w)
                    p.op("act", lambda e: e.activation(out=sq[:, 0:4, 0:w], in_=XT[:, 0:4, s:s + w], func=AF.Square), reads=xb, writes=[bSQa[bi % 2]])
                    p.op("act", lambda e: e.activation(out=sq[:, 4:8, 0:w], in_=XT[:, 4:8, s:s + w], func=AF.Square), reads=xb, writes=[bSQb[bi % 2]])
                    bk = next_bank()
                    mm_group(PS[bk][:, 0:w], bk, [(ONES[:], sq[:, kc, 0:w]) for kc in range(KC)], reads=[bSQa[bi % 2], bSQb[bi % 2], bONES])
                    p.op("act", lambda e: e.activation(out=rr[:, 0:w], in_=PS[bk][:, 0:w], func=AF.Sqrt, bias=EPSV[:, 0:1], scale=1.0),
                         reads=[bPS[bk], bEPS], writes=[brr])

                def stats_b(bi):
                    (s, w, ic) = blocks[bi]
                    rr, brr = RR[bi % 2], bRR[bi % 2]
                    p.op("dve", lambda e: e.reciprocal(out=rr[:, 0:w], in_=rr[:, 0:w]), reads=[brr], writes=[brr])

                def affine(bi):
                    (s, w, ic) = blocks[bi]
                    rr, brr = RR[bi % 2], bRR[bi % 2]
                    xb = segs(bXT, s, s + w)
                    hbs = (segs(bHT, s, s + w, 0), segs(bHT, s, s + w, 1))
                    for half in range(2):
                        tm, btm = TM[half], bTM[half]
                        p.op("dve", lambda e: e.tensor_tensor(
                            out=tm[:, :, 0:w], in0=XT[:, 4 * half:4 * half + 4, s:s + w],
                            in1=rr[:, 0:w].unsqueeze(1).broadcast_to([128, 4, w]), op=ALU.mult),
                            reads=xb + [brr], writes=[btm])
                        for q in range(4):
                            kc = 4 * half + q
                            a_ap = cur["AM"][:, which, kc, ic:ic + 1]
                            b_ap = cur["MOD"][:, osh + kc, ic:ic + 1]
                            if q % 2 == 0 and not (half == 1 and q == 2):
                                p.op("dve", lambda e: e.tensor_scalar(
                                    out=HT[:, kc, s:s + w], in0=tm[:, q, 0:w], scalar1=a_ap, scalar2=b_ap, op0=ALU.mult, op1=ALU.add),
                                    reads=[btm, cur["bAM"], cur["bMOD"]], writes=hbs[0])
                            else:
                                p.op("act", lambda e: e.activation(
                                    out=HT[:, kc, s:s + w], in_=tm[:, q, 0:w], func=AF.Identity, scale=a_ap, bias=b_ap),
                                    reads=[btm, cur["bAM"], cur["bMOD"]], writes=hbs[1])

                stats(0)
                stats_b(0)
                for bi in range(len(blocks)):
                    if bi + 1 < len(blocks):
                        stats(bi + 1)
                    affine(bi)
                    if bi + 1 < len(blocks):
                        stats_b(bi + 1)
                p.barrier()

        def resid_add(bk, m, s, w, ic, gcol):
            g_ap = cur["MOD"][:, gcol + m, ic:ic + 1]
            xb = segs(bXT, s, s + w)
            p.op("dve", lambda e: e.scalar_tensor_tensor(out=XT[:, m, s:s + w], in0=PS[bk][:, 0:w], scalar=g_ap,
                                                         in1=XT[:, m, s:s + w], op0=ALU.mult, op1=ALU.add),
                 reads=[bPS[bk], cur["bMOD"]] + xb, writes=xb)

        def ffn_phase(L, halo, plain, side):
            vb = L * LW
            wup = ffn_w_up[L].rearrange("(k p) n -> p k n", p=128)
            groups = [list(range(0, 6)), list(range(6, 12)), list(range(12, 17)), list(range(17, 22))]
            ncols = 256
            loads = []
            for grp in groups:
                for i in grp:
                    loads.append([(0, KC, 128, 256, wup[:, :, 128 * i:128 * i + 128]),
                                  (128, KC, 128, 256, wup[:, :, DFF + 128 * i:DFF + 128 * i + 128])])
                gp, i0 = len(grp), grp[0]
                wdn = ffn_w_down[L][128 * i0:128 * (i0 + gp), :].rearrange("(k p) n -> p k n", p=128)
                for pj in range(1024 // ncols):
                    loads.append([(0, gp, ncols, ncols, wdn[:, :, ncols * pj:ncols * pj + ncols])])
            slots = {}
            PF = 2
            st["ring_n"], st["slot"] = NSLOT - 2, 0

            def get(idx):
                for k in range(idx, min(idx + PF + 1, len(loads))):
                    if k not in slots:
                        slots[k] = ring_load(loads[k])
                return slots[idx]

            with ExitStack() as ph:
                FT = sb("FT", [128, 6, NCOL], BF16, ph)
                bFT = [Buf(f"FT{i}") for i in range(5)]
                VA = [sb(f"VA{i}", [128, 412], F32, ph) for i in range(3)]
                GA = [sb(f"GA{i}", [128, 412], F32, ph) for i in range(3)]
                SG = [sb(f"SG{i}", [128, 412], F32, ph) for i in range(3)]
                bVA, bGA, bSG = [Buf() for _ in range(3)], [Buf() for _ in range(3)], [Buf() for _ in range(3)]
                it = 0
                li = 0
                pend_s = [None]

                def kcol(d, ch):
                    return VEC[:, vb + O_FK + d * 44 + ch:vb + O_FK + d * 44 + ch + 1]

                def stage_s():
                    if pend_s[0] is None:
                        return
                    (sg, ga, va, bsg, bga, bva, n, il, s) = pend_s[0]
                    pend_s[0] = None
                    p.op("act", lambda e: e.activation(out=sg[:, 0:n], in_=ga[:, 0:n], func=AF.Silu), reads=[bga], writes=[bsg])
                    p.op(FFN_MUL_ENG, lambda e: e.tensor_tensor(out=FT[:, il, s + 1:s + 1 + n], in0=va[:, 0:n], in1=sg[:, 0:n], op=ALU.mult),
                         reads=[bva, bsg], writes=segs(bFT, s + 1, s + 1 + n))

                for grp in groups:
                    gp = len(grp)
                    for il, i in enumerate(grp):
                        s_ = get(li)
                        li += 1
                        wv = wview(s_, KC, 256)
                        for (s, w, ic) in halo:
                            ba, bg = next_bank(), next_bank()
                            hb = segs(bHT, s, s + w)
                            mm_group(PS[bg][:, 0:w], bg, [(wv[:, kc, 128:256], HT[:, kc, s:s + w]) for kc in range(KC)], reads=[bRING[s_]] + hb)
                            mm_group(PS[ba][:, 0:w], ba, [(wv[:, kc, 0:128], HT[:, kc, s:s + w]) for kc in range(KC)], reads=[bRING[s_]] + hb)
                            va, ga, sg = VA[it % 3], GA[it % 3], SG[it % 3]
                            bva, bga, bsg = bVA[it % 3], bGA[it % 3], bSG[it % 3]
                            it += 1
                            n = w - 2
                            rows = ((ga, bga, bg, NPAIR + i), (va, bva, ba, i))
                            for (acc, bacc, bkk, ch) in rows:
                                p.op("act", lambda e, acc=acc, bkk=bkk, ch=ch: e.activation(
                                    out=acc[:, 0:n], in_=PS[bkk][:, 1:1 + n], func=AF.Identity, scale=kcol(1, ch),
                                    bias=VEC[:, vb + O_FB + ch:vb + O_FB + ch + 1]), reads=[bPS[bkk], bVEC], writes=[bacc])
                            stage_s()
                            for d in (0, 2):
                                for (acc, bacc, bkk, ch) in rows:
                                    p.op("dve", lambda e, acc=acc, bkk=bkk, ch=ch, d=d: e.scalar_tensor_tensor(
                                        out=acc[:, 0:n], in0=PS[bkk][:, d:d + n], scalar=kcol(d, ch), in1=acc[:, 0:n],
                                        op0=ALU.mult, op1=ALU.add), reads=[bPS[bkk], bVEC, bacc], writes=[bacc])
                            pend_s[0] = (sg, ga, va, bsg, bga, bva, n, il, s)
                        next(side, None)
                    stage_s()
                    for pj in range(1024 // ncols):
                        s_ = get(li)
                        li += 1
                        wv = wview(s_, gp, ncols)
                        for mt in range(ncols // 128):
                            m = pj * (ncols // 128) + mt
                            for (s, w, ic) in plain:
                                bk = next_bank()
                                mm_group(PS[bk][:, 0:w], bk, [(wv[:, k, 128 * mt:128 * mt + 128], FT[:, k, s:s + w]) for k in range(gp)],
                                         reads=[bRING[s_]] + segs(bFT, s, s + w))
                                resid_add(bk, m, s, w, ic, 40)
                    next(side, None)
                for _ in side:
                    pass
                p.barrier()
                st["ring_n"], st["slot"] = NSLOT, 0

        def conv_phase(L, halo, plain):
            j = L // 3
            vb = L * LW
            win = conv_w_in[j].rearrange("(k p) n -> p k n", p=128)
            wout = conv_w_out[j].rearrange("(k p) n -> p k n", p=128)
            with ExitStack() as ph:
                MT = sb("MT", [128, KC, NCOL], BF16, ph)
                bMT = [Buf(f"MT{i}") for i in range(5)]
                CS = [sb(f"CS{i}", [128, 412], F32, ph) for i in range(2)]
                CVV = [sb(f"CVV{i}", [128, 412], F32, ph) for i in range(2)]
                TT = [sb(f"TT{i}", [128, 412], F32, ph) for i in range(2)]
                bCS, bCVV, bTT = [Buf(), Buf()], [Buf(), Buf()], [Buf(), Buf()]
                it = 0
                for jc in range(KC):
                    s1 = ring_load([(0, KC, 128, 256, win[:, :, D + 128 * jc:D + 128 * jc + 128]),
                                    (128, KC, 128, 256, win[:, :, 2 * D + 128 * jc:2 * D + 128 * jc + 128])])
                    s2 = ring_load([(0, KC, 128, 128, win[:, :, 128 * jc:128 * jc + 128])])
                    w1 = wview(s1, KC, 256)
                    w2 = wview(s2, KC, 128)
                    for (s, w, ic) in halo:
                        bc, bv, bb = next_bank(), next_bank(), next_bank()
                        hb = segs(bHT, s, s + w)
                        mm_group(PS[bc][:, 0:w], bc, [(w1[:, kc, 0:128], HT[:, kc, s:s + w]) for kc in range(KC)], reads=[bRING[s1]] + hb)
                        mm_group(PS[bv][:, 0:w], bv, [(w1[:, kc, 128:256], HT[:, kc, s:s + w]) for kc in range(KC)], reads=[bRING[s1]] + hb)
                        mm_group(PS[bb][:, 0:w], bb, [(w2[:, kc, 0:128], HT[:, kc, s:s + w]) for kc in range(KC)], reads=[bRING[s2]] + hb)
                        cs, cvv, tt = CS[it % 2], CVV[it % 2], TT[it % 2]
                        bcs, bcvv, btt = bCS[it % 2], bCVV[it % 2], bTT[it % 2]
                        it += 1
                        n = w - 2

                        def kcol(d):
                            return VEC[:, vb + O_MK + d * 8 + jc:vb + O_MK + d * 8 + jc + 1]
                        p.op("act", lambda e, cs=cs, bc=bc, w=w: e.activation(out=cs[:, 0:w], in_=PS[bc][:, 0:w], func=AF.Copy),
                             reads=[bPS[bc]], writes=[bcs])
                        p.op("dve", lambda e, cvv=cvv, cs=cs, bv=bv, w=w: e.tensor_tensor(out=cvv[:, 0:w], in0=cs[:, 0:w], in1=PS[bv][:, 0:w], op=ALU.mult),
                             reads=[bcs, bPS[bv]], writes=[bcvv])
                        p.op("act", lambda e, tt=tt, cvv=cvv, n=n: e.activation(out=tt[:, 0:n], in_=cvv[:, 1:1 + n], func=AF.Copy, scale=kcol(1)),
                             reads=[bcvv, bVEC], writes=[btt])
                        for d in (0, 2):
                            p.op("dve", lambda e, tt=tt, cvv=cvv, n=n, d=d: e.scalar_tensor_tensor(
                                out=tt[:, 0:n], in0=cvv[:, d:d + n], scalar=kcol(d), in1=tt[:, 0:n], op0=ALU.mult, op1=ALU.add),
                                reads=[bcvv, bVEC, btt], writes=[btt])
                        p.op("dve", lambda e, tt=tt, bb=bb, n=n, s=s: e.tensor_tensor(
                            out=MT[:, jc, s + 1:s + 1 + n], in0=tt[:, 0:n], in1=PS[bb][:, 1:1 + n], op=ALU.mult),
                            reads=[btt, bPS[bb]], writes=segs(bMT, s + 1, s + 1 + n))
                for pj in range(4):
                    s_ = ring_load([(0, KC, 256, 256, wout[:, :, 256 * pj:256 * pj + 256])])
                    wv = wview(s_, KC, 256)
                    for mt in range(2):
                        m = 2 * pj + mt
                        for (s, w, ic) in plain:
                            bk = next_bank()
                            mm_group(PS[bk][:, 0:w], bk, [(wv[:, kc, 128 * mt:128 * mt + 128], MT[:, kc, s:s + w]) for kc in range(KC)],
                                     reads=[bRING[s_]] + segs(bMT, s, s + w))
                            resid_add(bk, m, s, w, ic, 16)
                p.barrier()


        def attn_phase(L):
            wq_all = attn_wqkv.rearrange("(k p) n -> p k n", p=128)
            plain_all = PLAIN_LAT + PLAIN_CTX
            with ExitStack() as ph:
                KT = sb("KT", [128, 2, NCOL], BF16, ph)
                VA = sb("VA", [128, 18, 4, 128], BF16, ph)
                QT = [sb(f"QT{i}", [128, NCOL], BF16, ph) for i in range(2)]
                OT = [sb(f"OT{i}", [128, NCOL], BF16, ph) for i in range(2)]
                TAB = [sb(f"TAB{i}", [128, 2, 512], F32, ph) for i in range(2)]
                NPT = 6
                PT = [sb(f"PT{i}", [128, 512], BF16, ph) for i in range(NPT)]
                SQ = sb("aSQ", [128, 512], BF16, ph)
                QG = sb("aQG", [128, 512], BF16, ph)
                RR = sb("aRR", [128, 512], F32, ph)
                T1 = sb("aT1", [128, 512], F32, ph)
                T2 = sb("aT2", [128, 512], F32, ph)
                RC = [sb(f"aRC{i}", [128, 512], F32, ph) for i in range(2)]
                bKT = [Buf(), Buf()]
                bVA = Buf()
                bOT = [Buf(), Buf()]
                bQT = [Buf(), Buf()]
                bTAB = [Buf(), Buf()]
                bPT = [Buf() for _ in range(NPT)]
                bSQ, bQG, bRR, bT1, bT2 = Buf(), Buf(), Buf(), Buf(), Buf()
                bRC = [Buf(), Buf()]
                cnt = {"tab": 0, "pt": 0, "rc": 0}

                p.op("pool", lambda e: e.memset(VA[:].rearrange("p a b c -> p (a b c)"), 1.0), writes=[bVA])

                def reserve(b):
                    st["reserved"].add(b)

                def release(b):
                    st["reserved"].discard(b)

                def qk_gen(wv, s_, dst_of, bdst, gcol):
                    g_ap = VEC[:, gcol:gcol + 1]
                    for (s, w, ic) in plain_all:
                        rope = (ic == 0)
                        dst_ap = dst_of(s, w)
                        bk = next_bank()
                        reserve(bk)
                        mm_group(PS[bk][:, 0:w], bk, [(wv[:, kc, :], HT[:, kc, s:s + w]) for kc in range(KC)], reads=[bRING[s_]] + segs(bHT, s, s + w))
                        if rope:
                            ti = cnt["tab"] % 2
                            cnt["tab"] += 1
                            tab, btab = TAB[ti], bTAB[ti]
                            t0 = s - LAT0
                            p.dma("sp", lambda e: e.dma_start(out=tab[:, 0, :], in_=rope_a[0][:, t0:t0 + 512]), writes=[btab])
                            p.dma("sp", lambda e: e.dma_start(out=tab[:, 1, :], in_=rope_a[1][:, t0:t0 + 512]), writes=[btab])
                        yield
                        p.op("act", lambda e: e.activation(out=SQ[:, 0:w], in_=PS[bk][:, 0:w], func=AF.Square), reads=[bPS[bk]], writes=[bSQ])
                        if rope:
                            p.op("act", lambda e: e.activation(out=QG[:, 0:w], in_=PS[bk][:, 0:w], func=AF.Copy, scale=g_ap), reads=[bPS[bk], bVEC], writes=[bQG])
                        yield
                        b2 = next_bank()
                        reserve(b2)
                        mm_group(PS[b2][:, 0:w], b2, [(CM[:, 0, :], SQ[:, 0:w])], reads=[bSQ, bCM])
                        yield
                        p.op("act", lambda e: e.activation(out=RR[:, 0:w], in_=PS[b2][:, 0:w], func=AF.Ln, bias=EPSV[:, 1:2], scale=1.0 / 64),
                             reads=[bPS[b2], bEPS], writes=[bRR])
                        if rope:
                            mm_group(PS[b2][:, 0:w], b2, [(CM[:, 1, :], QG[:, 0:w])], reads=[bQG, bCM])
                        yield
                        p.op("act", lambda e: e.activation(out=RR[:, 0:w], in_=RR[:, 0:w], func=AF.Exp, scale=-0.5), reads=[bRR], writes=[bRR])
                        if not rope:
                            p.op("dve", lambda e: e.scalar_tensor_tensor(out=dst_ap, in0=PS[bk][:, 0:w], scalar=g_ap, in1=RR[:, 0:w],
                                                                         op0=ALU.mult, op1=ALU.mult), reads=[bPS[bk], bVEC, bRR], writes=[bdst])
                        else:
                            p.op("dve", lambda e: e.scalar_tensor_tensor(out=T1[:, 0:w], in0=PS[bk][:, 0:w], scalar=g_ap, in1=tab[:, 0, 0:w],
                                                                         op0=ALU.mult, op1=ALU.mult), reads=[bPS[bk], bVEC, btab], writes=[bT1])
                            p.op("dve", lambda e: e.tensor_tensor(out=T2[:, 0:w], in0=PS[b2][:, 0:w], in1=tab[:, 1, 0:w], op=ALU.mult),
                                 reads=[bPS[b2], btab], writes=[bT2])
                            yield
                            p.op("dve", lambda e: e.tensor_tensor(out=T1[:, 0:w], in0=T1[:, 0:w], in1=T2[:, 0:w], op=ALU.add), reads=[bT1, bT2], writes=[bT1])
                            p.op("dve", lambda e: e.tensor_tensor(out=dst_ap, in0=T1[:, 0:w], in1=RR[:, 0:w], op=ALU.mult), reads=[bT1, bRR], writes=[bdst])
                        release(bk)
                        release(b2)
                        yield

                for g2 in range(2):
                    s_ = ring_load([(0, KC, 128, 128, wq_all[:, :, 1024 + 128 * g2:1024 + 128 * g2 + 128])])
                    for _ in qk_gen(wview(s_, KC, 128), s_, lambda s, w, g2=g2: KT[:, g2, s:s + w], bKT[g2], O_KG):
                        pass
                s_ = ring_load([(0, KC, 256, 256, wq_all[:, :, 1280:1536])])
                wv = wview(s_, KC, 256)
                for kt in range(18):
                    col = LAT0 + 128 * kt if kt < 16 else CTX0 + 128 * (kt - 16)
                    bk = next_bank()
                    mm_group(PS[bk][:, 0:256], bk, [(HT[:, kc, col:col + 128], wv[:, kc, :]) for kc in range(KC)],
                             reads=[bRING[s_]] + segs(bHT, col, col + 128))
                    p.op("act", lambda e, kt=kt, bk=bk: e.activation(out=VA[:, kt, :, 0:64], in_=PS[bk][:, 0:256].rearrange("p (h d) -> p h d", d=64), func=AF.Copy),
                         reads=[bPS[bk]], writes=[bVA])

                def heads_of(j):
                    g2, r = j // 4, j % 4
                    return g2, 8 * g2 + r, 8 * g2 + 4 + r

                def q_gen(j):
                    g2, ha, hb = heads_of(j)
                    s_ = ring_load([(0, KC, 64, 128, wq_all[:, :, 64 * ha:64 * ha + 64]),
                                    (64, KC, 64, 128, wq_all[:, :, 64 * hb:64 * hb + 64])])
                    yield from qk_gen(wview(s_, KC, 128), s_, lambda s, w: QT[j % 2][:, s:s + w], bQT[j % 2], O_QG)

                def outproj_gen(j):
                    g2, ha, hb = heads_of(j)
                    ot, bot = OT[j % 2], bOT[j % 2]
                    s_ = st["slot"]
                    st["slot"] = (s_ + 1) % NSLOT
                    for hh, hd in enumerate((ha, hb)):
                        p.dma("pool", lambda e, hh=hh, hd=hd: e.dma_start(out=RING[64 * hh:64 * hh + 64, s_, 0:1024], in_=attn_w_out[64 * hd:64 * hd + 64, :]),
                              writes=[bRING[s_]])
                    yield
                    for m in range(8):
                        for (s, w, ic) in plain_all:
                            bk = next_bank()
                            mm_group(PS[bk][:, 0:w], bk, [(RING[:, s_, 128 * m:128 * m + 128], ot[:, s:s + w])], reads=[bRING[s_], bot])
                            resid_add(bk, m, s, w, ic, 16)
                            yield

                def attend(j, sides):
                    g2, ha, hb = heads_of(j)
                    qt, bqt = QT[j % 2], bQT[j % 2]
                    ot, bot = OT[j % 2], bOT[j % 2]
                    rr_i = [0]

                    def pull():
                        for _ in range(len(sides)):
                            g = sides[rr_i[0] % len(sides)]
                            rr_i[0] += 1
                            try:
                                next(g)
                                return
                            except StopIteration:
                                continue

                    def pv(item, nk):
                        (hh, pi, kt_, idx_, oa, w) = item
                        p.op("pe", lambda e: e.matmul(PS[oa[hh]][:, 0:w], lhsT=VA[:, kt_, 2 * g2 + hh, :], rhs=PT[pi][:, 0:w],
                                                      start=(idx_ == 0), stop=(idx_ == nk - 1)),
                             reads=[bVA, bPT[pi]], writes=[bPS[oa[hh]]], inc=True)

                    for (s, w, ic) in plain_all:
                        kts = list(range(18)) if ic == 0 else [16, 17]
                        oa = [next_bank(), next_bank()]
                        reserve(oa[0])
                        reserve(oa[1])
                        pend = None
                        for idx, kt in enumerate(kts):
                            kcol = LAT0 + 128 * kt if kt < 16 else CTX0 + 128 * (kt - 16)
                            cur_ = []
                            for hh in range(2):
                                bs = next_bank()
                                lo = 64 * hh
                                mm_group(PS[bs][:, 0:w], bs, [(KT[lo:lo + 64, g2, kcol:kcol + 128], qt[lo:lo + 64, s:s + w])], reads=[bKT[g2], bqt])
                                pi = cnt["pt"] % NPT
                                cnt["pt"] += 1
                                p.op("act", lambda e, pi=pi, bs=bs: e.activation(out=PT[pi][:, 0:w], in_=PS[bs][:, 0:w], func=AF.Exp, scale=0.125),
                                     reads=[bPS[bs]], writes=[bPT[pi]])
                                cur_.append((hh, pi, kt, idx, oa, w))
                            if pend is not None:
                                for item in pend:
                                    pv(item, len(kts))
                            pend = cur_
                            pull()
                        for item in pend:
                            pv(item, len(kts))
                        for hh in range(2):
                            ri = cnt["rc"] % 2
                            cnt["rc"] += 1
                            rc, brc = RC[ri], bRC[ri]
                            p.op("dve", lambda e, rc=rc, hh=hh: e.reciprocal(out=rc[0:64, 0:w], in_=PS[oa[hh]][64:128, 0:w]), reads=[bPS[oa[hh]]], writes=[brc])
                            p.op("dve", lambda e, rc=rc, hh=hh: e.tensor_tensor(out=ot[64 * hh:64 * hh + 64, s:s + w], in0=PS[oa[hh]][0:64, 0:w],
                                                                                in1=rc[0:64, 0:w], op=ALU.mult), reads=[bPS[oa[hh]], brc], writes=[bot])
                        release(oa[0])
                        release(oa[1])
                    for g in sides:
                        for _ in g:
                            pass

                for _ in q_gen(0):
                    pass
                prev_out = None
                for j in range(8):
                    sides = []
                    if j + 1 < 8:
                        sides.append(q_gen(j + 1))
                    if prev_out is not None:
                        sides.append(prev_out)
                    attend(j, sides)
                    prev_out = outproj_gen(j)
                for _ in prev_out:
                    pass
                p.barrier()

        def ret_phase(L):
            import math
            LN16 = math.log(16.0)
            win = ret_w_in.rearrange("(k p) n -> p k n", p=128)
            with ExitStack() as ph:
                QTh = sb("QTh", [128, 2, SEQ], BF16, ph)
                KTh = sb("KTh", [128, 2, NCOL], BF16, ph)
                VH = sb("VH", [128, 18, 512], BF16, ph)
                GS = [sb(f"GS{i}", [128, 4, 512], BF16, ph) for i in range(2)]
                SQY = sb("SQY", [128, 4, 512], BF16, ph)
                Z = SQY
                TMP = sb("rTMP", [128, 512], F32, ph)
                NPTR, NMK = 3, 4
                PT = [sb(f"rPT{i}", [128, 512], BF16, ph) for i in range(NPTR)]
                MK = [sb(f"rMK{i}", [128, 512], F32, ph) for i in range(NMK)]
                QB = sb("rQB", [128, 512], BF16, ph)
                IOTA = sb("IOTA", [128, 512], F32, ph)
                RTAB = sb("RTAB", [128, 2, 96], F32, ph)
                RRr = sb("rRR", [128, 512], F32, ph)
                LG = sb("LG", [128, 8], F32, ph)
                NLG = sb("NLG", [128, 8], F32, ph)
                LT = sb("LT", [128, 8], F32, ph)
                BIAS = sb("BIAS", [128, 8], F32, ph)
                bQTh, bKTh, bVH, bSQY, bTMP, bQB, bIOTA, bRTAB, bRRr, bLG, bLT = [Buf() for _ in range(11)]
                bZ = bSQY
                bGS = [Buf(), Buf()]
                bPT = [Buf() for _ in range(NPTR)]
                bMK = [Buf() for _ in range(NMK)]
                bBIAS = [Buf() for _ in range(8)]
                cnt = {"pt": 0, "mk": 0, "bias": 0, "rb": 0}

                def rb():
                    b = 4 + cnt["rb"] % 4
                    cnt["rb"] += 1
                    return b

                def nmk():
                    i = cnt["mk"] % NMK
                    cnt["mk"] += 1
                    return i

                def nbias():
                    i = cnt["bias"] % 8
                    cnt["bias"] += 1
                    return i

                p.dma("sp", lambda e: e.dma_start(out=IOTA[:], in_=iota_d), writes=[bIOTA])
                p.dma("sp", lambda e: e.dma_start(out=RTAB[:, 0, :], in_=rope_r[0]), writes=[bRTAB])
                p.dma("sp", lambda e: e.dma_start(out=RTAB[:, 1, :], in_=rope_r[1]), writes=[bRTAB])
                p.op("act", lambda e: e.activation(out=LT[:], in_=VEC[:, O_DEC:O_DEC + 8], func=AF.Exp, scale=-math.log(2.0)), reads=[bVEC], writes=[bLT])
                p.op("dve", lambda e: e.tensor_scalar(out=LG[:], in0=LT[:], scalar1=1.0 / 6, scalar2=0.2, op0=ALU.mult, op1=ALU.add), reads=[bLT], writes=[bLG])
                for cst in (0.25, 1.0 / 3, 0.5, 1.0):
                    p.op("dve", lambda e: e.tensor_tensor(out=LG[:], in0=LG[:], in1=LT[:], op=ALU.mult), reads=[bLG, bLT], writes=[bLG])
                    p.op("dve", lambda e, cst=cst: e.tensor_scalar(out=LG[:], in0=LG[:], scalar1=cst, scalar2=None, op0=ALU.add), reads=[bLG], writes=[bLG])
                p.op("dve", lambda e: e.tensor_tensor(out=NLG[:], in0=LG[:], in1=LT[:], op=ALU.mult), reads=[bLG, bLT], writes=[bLG])
                p.op("dve", lambda e: e.tensor_scalar(out=LG[:], in0=NLG[:], scalar1=-1.0, scalar2=None, op0=ALU.mult), reads=[bLG], writes=[bLG])

                def rope_post(bk, dst3, bdst, sg, blk):
                    if sg == 0:
                        c_ap = RTAB[:, 0, 8 * blk:8 * blk + 8].unsqueeze(2).broadcast_to([128, 8, 64])
                        s_ap = RTAB[:, 1, 8 * blk:8 * blk + 8].unsqueeze(2).broadcast_to([128, 8, 64])
                    else:
                        c_ap = RTAB[:, 0, 32:96].unsqueeze(1).broadcast_to([128, 8, 64])
                        s_ap = RTAB[:, 1, 32:96].unsqueeze(1).broadcast_to([128, 8, 64])
                    if RET_DBG == 21:
                        c_ap = IOTA[:].rearrange("p (r c) -> p r c", c=64)
                        s_ap = IOTA[:].rearrange("p (r c) -> p r c", c=64)
                    if RET_DBG == 22 and sg == 1:
                        c_ap = RTAB[:, 0, 0:8].unsqueeze(2).broadcast_to([128, 8, 64])
                        s_ap = RTAB[:, 1, 0:8].unsqueeze(2).broadcast_to([128, 8, 64])
                    p.op("act", lambda e: e.activation(out=QB[:], in_=PS[bk][:], func=AF.Copy), reads=[bPS[bk]], writes=[bQB])
                    b3 = next_bank()
                    mm_group(PS[b3][:], b3, [(CM[:, 2, :], QB[:])], reads=[bQB, bCM])
                    i1, i2 = nmk(), nmk()
                    v3 = lambda ap: ap.rearrange("p (r c) -> p r c", c=64)
                    p.op("dve", lambda e: e.tensor_tensor(out=v3(MK[i1][:]), in0=v3(PS[bk][:]), in1=c_ap, op=ALU.mult), reads=[bPS[bk], bRTAB], writes=[bMK[i1]])
                    p.op("dve", lambda e: e.tensor_tensor(out=v3(MK[i2][:]), in0=v3(PS[b3][:]), in1=s_ap, op=ALU.mult), reads=[bPS[b3], bRTAB], writes=[bMK[i2]])
                    p.op("dve", lambda e: e.tensor_tensor(out=dst3, in0=MK[i1][:], in1=MK[i2][:], op=ALU.add), reads=[bMK[i1], bMK[i2]], writes=[bdst])

                for h in range(RET_NH):
                    if RET_DBG == 1:
                        break
                    lgf, nlgf = LG[:, h:h + 1], NLG[:, h:h + 1]
                    lgb, nlgb = LG[:, 4 + h:5 + h], NLG[:, 4 + h:5 + h]
                    for which, (dstT, bdst, c0) in enumerate(((QTh, bQTh, 256 * h), (KTh, bKTh, 1024 + 256 * h))):
                        s_ = ring_load([(0, KC, 256, 256, win[:, :, c0:c0 + 256])])
                        wv = wview(s_, KC, 256)
                        for sg in range(2):
                            for blk, (s, w, ic) in enumerate(PLAIN_LAT):
                                bk = next_bank()
                                mm_group(PS[bk][:], bk, [(wv[:, kc, 128 * sg:128 * sg + 128], HT[:, kc, s:s + w]) for kc in range(KC)],
                                         reads=[bRING[s_]] + segs(bHT, s, s + w))
                                cbase = (s - LAT0) if which == 0 else s
                                rope_post(bk, dstT[:, sg, cbase:cbase + 512], bdst, sg, blk)
                            if which == 1:
                                (s, w, ic) = PLAIN_CTX[0]
                                bk = next_bank()
                                mm_group(PS[bk][:, 0:w], bk, [(wv[:, kc, 128 * sg:128 * sg + 128], HT[:, kc, s:s + w]) for kc in range(KC)],
                                         reads=[bRING[s_]] + segs(bHT, s, s + w))
                                p.op("act", lambda e, bk=bk, sg=sg, s=s, w=w: e.activation(out=KTh[:, sg, s:s + w], in_=PS[bk][:, 0:w], func=AF.Copy),
                                     reads=[bPS[bk]], writes=[bKTh])
                    if RET_DBG in (2, 21, 22):
                        break
                    sv = [ring_load([(0, KC, 256, 256, win[:, :, 2048 + 512 * h + 256 * i:2048 + 512 * h + 256 * i + 256])]) for i in range(2)]
                    for kt in range(18):
                        col = LAT0 + 128 * kt if kt < 16 else CTX0 + 128 * (kt - 16)
                        bk = next_bank()
                        for i in range(2):
                            wv = wview(sv[i], KC, 256)
                            mm_group(PS[bk][:, 256 * i:256 * i + 256], bk, [(HT[:, kc, col:col + 128], wv[:, kc, :]) for kc in range(KC)],
                                     reads=[bRING[sv[i]]] + segs(bHT, col, col + 128))
                        if kt % 2 == 0:
                            p.op("act", lambda e, kt=kt, bk=bk: e.activation(out=VH[:, kt, :], in_=PS[bk][:], func=AF.Copy), reads=[bPS[bk]], writes=[bVH])
                        else:
                            p.op("dve", lambda e, kt=kt, bk=bk: e.tensor_copy(out=VH[:, kt, :], in_=PS[bk][:]), reads=[bPS[bk]], writes=[bVH])

                    if RET_DBG == 3:
                        break
                    def gen_mask(qb, kt):
                        if kt < 16:
                            off = 512 * qb - 128 * kt
                            if off >= 128 or off <= -512:
                                sc = lgf if off >= 128 else nlgb
                                bi = nbias()
                                p.op("dve", lambda e: e.tensor_scalar(out=BIAS[:, bi:bi + 1], in0=sc, scalar1=float(off), scalar2=-LN16,
                                                                      op0=ALU.mult, op1=ALU.add), reads=[bLG], writes=[bBIAS[bi]])
                                mi = nmk()
                                p.op("act", lambda e: e.activation(out=MK[mi][:], in_=IOTA[:], func=AF.Exp, scale=sc, bias=BIAS[:, bi:bi + 1]),
                                     reads=[bIOTA, bLG, bBIAS[bi]], writes=[bMK[mi]])
                                return [mi]
                            m1, m2 = nmk(), nmk()
                            bi = nbias()
                            p.op("dve", lambda e: e.memset(BIAS[:, bi:bi + 1], -LN16), writes=[bBIAS[bi]])
                            p.op("dve", lambda e: e.tensor_scalar(out=MK[m1][:], in0=IOTA[:], scalar1=float(off), scalar2=0.0, op0=ALU.add, op1=ALU.max),
                                 reads=[bIOTA], writes=[bMK[m1]])
                            p.op("dve", lambda e: e.scalar_tensor_tensor(out=MK[m2][:], in0=IOTA[:], scalar=float(off), in1=MK[m1][:],
                                                                         op0=ALU.add, op1=ALU.subtract), reads=[bIOTA, bMK[m1]], writes=[bMK[m2]])
                            p.op("act", lambda e: e.activation(out=MK[m1][:], in_=MK[m1][:], func=AF.Exp, scale=lgf, bias=BIAS[:, bi:bi + 1]),
                                 reads=[bMK[m1], bLG, bBIAS[bi]], writes=[bMK[m1]])
                            p.op("act", lambda e: e.activation(out=MK[m2][:], in_=MK[m2][:], func=AF.Exp, scale=nlgb), reads=[bMK[m2], bLG], writes=[bMK[m2]])
                            return [m1, m2]
                        a_ = kt - 16
                        m1, m2 = nmk(), nmk()
                        b1, b2 = nbias(), nbias()
                        o1 = float(512 * qb + 256 - 128 * a_)
                        o2 = float(2048 - 512 * qb + 128 * a_)
                        p.op("dve", lambda e: e.tensor_scalar(out=BIAS[:, b1:b1 + 1], in0=lgf, scalar1=o1, scalar2=-LN16, op0=ALU.mult, op1=ALU.add),
                             reads=[bLG], writes=[bBIAS[b1]])
                        p.op("dve", lambda e: e.tensor_scalar(out=BIAS[:, b2:b2 + 1], in0=lgb, scalar1=o2, scalar2=-LN16, op0=ALU.mult, op1=ALU.add),
                             reads=[bLG], writes=[bBIAS[b2]])
                        p.op("act", lambda e: e.activation(out=MK[m1][:], in_=IOTA[:], func=AF.Exp, scale=lgf, bias=BIAS[:, b1:b1 + 1]),
                             reads=[bIOTA, bLG, bBIAS[b1]], writes=[bMK[m1]])
                        p.op("act", lambda e: e.activation(out=MK[m2][:], in_=IOTA[:], func=AF.Exp, scale=nlgb, bias=BIAS[:, b2:b2 + 1]),
                             reads=[bIOTA, bLG, bBIAS[b2]], writes=[bMK[m2]])
                        p.op("dve", lambda e: e.tensor_tensor(out=MK[m1][:], in0=MK[m1][:], in1=MK[m2][:], op=ALU.add),
                             reads=[bMK[m1], bMK[m2]], writes=[bMK[m1]])
                        return [m1]

                    def st_mm(qb, kt):
                        q0 = 512 * qb
                        kcol = LAT0 + 128 * kt if kt < 16 else CTX0 + 128 * (kt - 16)
                        bs = rb()
                        mm_group(PS[bs][:], bs, [(KTh[:, sg, kcol:kcol + 128], QTh[:, sg, q0:q0 + 512]) for sg in range(2)], reads=[bKTh, bQTh])
                        return bs

                    def apply_mask(bs, mks):
                        pi = cnt["pt"] % NPTR
                        cnt["pt"] += 1
                        if len(mks) == 1:
                            p.op("dve", lambda e: e.tensor_tensor(out=PT[pi][:], in0=PS[bs][:], in1=MK[mks[0]][:], op=ALU.mult),
                                 reads=[bPS[bs], bMK[mks[0]]], writes=[bPT[pi]])
                        else:
                            p.op("dve", lambda e: e.tensor_tensor(out=TMP[:], in0=PS[bs][:], in1=MK[mks[0]][:], op=ALU.mult),
                                 reads=[bPS[bs], bMK[mks[0]]], writes=[bTMP])
                            p.op("dve", lambda e: e.tensor_tensor(out=PT[pi][:], in0=TMP[:], in1=MK[mks[1]][:], op=ALU.mult),
                                 reads=[bTMP, bMK[mks[1]]], writes=[bPT[pi]])
                        return pi

                    def pv_mm(kt, pi):
                        for dv in range(4):
                            p.op("pe", lambda e, dv=dv: e.matmul(PS[dv][:], lhsT=VH[:, kt, 128 * dv:128 * dv + 128], rhs=PT[pi][:],
                                                                 start=(kt == 0), stop=(kt == 17)),
                                 reads=[bVH, bPT[pi]], writes=[bPS[dv]], inc=(dv == 3))

                    def g_proj(qb):
                        (s, w, ic) = PLAIN_LAT[qb]
                        sg_ = [ring_load([(0, KC, 256, 256, win[:, :, 4096 + 512 * h + 256 * i:4096 + 512 * h + 256 * i + 256])]) for i in range(2)]
                        for m in range(4):
                            wv = wview(sg_[m // 2], KC, 256)
                            bk = rb()
                            mm_group(PS[bk][:], bk, [(wv[:, kc, 128 * (m % 2):128 * (m % 2) + 128], HT[:, kc, s:s + w]) for kc in range(KC)],
                                     reads=[bRING[sg_[m // 2]]] + segs(bHT, s, s + w))
                            p.op("act", lambda e, m=m, bk=bk: e.activation(out=GS[qb % 2][:, m, :], in_=PS[bk][:], func=AF.Silu), reads=[bPS[bk]], writes=[bGS[qb % 2]])

                    g_proj(0)
                    for qb, (s, w, ic) in enumerate(PLAIN_LAT):
                        LOOK = 1
                        masks = {}
                        for k in range(min(LOOK, 18)):
                            masks[k] = gen_mask(qb, k)
                        stb = {0: st_mm(qb, 0)}
                        pts = {}
                        for i in range(18):
                            if i + LOOK < 18:
                                masks[i + LOOK] = gen_mask(qb, i + LOOK)
                            if i + 1 < 18:
                                stb[i + 1] = st_mm(qb, i + 1)
                            pts[i] = apply_mask(stb[i], masks[i])
                            if i >= 1:
                                pv_mm(i - 1, pts[i - 1])
                        pv_mm(17, pts[17])
                        for dv in range(4):
                            p.op("act", lambda e, dv=dv: e.activation(out=SQY[:, dv, :], in_=PS[dv][:], func=AF.Square), reads=[bPS[dv]], writes=[bSQY])
                        bss = rb()
                        mm_group(PS[bss][:], bss, [(ONES[:], SQY[:, dv, :]) for dv in range(4)], reads=[bSQY, bONES])
                        p.op("act", lambda e, bss=bss: e.activation(out=RRr[:], in_=PS[bss][:], func=AF.Ln, bias=EPSV[:, 1:2], scale=1.0 / 512),
                             reads=[bPS[bss], bEPS], writes=[bRRr])
                        if qb + 1 < 4:
                            g_proj(qb + 1)
                        p.op("act", lambda e: e.activation(out=RRr[:], in_=RRr[:], func=AF.Exp, scale=-0.5), reads=[bRRr], writes=[bRRr])
                        for dv in range(4):
                            p.op("dve", lambda e, dv=dv: e.tensor_tensor(out=TMP[:], in0=PS[dv][:], in1=RRr[:], op=ALU.mult), reads=[bPS[dv], bRRr], writes=[bTMP])
                            p.op("dve", lambda e, dv=dv: e.tensor_tensor(out=Z[:, dv, :], in0=TMP[:], in1=GS[qb % 2][:, dv, :], op=ALU.mult), reads=[bTMP, bGS[qb % 2]], writes=[bZ])
                        wo = ret_w_out[512 * h:512 * h + 512, :].rearrange("(k p) n -> p k n", p=128)
                        for pj in range(2):
                            s_ = ring_load([(0, 4, 512, 512, wo[:, :, 512 * pj:512 * pj + 512])])
                            wv = wview(s_, 4, 512)
                            for mt in range(4):
                                bk = rb()
                                mm_group(PS[bk][:], bk, [(wv[:, dv, 128 * mt:128 * mt + 128], Z[:, dv, :]) for dv in range(4)], reads=[bRING[s_], bZ])
                                resid_add(bk, 4 * pj + mt, s, w, 0, 16)
                p.barrier()

        for L in layers:
            kind = L % 3
            has_ctx = L <= 1
            ctx_in = L <= 2
            plain = PLAIN_LAT + (PLAIN_CTX if has_ctx else [])
            halo = HALO_LAT + (HALO_CTX if has_ctx else [])
            if L == layers[0]:
                for _ in ada_gen(L):
                    pass
            set_layer(L)
            nxt = layers[layers.index(L) + 1] if layers.index(L) + 1 < len(layers) else None
            norm_phase(0, PLAIN_LAT + (PLAIN_CTX if ctx_in else []))
            if kind == 0:
                conv_phase(L, halo, plain)
            elif kind == 1:
                attn_phase(L)
            else:
                ret_phase(L)
            if not DBG_SKIP_FFN:
                norm_phase(1, plain)
                ffn_phase(L, halo, plain, ada_gen(nxt) if nxt is not None else iter(()))
            elif nxt is not None:
                for _ in ada_gen(nxt):
                    pass

        with ExitStack() as ph:
            SQ = [sb(f"fSQ{i}", [128, KC, 128], BF16, ph) for i in range(2)]
            RR = [sb(f"fRR{i}", [128, 128], F32, ph) for i in range(2)]
            YT = [sb(f"fYT{i}", [128, KC, 128], F32, ph) for i in range(2)]
            OS = [sb(f"fOS{i}", [128, D], F32, ph) for i in range(2)]
            bSQ, bRR, bYT, bOS = [Buf(), Buf()], [Buf(), Buf()], [Buf(), Buf()], [Buf(), Buf()]
            GF = sb("GF32", [128, KC], F32, ph)
            bGF = Buf()
            p.op("dve", lambda e: e.tensor_scalar(out=GF[:], in0=VEC[:, O_FIN:O_FIN + 8], scalar1=32.0, scalar2=None, op0=ALU.mult),
                 reads=[bVEC], writes=[bGF])
            for t in range(16):
                s = LAT0 + 128 * t
                sq, rr, yt, os_ = SQ[t % 2], RR[t % 2], YT[t % 2], OS[t % 2]
                bsq, brr, byt, bos = bSQ[t % 2], bRR[t % 2], bYT[t % 2], bOS[t % 2]
                xb = segs(bXT, s, s + 128)
                p.op("act", lambda e, sq=sq, s=s: e.activation(out=sq[:], in_=XT[:, :, s:s + 128], func=AF.Square), reads=xb, writes=[bsq])
                bk = next_bank()
                mm_group(PS[bk][:, 0:128], bk, [(ONES[:], sq[:, kc, :]) for kc in range(KC)], reads=[bsq, bONES])
                p.op("act", lambda e, rr=rr, bk=bk: e.activation(out=rr[:], in_=PS[bk][:, 0:128], func=AF.Sqrt, bias=EPSV[:, 0:1], scale=1.0),
                     reads=[bPS[bk], bEPS], writes=[brr])
                p.op("dve", lambda e, rr=rr: e.reciprocal(out=rr[:], in_=rr[:]), reads=[brr], writes=[brr])
                p.op("dve", lambda e, yt=yt, rr=rr, s=s: e.tensor_tensor(out=yt[:], in0=XT[:, :, s:s + 128],
                                                                          in1=rr[:].unsqueeze(1).broadcast_to([128, KC, 128]), op=ALU.mult),
                     reads=xb + [brr], writes=[byt])
                p.op("dve", lambda e, yt=yt: e.tensor_tensor(out=yt[:], in0=yt[:], in1=GF[:].unsqueeze(2).broadcast_to([128, KC, 128]), op=ALU.mult),
                     reads=[byt, bGF], writes=[byt])
                for half in range(2):
                    bk = next_bank()
                    for q in range(4):
                        kc = 4 * half + q
                        p.op("pe", lambda e, bk=bk, q=q, kc=kc, yt=yt: e.transpose(out=PS[bk][:, 128 * q:128 * q + 128], in_=yt[:, kc, :], identity=IDF[:]),
                             reads=[byt, bIDF], writes=[bPS[bk]], inc=(q == 3))
                    if half == 0:
                        p.op("act", lambda e, os_=os_, bk=bk: e.activation(out=os_[:, 0:512], in_=PS[bk][:], func=AF.Copy), reads=[bPS[bk]], writes=[bos])
                    else:
                        p.op("dve", lambda e, os_=os_, bk=bk: e.tensor_copy(out=os_[:, 512:1024], in_=PS[bk][:]), reads=[bPS[bk]], writes=[bos])
                p.dma("sp", lambda e, os_=os_, t=t: e.dma_start(out=out_d[128 * t:128 * t + 128, :], in_=os_[:]), reads=[bos], owner=bos)
            p.barrier()
        print(f"[build] ops={p.n_op} waits={p.n_wait} cnt={p.cnt}")
    return nc


def _cols(v):
    v = np.asarray(v, np.float32).reshape(-1, 128)
    return np.ascontiguousarray(v.T)


def _rope_tables():
    rows = np.repeat(np.arange(32, dtype=np.float32), 64)
    cols = np.tile(np.arange(64, dtype=np.float32), 32)
    q = 16
    inv = (10000.0 ** (-np.arange(q, dtype=np.float32) / q)).astype(np.float32)
    ca = np.zeros((128, SEQ), np.float32)
    sa = np.zeros((128, SEQ), np.float32)
    for pp in range(128):
        d = pp % 64
        seg, half, i = d // 32, (d % 32) // 16, d % 16
        ang = (rows if seg == 0 else cols) * inv[i]
        ca[pp] = np.cos(ang)
        sa[pp] = np.sin(ang) * (-1.0 if half == 0 else 1.0)
    q = 64
    inv = (10000.0 ** (-np.arange(q, dtype=np.float32) / q)).astype(np.float32)
    cr = np.zeros((128, 96), np.float32)
    sr = np.zeros((128, 96), np.float32)
    for pp in range(128):
        half, i = pp // 64, pp % 64
        sgn = -1.0 if half == 0 else 1.0
        a_row = np.arange(32, dtype=np.float32) * inv[i]
        a_col = np.arange(64, dtype=np.float32) * inv[i]
        cr[pp, 0:32] = np.cos(a_row)
        cr[pp, 32:96] = np.cos(a_col)
        sr[pp, 0:32] = np.sin(a_row) * sgn
        sr[pp, 32:96] = np.sin(a_col) * sgn
    return np.stack([ca, sa]), np.stack([cr, sr])


def _pack(inputs, layers):
    f = lambda a: np.ascontiguousarray(np.asarray(a, np.float32))
    vec = np.zeros((128, NV), np.float32)
    for L in range(4):
        b = L * LW
        vec[:, b + O_ADAB:b + O_ADAB + 48] = _cols(inputs["ada_b"][L])
        vec[:, b + O_GMIX:b + O_GMIX + 8] = _cols(inputs["norm_mix_g"][L])
        vec[:, b + O_GFFN:b + O_GFFN + 8] = _cols(inputs["norm_ffn_g"][L])
        for d in range(3):
            vec[:, b + O_FK + 44 * d:b + O_FK + 44 * d + 44] = _cols(inputs["ffn_conv_k"][L][d])
        vec[:, b + O_FB:b + O_FB + 44] = _cols(inputs["ffn_conv_b"][L])
        if L % 3 == 0:
            for d in range(3):
                vec[:, b + O_MK + 8 * d:b + O_MK + 8 * d + 8] = _cols(inputs["conv_k"][L // 3][d])
    vec[:, O_FIN:O_FIN + 8] = _cols(inputs["final_norm_g"])
    vec[:, O_QG] = np.tile(np.asarray(inputs["attn_q_norm_g"][0], np.float32), 2)
    vec[:, O_KG] = np.tile(np.asarray(inputs["attn_k_norm_g"][0], np.float32), 2)
    vec[:, O_DEC:O_DEC + 8] = np.broadcast_to(np.asarray(inputs["ret_decay"][0], np.float32).reshape(1, 8), (128, 8))
    ra, rr = _rope_tables()
    iota = (np.arange(512, dtype=np.float32)[None, :] - np.arange(128, dtype=np.float32)[:, None])
    wq = f(inputs["attn_w_qkv"][0])
    pidx = np.arange(128)
    bd = (pidx[:, None] // 64 == pidx[None, :] // 64).astype(np.float32)
    pa = ((pidx[:, None] ^ 16) == pidx[None, :]).astype(np.float32)
    pr = ((pidx[:, None] ^ 64) == pidx[None, :]).astype(np.float32)
    cmat = np.ascontiguousarray(np.concatenate([bd, pa, pr], axis=1))
    shared = {
        "vec": vec, "ada_w": f(inputs["ada_w"]), "conv_w_in": f(inputs["conv_w_in"]), "conv_w_out": f(inputs["conv_w_out"]),
        "attn_wqkv": wq, "attn_w_out": f(inputs["attn_w_out"][0]), "ret_w_in": f(inputs["ret_w_in"][0]),
        "ret_w_out": f(inputs["ret_w_out"][0]), "ffn_w_up": f(inputs["ffn_w_up"]), "ffn_w_down": f(inputs["ffn_w_down"]),
        "rope_a": ra, "rope_r": rr, "iota_d": np.ascontiguousarray(iota), "cmat": cmat,
    }
    maps = []
    for b in range(8):
        cv = np.stack([_cols(inputs["c"][b]), _cols(inputs["c_ctx"])], axis=-1).reshape(128, 16)
        m = dict(shared)
        m["x"] = f(inputs["x"][b])
        m["ctx"] = f(inputs["ctx"][b])
        m["cvec"] = np.ascontiguousarray(cv)
        maps.append(m)
    return maps


_NC_CACHE = {}


def kernel(_layers=(0, 1, 2, 3), _cores=8, **inputs):
    key = tuple(_layers)
    if key not in _NC_CACHE:
        _NC_CACHE[key] = build(key)
    nc = _NC_CACHE[key]
    maps = _pack(inputs, key)[:_cores]
    res = run_bass_kernel_spmd(nc, maps, core_ids=list(range(_cores)))
    out = np.stack([np.asarray(r["out"], np.float32) for r in res.results], axis=0)
    return out
```

```python
import numpy as np
from contextlib import ExitStack
import concourse.bass as bass
import concourse.mybir as mybir
from concourse.bass_utils import run_bass_kernel_spmd

F32 = mybir.dt.float32
BF16 = mybir.dt.bfloat16
ALU = mybir.AluOpType
AF = mybir.ActivationFunctionType

ENGS = ("pe", "act", "dve", "pool", "sp")

D = 1024
KC = 8
SEQ = 2048
CTXL = 256
NCOL = 2308
LAT0 = 1
CTX0 = 2051
DFF = 2816
NPAIR = 22
EPS = 1e-6
SEG = [0, 513, 1025, 1537, 2050, NCOL]
PLAIN_LAT = [(1 + 512 * b, 512, 0) for b in range(4)]
PLAIN_CTX = [(CTX0, 256, 1)]
HALO_LAT = [(410 * b, min(412, 2050 - 410 * b), 0) for b in range(5)]
HALO_CTX = [(2050, 258, 1)]

LW = 264
O_ADAB, O_GMIX, O_GFFN, O_FK, O_FB, O_MK = 0, 48, 56, 64, 196, 240
O_FIN = 4 * LW
O_QG = O_FIN + 8
O_KG = O_QG + 1
O_DEC = O_KG + 1
NV = O_DEC + 8

RET_NH = 4
RET_DBG = 0
FFN_MUL_ENG = "pool"
DBG_SKIP_FFN = False
RSLOT = 2048
NSLOT = 5


class Buf:
    __slots__ = ("name", "w", "r", "dsem", "excl")

    def __init__(self, name="", excl=False):
        self.name = name
        self.w = None
        self.r = {}
        self.dsem = None
        self.excl = excl


class Prog:
    def __init__(self, nc, eng_sems, dma_sems):
        self.nc = nc
        self.eng = {"pe": nc.tensor, "act": nc.scalar, "dve": nc.vector, "pool": nc.gpsimd, "sp": nc.sync}
        self.cnt = {e: 0 for e in ENGS}
        self.semh = dict(eng_sems)
        self.dma_free = list(dma_sems)
        self.dcnt = {}
        self.seen = {e: {} for e in ENGS}
        self.n_op = 0
        self.n_wait = 0

    def _deps(self, reads, writes, eng=None):
        deps = {}
        for b in reads:
            t = b.w
            if t is not None and deps.get(t[0], 0) < t[1]:
                deps[t[0]] = t[1]
            if b.excl:
                for k, v in b.r.items():
                    if k != eng and deps.get(k, 0) < v:
                        deps[k] = v
        for b in writes:
            t = b.w
            if t is not None and deps.get(t[0], 0) < t[1]:
                deps[t[0]] = t[1]
            for k, v in b.r.items():
                if deps.get(k, 0) < v:
                    deps[k] = v
        return deps

    def _waits(self, eng, deps):
        seen = self.seen[eng]
        for k, v in deps.items():
            if k == "pe" and eng == "pe":
                continue
            if seen.get(k, 0) < v:
                seen[k] = v
                self.eng[eng].wait_ge(self.semh[k], v)
                self.n_wait += 1

    def _mark(self, tok, reads, writes):
        k, v = tok
        for b in writes:
            b.w = tok
            b.r = {}
        for b in reads:
            if b.r.get(k, 0) < v:
                b.r[k] = v

    def op(self, eng, fn, reads=(), writes=(), inc=True):
        self._waits(eng, self._deps(reads, writes, eng))
        tok = (eng, self.cnt[eng] + 1)
        ins = fn(self.eng[eng])
        if inc:
            self.cnt[eng] += 1
            ins.then_inc(self.semh[eng], 1)
        self.n_op += 1
        self._mark(tok, reads, writes)
        return tok

    def _dsem(self, b):
        if b.dsem is None:
            h = self.dma_free.pop()
            key = ("dma", len(self.dcnt))
            self.semh[key] = h
            self.dcnt[key] = 0
            b.dsem = key
        return b.dsem

    def dma(self, queue, fn, reads=(), writes=(), owner=None):
        owner = owner or (writes[0] if writes else reads[0])
        key = self._dsem(owner)
        self._waits(queue, self._deps(reads, writes))
        self.dcnt[key] += 16
        tok = (key, self.dcnt[key])
        fn(self.eng[queue]).then_inc(self.semh[key], 16)
        self._mark(tok, reads, writes)
        return tok

    def barrier(self):
        deps = {e: self.cnt[e] for e in ("pe", "act", "dve", "pool") if self.cnt[e] > 0}
        for k, v in self.dcnt.items():
            if v > 0:
                deps[k] = v
        for e in ENGS:
            d = dict(deps)
            self._waits(e, d)


def segs(bufs, s, e, part=None):
    out = []
    for i in range(len(SEG) - 1):
        if s < SEG[i + 1] and e > SEG[i]:
            b = bufs[i]
            if isinstance(b, tuple):
                out.extend(b if part is None else (b[part],))
            else:
                out.append(b)
    return out


def build(layers=(0, 1, 2, 3)):
    nc = bass.Bass("TRN2", target_bir_lowering=False)

    def din(name, shape):
        return nc.dram_tensor(name, list(shape), F32, kind="ExternalInput").ap()

    x_d = din("x", [SEQ, D])
    ctx_d = din("ctx", [CTXL, D])
    cvec_d = din("cvec", [128, KC * 2])
    vec_d = din("vec", [128, NV])
    ada_w = din("ada_w", [4, D, 6 * D])
    conv_w_in = din("conv_w_in", [2, D, 3 * D])
    conv_w_out = din("conv_w_out", [2, D, D])
    attn_wqkv = din("attn_wqkv", [D, 1536])
    attn_w_out = din("attn_w_out", [D, D])
    ret_w_in = din("ret_w_in", [D, 6 * D])
    ret_w_out = din("ret_w_out", [2 * D, D])
    ffn_w_up = din("ffn_w_up", [4, D, 2 * DFF])
    ffn_w_down = din("ffn_w_down", [4, DFF, D])
    rope_a = din("rope_a", [2, 128, SEQ])
    rope_r = din("rope_r", [2, 128, 96])
    iota_d = din("iota_d", [128, 512])
    cmat_d = din("cmat", [128, 3 * 128])
    out_d = nc.dram_tensor("out", [SEQ, D], F32, kind="ExternalOutput").ap()

    with ExitStack() as es:
        uniq = [0]

        def sb(name, shape, dt, stack=es):
            uniq[0] += 1
            return stack.enter_context(nc.sbuf_tensor(f"{name}_{uniq[0]}", list(shape), dt))

        XT = sb("XT", [128, KC, NCOL], F32)
        HT = sb("HT", [128, KC, NCOL], BF16)
        RING = sb("RING", [128, NSLOT, RSLOT], BF16)
        VEC = sb("VEC", [128, NV], F32)
        CV = sb("CVEC", [128, KC, 2], F32)
        SC = sb("SC", [128, KC, 2], BF16)
        MODS = [sb(f"MOD{i}", [128, 48, 2], F32) for i in range(2)]
        AMS = [sb(f"AM{i}", [128, 2, KC, 2], F32) for i in range(2)]
        ONES = sb("ONES", [128, 128], BF16)
        IDF = sb("IDF", [128, 128], F32)
        EPSV = sb("EPSV", [128, 2], F32)
        CM = sb("CM", [128, 3, 128], BF16)
        PS = [es.enter_context(nc.psum_tensor(f"PS{i}", [128, 512], F32)) for i in range(8)]

        sems = {e: es.enter_context(nc.semaphore("s_" + e)) for e in ("pe", "act", "dve", "pool")}
        dsems = [es.enter_context(nc.semaphore(f"d{i}")) for i in range(60)]
        es.enter_context(nc.Block())
        p = Prog(nc, sems, dsems)

        bXT = [tuple(Buf(f"XT{i}_{k}") for k in range(KC)) for i in range(5)]
        bHT = [tuple(Buf(f"HT{i}_{k}") for k in range(KC)) for i in range(5)]
        bRING = [Buf(f"R{i}") for i in range(NSLOT)]
        bPS = [Buf(f"PS{i}", excl=True) for i in range(8)]
        bVEC, bCV, bSC, bONES, bIDF, bEPS, bCM = [Buf(n) for n in "VEC CV SC ONES IDF EPS CM".split()]
        bMODS = [Buf("MOD0"), Buf("MOD1")]
        bAMS = [Buf("AM0"), Buf("AM1")]
        cur = {}

        def set_layer(L):
            cur["MOD"], cur["AM"], cur["bMOD"], cur["bAM"] = MODS[L % 2], AMS[L % 2], bMODS[L % 2], bAMS[L % 2]
        st = {"slot": 0, "bank": 0, "reserved": set(), "ring_n": NSLOT, "side": 0}

        def next_bank():
            while True:
                b = st["bank"]
                st["bank"] = (b + 1) % 8
                if b not in st["reserved"]:
                    return b

        def ring_load(parts, side=False):
            if side:
                s = NSLOT - 2 + st["side"] % 2
                st["side"] += 1
            else:
                s = st["slot"]
                st["slot"] = (s + 1) % st["ring_n"]
            for (c0, kcn, n, tot, src) in parts:
                dst = RING[:, s, 0:kcn * tot].rearrange("p (k n) -> p k n", n=tot)[:, :, c0:c0 + n]
                p.dma("pool", lambda e, dst=dst, src=src: e.dma_start(out=dst, in_=src), writes=[bRING[s]])
            return s

        def wview(s, kcn, tot):
            return RING[:, s, 0:kcn * tot].rearrange("p (k n) -> p k n", n=tot)

        def mm_group(out_ap, bank, pairs, reads):
            n = len(pairs)
            for i, (l, r) in enumerate(pairs):
                p.op("pe", lambda e, l=l, r=r, i=i: e.matmul(out_ap, lhsT=l, rhs=r, start=(i == 0), stop=(i == n - 1)),
                     reads=reads, writes=[bPS[bank]], inc=(i == n - 1))

        p.dma("sp", lambda e: e.dma_start(out=VEC[:], in_=vec_d), writes=[bVEC])
        p.dma("sp", lambda e: e.dma_start(out=CV[:].rearrange("p k t -> p (k t)"), in_=cvec_d), writes=[bCV])
        p.dma("pool", lambda e: e.dma_start(out=CM[:].rearrange("p a b -> p (a b)"), in_=cmat_d), writes=[bCM])
        p.op("dve", lambda e: e.memset(ONES[:], 1.0), writes=[bONES])
        p.op("dve", lambda e: e.memset(EPSV[:, 0:1], 1024.0 * EPS), writes=[bEPS])
        p.op("dve", lambda e: e.memset(EPSV[:, 1:2], EPS), reads=[bEPS], writes=[bEPS])
        p.op("pool", lambda e: e.memset(IDF[:], 0.0), writes=[bIDF])
        p.op("pool", lambda e: e.affine_select(out=IDF[:], in_=IDF[:], pattern=[[-1, 128]], compare_op=ALU.not_equal,
                                               fill=1.0, base=0, channel_multiplier=1), reads=[bIDF], writes=[bIDF])
        p.op("pool", lambda e: e.memset(HT[:].rearrange("p k n -> p (k n)"), 0.0), writes=[b for t in bHT for b in t])
        p.op("pool", lambda e: e.memset(XT[:].rearrange("p k n -> p (k n)"), 0.0), writes=[b for t in bXT for b in t])
        p.op("act", lambda e: e.activation(out=SC[:], in_=CV[:], func=AF.Silu), reads=[bCV], writes=[bSC])

        with ExitStack() as ph:
            XS = [sb(f"XS{i}", [128, D], F32, ph) for i in range(2)]
            bXS = [Buf("XS0"), Buf("XS1")]
            for t in range(18):
                src = x_d[128 * t:128 * t + 128, :] if t < 16 else ctx_d[128 * (t - 16):128 * (t - 16) + 128, :]
                col = LAT0 + 128 * t if t < 16 else CTX0 + 128 * (t - 16)
                xs, bxs = XS[t % 2], bXS[t % 2]
                p.dma("sp", lambda e, xs=xs, src=src: e.dma_start(out=xs[:], in_=src), writes=[bxs])
                for half in range(2):
                    bk = next_bank()
                    for q in range(4):
                        kc = half * 4 + q
                        p.op("pe", lambda e, bk=bk, q=q, kc=kc, xs=xs: e.transpose(out=PS[bk][:, 128 * q:128 * q + 128],
                                                                                     in_=xs[:, 128 * kc:128 * kc + 128], identity=IDF[:]),
                             reads=[bxs, bIDF], writes=[bPS[bk]], inc=(q == 3))
                    eng = "act" if half == 0 else "dve"
                    dst = XT[:, half * 4:half * 4 + 4, col:col + 128]
                    srcp = PS[bk][:].rearrange("p (k n) -> p k n", n=128)
                    if eng == "act":
                        p.op("act", lambda e, dst=dst, srcp=srcp: e.activation(out=dst, in_=srcp, func=AF.Copy),
                             reads=[bPS[bk]], writes=[b for q in range(4) for b in segs(bXT, col, col + 128, half * 4 + q)])
                    else:
                        p.op("dve", lambda e, dst=dst, srcp=srcp: e.tensor_copy(out=dst, in_=srcp),
                             reads=[bPS[bk]], writes=[b for q in range(4) for b in segs(bXT, col, col + 128, half * 4 + q)])
            p.barrier()

        def ada_gen(L, standalone=False):
            MOD, AM, bMOD, bAM = MODS[L % 2], AMS[L % 2], bMODS[L % 2], bAMS[L % 2]
            bk = next_bank()
            st["reserved"].add(bk)
            psa = PS[bk]
            wsrc = ada_w[L].rearrange("(k p) n -> p k n", p=128)
            depth = 3 if standalone else 1
            issued = []

            def issue(k):
                issued.append(ring_load([(0, KC, 256, 256, wsrc[:, :, 256 * k:256 * k + 256])], side=not standalone))
            for k in range(min(depth, 24)):
                issue(k)
            for pi in range(24):
                if pi + depth < 24:
                    issue(pi + depth)
                s = issued[pi]
                wv = wview(s, KC, 256)
                for mt in range(2):
                    m = 2 * pi + mt
                    mm_group(psa[:, 2 * m:2 * m + 2], bk,
                             [(wv[:, kc, 128 * mt:128 * mt + 128], SC[:, kc, :]) for kc in range(KC)],
                             reads=[bRING[s], bSC])
                yield
            p.op("dve", lambda e: e.tensor_tensor(out=MOD[:], in0=psa[:, 0:96].rearrange("p (m t) -> p m t", t=2),
                                                  in1=VEC[:, L * LW + O_ADAB:L * LW + O_ADAB + 48].unsqueeze(2).broadcast_to([128, 48, 2]),
                                                  op=ALU.add), reads=[bPS[bk], bVEC], writes=[bMOD])
            st["reserved"].discard(bk)
            for which, (osc, og) in enumerate(((8, O_GMIX), (32, O_GFFN))):
                p.op("dve", lambda e, which=which, osc=osc: e.tensor_scalar(out=AM[:, which], in0=MOD[:, osc:osc + 8, :], scalar1=1.0, scalar2=32.0,
                                                                              op0=ALU.add, op1=ALU.mult), reads=[bMOD, bAM], writes=[bAM])
                p.op("dve", lambda e, which=which, og=og: e.tensor_tensor(out=AM[:, which], in0=AM[:, which],
                                                                            in1=VEC[:, L * LW + og:L * LW + og + 8].unsqueeze(2).broadcast_to([128, 8, 2]),
                                                                            op=ALU.mult), reads=[bAM, bVEC], writes=[bAM])

        def norm_phase(which, blocks):
            osh = 0 if which == 0 else 24
            with ExitStack() as ph:
                SQ = [sb(f"SQ{i}", [128, KC, 512], BF16, ph) for i in range(2)]
                RR = [sb(f"RR{i}", [128, 512], F32, ph) for i in range(2)]
                TM = [sb(f"TM{i}", [128, 4, 512], F32, ph) for i in range(2)]
                bSQa = [Buf(), Buf()]
                bSQb = [Buf(), Buf()]
                bRR = [Buf(), Buf()]
                bTM = [Buf(), Buf()]

                def stats(bi):
                    (s, w, ic) = blocks[bi]
                    sq, rr, brr = SQ[bi % 2], RR[bi % 2], bRR[bi % 2]
                    xb = segs(bXT, s, s + w)
                    p.op("act", lambda e: e.activation(out=sq[:, 0:4, 0:w], in_=XT[:, 0:4, s:s + w], func=AF.Square), reads=xb, writes=[bSQa[bi % 2]])
                    p.op("act", lambda e: e.activation(out=sq[:, 4:8, 0:w], in_=XT[:, 4:8, s:s + w], func=AF.Square), reads=xb, writes=[bSQb[bi % 2]])
                    bk = next_bank()
                    mm_group(PS[bk][:, 0:w], bk, [(ONES[:], sq[:, kc, 0:w]) for kc in range(KC)], reads=[bSQa[bi % 2], bSQb[bi % 2], bONES])
                    p.op("act", lambda e: e.activation(out=rr[:, 0:w], in_=PS[bk][:, 0:w], func=AF.Sqrt, bias=EPSV[:, 0:1], scale=1.0),
                         reads=[bPS[bk], bEPS], writes=[brr])

                def stats_b(bi):
                    (s, w, ic) = blocks[bi]
                    rr, brr = RR[bi % 2], bRR[bi % 2]
                    p.op("dve", lambda e: e.reciprocal(out=rr[:, 0:w], in_=rr[:, 0:w]), reads=[brr], writes=[brr])

                def affine(bi):
                    (s, w, ic) = blocks[bi]
                    rr, brr = RR[bi % 2], bRR[bi % 2]
                    xb = segs(bXT, s, s + w)
                    for half in range(2):
                        tm, btm = TM[half], bTM[half]
                        p.op("dve", lambda e: e.tensor_tensor(
                            out=tm[:, :, 0:w], in0=XT[:, 4 * half:4 * half + 4, s:s + w],
                            in1=rr[:, 0:w].unsqueeze(1).broadcast_to([128, 4, w]), op=ALU.mult),
                            reads=[b for q in range(4) for b in segs(bXT, s, s + w, 4 * half + q)] + [brr], writes=[btm])
                        for q in range(4):
                            kc = 4 * half + q
                            a_ap = cur["AM"][:, which, kc, ic:ic + 1]
                            b_ap = cur["MOD"][:, osh + kc, ic:ic + 1]
                            if q % 2 == 0 and not (half == 1 and q == 2):
                                p.op("dve", lambda e: e.tensor_scalar(
                                    out=HT[:, kc, s:s + w], in0=tm[:, q, 0:w], scalar1=a_ap, scalar2=b_ap, op0=ALU.mult, op1=ALU.add),
                                    reads=[btm, cur["bAM"], cur["bMOD"]], writes=segs(bHT, s, s + w, kc))
                            else:
                                p.op("act", lambda e: e.activation(
                                    out=HT[:, kc, s:s + w], in_=tm[:, q, 0:w], func=AF.Identity, scale=a_ap, bias=b_ap),
                                    reads=[btm, cur["bAM"], cur["bMOD"]], writes=segs(bHT, s, s + w, kc))

                stats(0)
                stats_b(0)
                for bi in range(len(blocks)):
                    if bi + 1 < len(blocks):
                        stats(bi + 1)
                    affine(bi)
                    if bi + 1 < len(blocks):
                        stats_b(bi + 1)
                p.barrier()

        def resid_add(bk, m, s, w, ic, gcol):
            g_ap = cur["MOD"][:, gcol + m, ic:ic + 1]
            xb = segs(bXT, s, s + w, m)
            p.op("dve", lambda e: e.scalar_tensor_tensor(out=XT[:, m, s:s + w], in0=PS[bk][:, 0:w], scalar=g_ap,
                                                         in1=XT[:, m, s:s + w], op0=ALU.mult, op1=ALU.add),
                 reads=[bPS[bk], cur["bMOD"]] + xb, writes=xb)

        def ffn_phase(L, halo, plain, side):
            vb = L * LW
            wup = ffn_w_up[L].rearrange("(k p) n -> p k n", p=128)
            groups = [list(range(0, 6)), list(range(6, 12)), list(range(12, 17)), list(range(17, 22))]
            ncols = 256
            loads = []
            for grp in groups:
                for i in grp:
                    loads.append([(0, KC, 128, 256, wup[:, :, 128 * i:128 * i + 128]),
                                  (128, KC, 128, 256, wup[:, :, DFF + 128 * i:DFF + 128 * i + 128])])
                gp, i0 = len(grp), grp[0]
                wdn = ffn_w_down[L][128 * i0:128 * (i0 + gp), :].rearrange("(k p) n -> p k n", p=128)
                for pj in range(1024 // ncols):
                    loads.append([(0, gp, ncols, ncols, wdn[:, :, ncols * pj:ncols * pj + ncols])])
            slots = {}
            PF = 2
            st["ring_n"], st["slot"] = NSLOT - 2, 0

            def get(idx):
                for k in range(idx, min(idx + PF + 1, len(loads))):
                    if k not in slots:
                        slots[k] = ring_load(loads[k])
                return slots[idx]

            with ExitStack() as ph:
                FT = sb("FT", [128, 6, NCOL], BF16, ph)
                bFT = [Buf(f"FT{i}") for i in range(5)]
                VA = [sb(f"VA{i}", [128, 412], F32, ph) for i in range(3)]
                GA = [sb(f"GA{i}", [128, 412], F32, ph) for i in range(3)]
                SG = [sb(f"SG{i}", [128, 412], F32, ph) for i in range(3)]
                bVA, bGA, bSG = [Buf() for _ in range(3)], [Buf() for _ in range(3)], [Buf() for _ in range(3)]
                it = 0
                li = 0
                pend_s = [None]

                def kcol(d, ch):
                    return VEC[:, vb + O_FK + d * 44 + ch:vb + O_FK + d * 44 + ch + 1]

                def stage_s():
                    if pend_s[0] is None:
                        return
                    (sg, ga, va, bsg, bga, bva, n, il, s) = pend_s[0]
                    pend_s[0] = None
                    p.op("act", lambda e: e.activation(out=sg[:, 0:n], in_=ga[:, 0:n], func=AF.Silu), reads=[bga], writes=[bsg])
                    p.op(FFN_MUL_ENG, lambda e: e.tensor_tensor(out=FT[:, il, s + 1:s + 1 + n], in0=va[:, 0:n], in1=sg[:, 0:n], op=ALU.mult),
                         reads=[bva, bsg], writes=segs(bFT, s + 1, s + 1 + n))

                for grp in groups:
                    gp = len(grp)
                    for il, i in enumerate(grp):
                        s_ = get(li)
                        li += 1
                        wv = wview(s_, KC, 256)
                        for (s, w, ic) in halo:
                            ba, bg = next_bank(), next_bank()
                            hb = segs(bHT, s, s + w)
                            mm_group(PS[bg][:, 0:w], bg, [(wv[:, kc, 128:256], HT[:, kc, s:s + w]) for kc in range(KC)], reads=[bRING[s_]] + hb)
                            mm_group(PS[ba][:, 0:w], ba, [(wv[:, kc, 0:128], HT[:, kc, s:s + w]) for kc in range(KC)], reads=[bRING[s_]] + hb)
                            va, ga, sg = VA[it % 3], GA[it % 3], SG[it % 3]
                            bva, bga, bsg = bVA[it % 3], bGA[it % 3], bSG[it % 3]
                            it += 1
                            n = w - 2
                            rows = ((ga, bga, bg, NPAIR + i), (va, bva, ba, i))
                            for (acc, bacc, bkk, ch) in rows:
                                p.op("act", lambda e, acc=acc, bkk=bkk, ch=ch: e.activation(
                                    out=acc[:, 0:n], in_=PS[bkk][:, 1:1 + n], func=AF.Identity, scale=kcol(1, ch),
                                    bias=VEC[:, vb + O_FB + ch:vb + O_FB + ch + 1]), reads=[bPS[bkk], bVEC], writes=[bacc])
                            stage_s()
                            for d in (0, 2):
                                for (acc, bacc, bkk, ch) in rows:
                                    p.op("dve", lambda e, acc=acc, bkk=bkk, ch=ch, d=d: e.scalar_tensor_tensor(
                                        out=acc[:, 0:n], in0=PS[bkk][:, d:d + n], scalar=kcol(d, ch), in1=acc[:, 0:n],
                                        op0=ALU.mult, op1=ALU.add), reads=[bPS[bkk], bVEC, bacc], writes=[bacc])
                            pend_s[0] = (sg, ga, va, bsg, bga, bva, n, il, s)
                        next(side, None)
                    stage_s()
                    for pj in range(1024 // ncols):
                        s_ = get(li)
                        li += 1
                        wv = wview(s_, gp, ncols)
                        for mt in range(ncols // 128):
                            m = pj * (ncols // 128) + mt
                            for (s, w, ic) in plain:
                                bk = next_bank()
                                mm_group(PS[bk][:, 0:w], bk, [(wv[:, k, 128 * mt:128 * mt + 128], FT[:, k, s:s + w]) for k in range(gp)],
                                         reads=[bRING[s_]] + segs(bFT, s, s + w))
                                resid_add(bk, m, s, w, ic, 40)
                    next(side, None)
                for _ in side:
                    pass
                p.barrier()
                st["ring_n"], st["slot"] = NSLOT, 0

        def conv_phase(L, halo, plain):
            j = L // 3
            vb = L * LW
            win = conv_w_in[j].rearrange("(k p) n -> p k n", p=128)
            wout = conv_w_out[j].rearrange("(k p) n -> p k n", p=128)
            with ExitStack() as ph:
                MT = sb("MT", [128, KC, NCOL], BF16, ph)
                bMT = [Buf(f"MT{i}") for i in range(5)]
                CS = [sb(f"CS{i}", [128, 412], F32, ph) for i in range(2)]
                CVV = [sb(f"CVV{i}", [128, 412], F32, ph) for i in range(2)]
                TT = [sb(f"TT{i}", [128, 412], F32, ph) for i in range(2)]
                bCS, bCVV, bTT = [Buf(), Buf()], [Buf(), Buf()], [Buf(), Buf()]
                it = 0
                for jc in range(KC):
                    s1 = ring_load([(0, KC, 128, 256, win[:, :, D + 128 * jc:D + 128 * jc + 128]),
                                    (128, KC, 128, 256, win[:, :, 2 * D + 128 * jc:2 * D + 128 * jc + 128])])
                    s2 = ring_load([(0, KC, 128, 128, win[:, :, 128 * jc:128 * jc + 128])])
                    w1 = wview(s1, KC, 256)
                    w2 = wview(s2, KC, 128)
                    for (s, w, ic) in halo:
                        bc, bv, bb = next_bank(), next_bank(), next_bank()
                        hb = segs(bHT, s, s + w)
                        mm_group(PS[bc][:, 0:w], bc, [(w1[:, kc, 0:128], HT[:, kc, s:s + w]) for kc in range(KC)], reads=[bRING[s1]] + hb)
                        mm_group(PS[bv][:, 0:w], bv, [(w1[:, kc, 128:256], HT[:, kc, s:s + w]) for kc in range(KC)], reads=[bRING[s1]] + hb)
                        mm_group(PS[bb][:, 0:w], bb, [(w2[:, kc, 0:128], HT[:, kc, s:s + w]) for kc in range(KC)], reads=[bRING[s2]] + hb)
                        cs, cvv, tt = CS[it % 2], CVV[it % 2], TT[it % 2]
                        bcs, bcvv, btt = bCS[it % 2], bCVV[it % 2], bTT[it % 2]
                        it += 1
                        n = w - 2

                        def kcol(d):
                            return VEC[:, vb + O_MK + d * 8 + jc:vb + O_MK + d * 8 + jc + 1]
                        p.op("act", lambda e, cs=cs, bc=bc, w=w: e.activation(out=cs[:, 0:w], in_=PS[bc][:, 0:w], func=AF.Copy),
                             reads=[bPS[bc]], writes=[bcs])
                        p.op("dve", lambda e, cvv=cvv, cs=cs, bv=bv, w=w: e.tensor_tensor(out=cvv[:, 0:w], in0=cs[:, 0:w], in1=PS[bv][:, 0:w], op=ALU.mult),
                             reads=[bcs, bPS[bv]], writes=[bcvv])
                        p.op("act", lambda e, tt=tt, cvv=cvv, n=n: e.activation(out=tt[:, 0:n], in_=cvv[:, 1:1 + n], func=AF.Copy, scale=kcol(1)),
                             reads=[bcvv, bVEC], writes=[btt])
                        for d in (0, 2):
                            p.op("dve", lambda e, tt=tt, cvv=cvv, n=n, d=d: e.scalar_tensor_tensor(
                                out=tt[:, 0:n], in0=cvv[:, d:d + n], scalar=kcol(d), in1=tt[:, 0:n], op0=ALU.mult, op1=ALU.add),
                                reads=[bcvv, bVEC, btt], writes=[btt])
                        p.op("dve", lambda e, tt=tt, bb=bb, n=n, s=s: e.tensor_tensor(
                            out=MT[:, jc, s + 1:s + 1 + n], in0=tt[:, 0:n], in1=PS[bb][:, 1:1 + n], op=ALU.mult),
                            reads=[btt, bPS[bb]], writes=segs(bMT, s + 1, s + 1 + n))
                for pj in range(4):
                    s_ = ring_load([(0, KC, 256, 256, wout[:, :, 256 * pj:256 * pj + 256])])
                    wv = wview(s_, KC, 256)
                    for mt in range(2):
                        m = 2 * pj + mt
                        for (s, w, ic) in plain:
                            bk = next_bank()
                            mm_group(PS[bk][:, 0:w], bk, [(wv[:, kc, 128 * mt:128 * mt + 128], MT[:, kc, s:s + w]) for kc in range(KC)],
                                     reads=[bRING[s_]] + segs(bMT, s, s + w))
                            resid_add(bk, m, s, w, ic, 16)
                p.barrier()


        def attn_phase(L):
            wq_all = attn_wqkv.rearrange("(k p) n -> p k n", p=128)
            plain_all = PLAIN_LAT + PLAIN_CTX
            with ExitStack() as ph:
                KT = sb("KT", [128, 2, NCOL], BF16, ph)
                VA = sb("VA", [128, 18, 4, 128], BF16, ph)
                QT = [sb(f"QT{i}", [128, NCOL], BF16, ph) for i in range(2)]
                OT = [sb(f"OT{i}", [128, NCOL], BF16, ph) for i in range(2)]
                TAB = [sb(f"TAB{i}", [128, 2, 512], F32, ph) for i in range(2)]
                NPT = 6
                PT = [sb(f"PT{i}", [128, 512], BF16, ph) for i in range(NPT)]
                SQ = sb("aSQ", [128, 512], BF16, ph)
                QG = sb("aQG", [128, 512], BF16, ph)
                RR = sb("aRR", [128, 512], F32, ph)
                T1 = sb("aT1", [128, 512], F32, ph)
                T2 = sb("aT2", [128, 512], F32, ph)
                RC = [sb(f"aRC{i}", [128, 512], F32, ph) for i in range(2)]
                bKT = [Buf(), Buf()]
                bVA = Buf()
                bOT = [Buf(), Buf()]
                bQT = [Buf(), Buf()]
                bTAB = [Buf(), Buf()]
                bPT = [Buf() for _ in range(NPT)]
                bSQ, bQG, bRR, bT1, bT2 = Buf(), Buf(), Buf(), Buf(), Buf()
                bRC = [Buf(), Buf()]
                cnt = {"tab": 0, "pt": 0, "rc": 0}

                p.op("pool", lambda e: e.memset(VA[:].rearrange("p a b c -> p (a b c)"), 1.0), writes=[bVA])

                def reserve(b):
                    st["reserved"].add(b)

                def release(b):
                    st["reserved"].discard(b)

                def qk_gen(wv, s_, dst_of, bdst, gcol):
                    g_ap = VEC[:, gcol:gcol + 1]
                    for (s, w, ic) in plain_all:
                        rope = (ic == 0)
                        dst_ap = dst_of(s, w)
                        bk = next_bank()
                        reserve(bk)
                        mm_group(PS[bk][:, 0:w], bk, [(wv[:, kc, :], HT[:, kc, s:s + w]) for kc in range(KC)], reads=[bRING[s_]] + segs(bHT, s, s + w))
                        if rope:
                            ti = cnt["tab"] % 2
                            cnt["tab"] += 1
                            tab, btab = TAB[ti], bTAB[ti]
                            t0 = s - LAT0
                            p.dma("sp", lambda e: e.dma_start(out=tab[:, 0, :], in_=rope_a[0][:, t0:t0 + 512]), writes=[btab])
                            p.dma("sp", lambda e: e.dma_start(out=tab[:, 1, :], in_=rope_a[1][:, t0:t0 + 512]), writes=[btab])
                        yield
                        p.op("act", lambda e: e.activation(out=SQ[:, 0:w], in_=PS[bk][:, 0:w], func=AF.Square), reads=[bPS[bk]], writes=[bSQ])
                        if rope:
                            p.op("act", lambda e: e.activation(out=QG[:, 0:w], in_=PS[bk][:, 0:w], func=AF.Copy, scale=g_ap), reads=[bPS[bk], bVEC], writes=[bQG])
                        yield
                        b2 = next_bank()
                        reserve(b2)
                        mm_group(PS[b2][:, 0:w], b2, [(CM[:, 0, :], SQ[:, 0:w])], reads=[bSQ, bCM])
                        yield
                        p.op("act", lambda e: e.activation(out=RR[:, 0:w], in_=PS[b2][:, 0:w], func=AF.Ln, bias=EPSV[:, 1:2], scale=1.0 / 64),
                             reads=[bPS[b2], bEPS], writes=[bRR])
                        if rope:
                            mm_group(PS[b2][:, 0:w], b2, [(CM[:, 1, :], QG[:, 0:w])], reads=[bQG, bCM])
                        yield
                        p.op("act", lambda e: e.activation(out=RR[:, 0:w], in_=RR[:, 0:w], func=AF.Exp, scale=-0.5), reads=[bRR], writes=[bRR])
                        if not rope:
                            p.op("dve", lambda e: e.scalar_tensor_tensor(out=dst_ap, in0=PS[bk][:, 0:w], scalar=g_ap, in1=RR[:, 0:w],
                                                                         op0=ALU.mult, op1=ALU.mult), reads=[bPS[bk], bVEC, bRR], writes=[bdst])
                        else:
                            p.op("dve", lambda e: e.scalar_tensor_tensor(out=T1[:, 0:w], in0=PS[bk][:, 0:w], scalar=g_ap, in1=tab[:, 0, 0:w],
                                                                         op0=ALU.mult, op1=ALU.mult), reads=[bPS[bk], bVEC, btab], writes=[bT1])
                            p.op("dve", lambda e: e.tensor_tensor(out=T2[:, 0:w], in0=PS[b2][:, 0:w], in1=tab[:, 1, 0:w], op=ALU.mult),
                                 reads=[bPS[b2], btab], writes=[bT2])
                            yield
                            p.op("dve", lambda e: e.tensor_tensor(out=T1[:, 0:w], in0=T1[:, 0:w], in1=T2[:, 0:w], op=ALU.add), reads=[bT1, bT2], writes=[bT1])
                            p.op("dve", lambda e: e.tensor_tensor(out=dst_ap, in0=T1[:, 0:w], in1=RR[:, 0:w], op=ALU.mult), reads=[bT1, bRR], writes=[bdst])
                        release(bk)
                        release(b2)
                        yield

                for g2 in range(2):
                    s_ = ring_load([(0, KC, 128, 128, wq_all[:, :, 1024 + 128 * g2:1024 + 128 * g2 + 128])])
                    for _ in qk_gen(wview(s_, KC, 128), s_, lambda s, w, g2=g2: KT[:, g2, s:s + w], bKT[g2], O_KG):
                        pass
                s_ = ring_load([(0, KC, 256, 256, wq_all[:, :, 1280:1536])])
                wv = wview(s_, KC, 256)
                for kt in range(18):
                    col = LAT0 + 128 * kt if kt < 16 else CTX0 + 128 * (kt - 16)
                    bk = next_bank()
                    mm_group(PS[bk][:, 0:256], bk, [(HT[:, kc, col:col + 128], wv[:, kc, :]) for kc in range(KC)],
                             reads=[bRING[s_]] + segs(bHT, col, col + 128))
                    p.op("act", lambda e, kt=kt, bk=bk: e.activation(out=VA[:, kt, :, 0:64], in_=PS[bk][:, 0:256].rearrange("p (h d) -> p h d", d=64), func=AF.Copy),
                         reads=[bPS[bk]], writes=[bVA])

                def heads_of(j):
                    g2, r = j // 4, j % 4
                    return g2, 8 * g2 + r, 8 * g2 + 4 + r

                def q_gen(j):
                    g2, ha, hb = heads_of(j)
                    s_ = ring_load([(0, KC, 64, 128, wq_all[:, :, 64 * ha:64 * ha + 64]),
                                    (64, KC, 64, 128, wq_all[:, :, 64 * hb:64 * hb + 64])])
                    yield from qk_gen(wview(s_, KC, 128), s_, lambda s, w: QT[j % 2][:, s:s + w], bQT[j % 2], O_QG)

                def outproj_gen(j):
                    g2, ha, hb = heads_of(j)
                    ot, bot = OT[j % 2], bOT[j % 2]
                    s_ = st["slot"]
                    st["slot"] = (s_ + 1) % NSLOT
                    for hh, hd in enumerate((ha, hb)):
                        p.dma("pool", lambda e, hh=hh, hd=hd: e.dma_start(out=RING[64 * hh:64 * hh + 64, s_, 0:1024], in_=attn_w_out[64 * hd:64 * hd + 64, :]),
                              writes=[bRING[s_]])
                    yield
                    for m in range(8):
                        for (s, w, ic) in plain_all:
                            bk = next_bank()
                            mm_group(PS[bk][:, 0:w], bk, [(RING[:, s_, 128 * m:128 * m + 128], ot[:, s:s + w])], reads=[bRING[s_], bot])
                            resid_add(bk, m, s, w, ic, 16)
                            yield

                def attend(j, sides):
                    g2, ha, hb = heads_of(j)
                    qt, bqt = QT[j % 2], bQT[j % 2]
                    ot, bot = OT[j % 2], bOT[j % 2]
                    rr_i = [0]

                    def pull():
                        for _ in range(len(sides)):
                            g = sides[rr_i[0] % len(sides)]
                            rr_i[0] += 1
                            try:
                                next(g)
                                return
                            except StopIteration:
                                continue

                    def pv(item, nk):
                        (hh, pi, kt_, idx_, oa, w) = item
                        p.op("pe", lambda e: e.matmul(PS[oa[hh]][:, 0:w], lhsT=VA[:, kt_, 2 * g2 + hh, :], rhs=PT[pi][:, 0:w],
                                                      start=(idx_ == 0), stop=(idx_ == nk - 1)),
                             reads=[bVA, bPT[pi]], writes=[bPS[oa[hh]]], inc=True)

                    for (s, w, ic) in plain_all:
                        kts = list(range(18)) if ic == 0 else [16, 17]
                        oa = [next_bank(), next_bank()]
                        reserve(oa[0])
                        reserve(oa[1])
                        pend = None
                        for idx, kt in enumerate(kts):
                            kcol = LAT0 + 128 * kt if kt < 16 else CTX0 + 128 * (kt - 16)
                            cur_ = []
                            for hh in range(2):
                                bs = next_bank()
                                lo = 64 * hh
                                mm_group(PS[bs][:, 0:w], bs, [(KT[lo:lo + 64, g2, kcol:kcol + 128], qt[lo:lo + 64, s:s + w])], reads=[bKT[g2], bqt])
                                pi = cnt["pt"] % NPT
                                cnt["pt"] += 1
                                p.op("act", lambda e, pi=pi, bs=bs: e.activation(out=PT[pi][:, 0:w], in_=PS[bs][:, 0:w], func=AF.Exp, scale=0.125),
                                     reads=[bPS[bs]], writes=[bPT[pi]])
                                cur_.append((hh, pi, kt, idx, oa, w))
                            if pend is not None:
                                for item in pend:
                                    pv(item, len(kts))
                            pend = cur_
                            pull()
                        for item in pend:
                            pv(item, len(kts))
                        for hh in range(2):
                            ri = cnt["rc"] % 2
                            cnt["rc"] += 1
                            rc, brc = RC[ri], bRC[ri]
                            p.op("dve", lambda e, rc=rc, hh=hh: e.reciprocal(out=rc[0:64, 0:w], in_=PS[oa[hh]][64:128, 0:w]), reads=[bPS[oa[hh]]], writes=[brc])
                            p.op("dve", lambda e, rc=rc, hh=hh: e.tensor_tensor(out=ot[64 * hh:64 * hh + 64, s:s + w], in0=PS[oa[hh]][0:64, 0:w],
                                                                                in1=rc[0:64, 0:w], op=ALU.mult), reads=[bPS[oa[hh]], brc], writes=[bot])
                        release(oa[0])
                        release(oa[1])
                    for g in sides:
                        for _ in g:
                            pass

                for _ in q_gen(0):
                    pass
                prev_out = None
                for j in range(8):
                    sides = []
                    if j + 1 < 8:
                        sides.append(q_gen(j + 1))
                    if prev_out is not None:
                        sides.append(prev_out)
                    attend(j, sides)
                    prev_out = outproj_gen(j)
                for _ in prev_out:
                    pass
                p.barrier()

        def ret_phase(L):
            import math
            LN16 = math.log(16.0)
            win = ret_w_in.rearrange("(k p) n -> p k n", p=128)
            with ExitStack() as ph:
                QTh = sb("QTh", [128, 2, SEQ], BF16, ph)
                KTh = sb("KTh", [128, 2, NCOL], BF16, ph)
                VH = sb("VH", [128, 18, 512], BF16, ph)
                GS = [sb(f"GS{i}", [128, 4, 512], BF16, ph) for i in range(2)]
                SQY = sb("SQY", [128, 4, 512], BF16, ph)
                Z = SQY
                TMP = sb("rTMP", [128, 512], F32, ph)
                NPTR, NMK = 3, 4
                PT = [sb(f"rPT{i}", [128, 512], BF16, ph) for i in range(NPTR)]
                MK = [sb(f"rMK{i}", [128, 512], F32, ph) for i in range(NMK)]
                QB = sb("rQB", [128, 512], BF16, ph)
                IOTA = sb("IOTA", [128, 512], F32, ph)
                RTAB = sb("RTAB", [128, 2, 96], F32, ph)
                RRr = sb("rRR", [128, 512], F32, ph)
                LG = sb("LG", [128, 8], F32, ph)
                NLG = sb("NLG", [128, 8], F32, ph)
                LT = sb("LT", [128, 8], F32, ph)
                BIAS = sb("BIAS", [128, 8], F32, ph)
                bQTh, bKTh, bVH, bSQY, bTMP, bQB, bIOTA, bRTAB, bRRr, bLG, bLT = [Buf() for _ in range(11)]
                bZ = bSQY
                bVH2 = Buf()
                bGS = [Buf(), Buf()]
                bPT = [Buf() for _ in range(NPTR)]
                bMK = [Buf() for _ in range(NMK)]
                bBIAS = [Buf() for _ in range(8)]
                cnt = {"pt": 0, "mk": 0, "bias": 0, "rb": 0}

                def rb():
                    b = 4 + cnt["rb"] % 4
                    cnt["rb"] += 1
                    return b

                def nmk():
                    i = cnt["mk"] % NMK
                    cnt["mk"] += 1
                    return i

                def nbias():
                    i = cnt["bias"] % 8
                    cnt["bias"] += 1
                    return i

                p.dma("sp", lambda e: e.dma_start(out=IOTA[:], in_=iota_d), writes=[bIOTA])
                p.dma("sp", lambda e: e.dma_start(out=RTAB[:, 0, :], in_=rope_r[0]), writes=[bRTAB])
                p.dma("sp", lambda e: e.dma_start(out=RTAB[:, 1, :], in_=rope_r[1]), writes=[bRTAB])
                p.op("act", lambda e: e.activation(out=LT[:], in_=VEC[:, O_DEC:O_DEC + 8], func=AF.Exp, scale=-math.log(2.0)), reads=[bVEC], writes=[bLT])
                p.op("dve", lambda e: e.tensor_scalar(out=LG[:], in0=LT[:], scalar1=1.0 / 6, scalar2=0.2, op0=ALU.mult, op1=ALU.add), reads=[bLT], writes=[bLG])
                for cst in (0.25, 1.0 / 3, 0.5, 1.0):
                    p.op("dve", lambda e: e.tensor_tensor(out=LG[:], in0=LG[:], in1=LT[:], op=ALU.mult), reads=[bLG, bLT], writes=[bLG])
                    p.op("dve", lambda e, cst=cst: e.tensor_scalar(out=LG[:], in0=LG[:], scalar1=cst, scalar2=None, op0=ALU.add), reads=[bLG], writes=[bLG])
                p.op("dve", lambda e: e.tensor_tensor(out=NLG[:], in0=LG[:], in1=LT[:], op=ALU.mult), reads=[bLG, bLT], writes=[bLG])
                p.op("dve", lambda e: e.tensor_scalar(out=LG[:], in0=NLG[:], scalar1=-1.0, scalar2=None, op0=ALU.mult), reads=[bLG], writes=[bLG])

                def rope_post(bk, dst3, bdst, sg, blk):
                    if sg == 0:
                        c_ap = RTAB[:, 0, 8 * blk:8 * blk + 8].unsqueeze(2).broadcast_to([128, 8, 64])
                        s_ap = RTAB[:, 1, 8 * blk:8 * blk + 8].unsqueeze(2).broadcast_to([128, 8, 64])
                    else:
                        c_ap = RTAB[:, 0, 32:96].unsqueeze(1).broadcast_to([128, 8, 64])
                        s_ap = RTAB[:, 1, 32:96].unsqueeze(1).broadcast_to([128, 8, 64])
                    if RET_DBG == 21:
                        c_ap = IOTA[:].rearrange("p (r c) -> p r c", c=64)
                        s_ap = IOTA[:].rearrange("p (r c) -> p r c", c=64)
                    if RET_DBG == 22 and sg == 1:
                        c_ap = RTAB[:, 0, 0:8].unsqueeze(2).broadcast_to([128, 8, 64])
                        s_ap = RTAB[:, 1, 0:8].unsqueeze(2).broadcast_to([128, 8, 64])
                    p.op("act", lambda e: e.activation(out=QB[:], in_=PS[bk][:], func=AF.Copy), reads=[bPS[bk]], writes=[bQB])
                    b3 = next_bank()
                    mm_group(PS[b3][:], b3, [(CM[:, 2, :], QB[:])], reads=[bQB, bCM])
                    i1, i2 = nmk(), nmk()
                    v3 = lambda ap: ap.rearrange("p (r c) -> p r c", c=64)
                    p.op("dve", lambda e: e.tensor_tensor(out=v3(MK[i1][:]), in0=v3(PS[bk][:]), in1=c_ap, op=ALU.mult), reads=[bPS[bk], bRTAB], writes=[bMK[i1]])
                    p.op("dve", lambda e: e.tensor_tensor(out=v3(MK[i2][:]), in0=v3(PS[b3][:]), in1=s_ap, op=ALU.mult), reads=[bPS[b3], bRTAB], writes=[bMK[i2]])
                    p.op("dve", lambda e: e.tensor_tensor(out=dst3, in0=MK[i1][:], in1=MK[i2][:], op=ALU.add), reads=[bMK[i1], bMK[i2]], writes=[bdst])

                for h in range(RET_NH):
                    if RET_DBG == 1:
                        break
                    lgf, nlgf = LG[:, h:h + 1], NLG[:, h:h + 1]
                    lgb, nlgb = LG[:, 4 + h:5 + h], NLG[:, 4 + h:5 + h]
                    for which, (dstT, bdst, c0) in enumerate(((QTh, bQTh, 256 * h), (KTh, bKTh, 1024 + 256 * h))):
                        s_ = ring_load([(0, KC, 256, 256, win[:, :, c0:c0 + 256])])
                        wv = wview(s_, KC, 256)
                        for sg in range(2):
                            for blk, (s, w, ic) in enumerate(PLAIN_LAT):
                                bk = next_bank()
                                mm_group(PS[bk][:], bk, [(wv[:, kc, 128 * sg:128 * sg + 128], HT[:, kc, s:s + w]) for kc in range(KC)],
                                         reads=[bRING[s_]] + segs(bHT, s, s + w))
                                cbase = (s - LAT0) if which == 0 else s
                                rope_post(bk, dstT[:, sg, cbase:cbase + 512], bdst, sg, blk)
                            if which == 1:
                                (s, w, ic) = PLAIN_CTX[0]
                                bk = next_bank()
                                mm_group(PS[bk][:, 0:w], bk, [(wv[:, kc, 128 * sg:128 * sg + 128], HT[:, kc, s:s + w]) for kc in range(KC)],
                                         reads=[bRING[s_]] + segs(bHT, s, s + w))
                                p.op("act", lambda e, bk=bk, sg=sg, s=s, w=w: e.activation(out=KTh[:, sg, s:s + w], in_=PS[bk][:, 0:w], func=AF.Copy),
                                     reads=[bPS[bk]], writes=[bKTh])
                    if RET_DBG in (2, 21, 22):
                        break
                    sv = [ring_load([(0, KC, 256, 256, win[:, :, 2048 + 512 * h + 256 * i:2048 + 512 * h + 256 * i + 256])]) for i in range(2)]
                    for kt in range(18):
                        col = LAT0 + 128 * kt if kt < 16 else CTX0 + 128 * (kt - 16)
                        bk = next_bank()
                        for i in range(2):
                            wv = wview(sv[i], KC, 256)
                            mm_group(PS[bk][:, 256 * i:256 * i + 256], bk, [(HT[:, kc, col:col + 128], wv[:, kc, :]) for kc in range(KC)],
                                     reads=[bRING[sv[i]]] + segs(bHT, col, col + 128))
                        if kt % 2 == 0:
                            p.op("act", lambda e, kt=kt, bk=bk: e.activation(out=VH[:, kt, :], in_=PS[bk][:], func=AF.Copy), reads=[bPS[bk]], writes=[bVH])
                        else:
                            p.op("dve", lambda e, kt=kt, bk=bk: e.tensor_copy(out=VH[:, kt, :], in_=PS[bk][:]), reads=[bPS[bk]], writes=[bVH2])

                    if RET_DBG == 3:
                        break
                    def gen_mask(qb, kt):
                        if kt < 16:
                            off = 512 * qb - 128 * kt
                            if off >= 128 or off <= -512:
                                sc = lgf if off >= 128 else nlgb
                                bi = nbias()
                                p.op("dve", lambda e: e.tensor_scalar(out=BIAS[:, bi:bi + 1], in0=sc, scalar1=float(off), scalar2=-LN16,
                                                                      op0=ALU.mult, op1=ALU.add), reads=[bLG], writes=[bBIAS[bi]])
                                mi = nmk()
                                p.op("act", lambda e: e.activation(out=MK[mi][:], in_=IOTA[:], func=AF.Exp, scale=sc, bias=BIAS[:, bi:bi + 1]),
                                     reads=[bIOTA, bLG, bBIAS[bi]], writes=[bMK[mi]])
                                return [mi]
                            m1, m2 = nmk(), nmk()
                            bi = nbias()
                            p.op("dve", lambda e: e.memset(BIAS[:, bi:bi + 1], -LN16), writes=[bBIAS[bi]])
                            p.op("dve", lambda e: e.tensor_scalar(out=MK[m1][:], in0=IOTA[:], scalar1=float(off), scalar2=0.0, op0=ALU.add, op1=ALU.max),
                                 reads=[bIOTA], writes=[bMK[m1]])
                            p.op("dve", lambda e: e.scalar_tensor_tensor(out=MK[m2][:], in0=IOTA[:], scalar=float(off), in1=MK[m1][:],
                                                                         op0=ALU.add, op1=ALU.subtract), reads=[bIOTA, bMK[m1]], writes=[bMK[m2]])
                            p.op("act", lambda e: e.activation(out=MK[m1][:], in_=MK[m1][:], func=AF.Exp, scale=lgf, bias=BIAS[:, bi:bi + 1]),
                                 reads=[bMK[m1], bLG, bBIAS[bi]], writes=[bMK[m1]])
                            p.op("act", lambda e: e.activation(out=MK[m2][:], in_=MK[m2][:], func=AF.Exp, scale=nlgb), reads=[bMK[m2], bLG], writes=[bMK[m2]])
                            return [m1, m2]
                        a_ = kt - 16
                        m1, m2 = nmk(), nmk()
                        b1, b2 = nbias(), nbias()
                        o1 = float(512 * qb + 256 - 128 * a_)
                        o2 = float(2048 - 512 * qb + 128 * a_)
                        p.op("dve", lambda e: e.tensor_scalar(out=BIAS[:, b1:b1 + 1], in0=lgf, scalar1=o1, scalar2=-LN16, op0=ALU.mult, op1=ALU.add),
                             reads=[bLG], writes=[bBIAS[b1]])
                        p.op("dve", lambda e: e.tensor_scalar(out=BIAS[:, b2:b2 + 1], in0=lgb, scalar1=o2, scalar2=-LN16, op0=ALU.mult, op1=ALU.add),
                             reads=[bLG], writes=[bBIAS[b2]])
                        p.op("act", lambda e: e.activation(out=MK[m1][:], in_=IOTA[:], func=AF.Exp, scale=lgf, bias=BIAS[:, b1:b1 + 1]),
                             reads=[bIOTA, bLG, bBIAS[b1]], writes=[bMK[m1]])
                        p.op("act", lambda e: e.activation(out=MK[m2][:], in_=IOTA[:], func=AF.Exp, scale=nlgb, bias=BIAS[:, b2:b2 + 1]),
                             reads=[bIOTA, bLG, bBIAS[b2]], writes=[bMK[m2]])
                        p.op("dve", lambda e: e.tensor_tensor(out=MK[m1][:], in0=MK[m1][:], in1=MK[m2][:], op=ALU.add),
                             reads=[bMK[m1], bMK[m2]], writes=[bMK[m1]])
                        return [m1]

                    def st_mm(qb, kt):
                        q0 = 512 * qb
                        kcol = LAT0 + 128 * kt if kt < 16 else CTX0 + 128 * (kt - 16)
                        bs = rb()
                        mm_group(PS[bs][:], bs, [(KTh[:, sg, kcol:kcol + 128], QTh[:, sg, q0:q0 + 512]) for sg in range(2)], reads=[bKTh, bQTh])
                        return bs

                    def apply_mask(bs, mks):
                        pi = cnt["pt"] % NPTR
                        cnt["pt"] += 1
                        if len(mks) == 1:
                            p.op("dve", lambda e: e.tensor_tensor(out=PT[pi][:], in0=PS[bs][:], in1=MK[mks[0]][:], op=ALU.mult),
                                 reads=[bPS[bs], bMK[mks[0]]], writes=[bPT[pi]])
                        else:
                            p.op("dve", lambda e: e.tensor_tensor(out=TMP[:], in0=PS[bs][:], in1=MK[mks[0]][:], op=ALU.mult),
                                 reads=[bPS[bs], bMK[mks[0]]], writes=[bTMP])
                            p.op("dve", lambda e: e.tensor_tensor(out=PT[pi][:], in0=TMP[:], in1=MK[mks[1]][:], op=ALU.mult),
                                 reads=[bTMP, bMK[mks[1]]], writes=[bPT[pi]])
                        return pi

                    def pv_mm(kt, pi):
                        for dv in range(4):
                            p.op("pe", lambda e, dv=dv: e.matmul(PS[dv][:], lhsT=VH[:, kt, 128 * dv:128 * dv + 128], rhs=PT[pi][:],
                                                                 start=(kt == 0), stop=(kt == 17)),
                                 reads=[bVH, bVH2, bPT[pi]], writes=[bPS[dv]], inc=(dv == 3))

                    def g_proj(qb):
                        (s, w, ic) = PLAIN_LAT[qb]
                        sg_ = [ring_load([(0, KC, 256, 256, win[:, :, 4096 + 512 * h + 256 * i:4096 + 512 * h + 256 * i + 256])]) for i in range(2)]
                        for m in range(4):
                            wv = wview(sg_[m // 2], KC, 256)
                            bk = rb()
                            mm_group(PS[bk][:], bk, [(wv[:, kc, 128 * (m % 2):128 * (m % 2) + 128], HT[:, kc, s:s + w]) for kc in range(KC)],
                                     reads=[bRING[sg_[m // 2]]] + segs(bHT, s, s + w))
                            p.op("act", lambda e, m=m, bk=bk: e.activation(out=GS[qb % 2][:, m, :], in_=PS[bk][:], func=AF.Silu), reads=[bPS[bk]], writes=[bGS[qb % 2]])

                    g_proj(0)
                    for qb, (s, w, ic) in enumerate(PLAIN_LAT):
                        LOOK = 1
                        masks = {}
                        for k in range(min(LOOK, 18)):
                            masks[k] = gen_mask(qb, k)
                        stb = {0: st_mm(qb, 0)}
                        pts = {}
                        for i in range(18):
                            if i + LOOK < 18:
                                masks[i + LOOK] = gen_mask(qb, i + LOOK)
                            if i + 1 < 18:
                                stb[i + 1] = st_mm(qb, i + 1)
                            pts[i] = apply_mask(stb[i], masks[i])
                            if i >= 1:
                                pv_mm(i - 1, pts[i - 1])
                        pv_mm(17, pts[17])
                        for dv in range(4):
                            p.op("act", lambda e, dv=dv: e.activation(out=SQY[:, dv, :], in_=PS[dv][:], func=AF.Square), reads=[bPS[dv]], writes=[bSQY])
                        bss = rb()
                        mm_group(PS[bss][:], bss, [(ONES[:], SQY[:, dv, :]) for dv in range(4)], reads=[bSQY, bONES])
                        p.op("act", lambda e, bss=bss: e.activation(out=RRr[:], in_=PS[bss][:], func=AF.Ln, bias=EPSV[:, 1:2], scale=1.0 / 512),
                             reads=[bPS[bss], bEPS], writes=[bRRr])
                        if qb + 1 < 4:
                            g_proj(qb + 1)
                        p.op("act", lambda e: e.activation(out=RRr[:], in_=RRr[:], func=AF.Exp, scale=-0.5), reads=[bRRr], writes=[bRRr])
                        for dv in range(4):
                            p.op("dve", lambda e, dv=dv: e.tensor_tensor(out=TMP[:], in0=PS[dv][:], in1=RRr[:], op=ALU.mult), reads=[bPS[dv], bRRr], writes=[bTMP])
                            p.op("dve", lambda e, dv=dv: e.tensor_tensor(out=Z[:, dv, :], in0=TMP[:], in1=GS[qb % 2][:, dv, :], op=ALU.mult), reads=[bTMP, bGS[qb % 2]], writes=[bZ])
                        wo = ret_w_out[512 * h:512 * h + 512, :].rearrange("(k p) n -> p k n", p=128)
                        for pj in range(2):
                            s_ = ring_load([(0, 4, 512, 512, wo[:, :, 512 * pj:512 * pj + 512])])
                            wv = wview(s_, 4, 512)
                            for mt in range(4):
                                bk = rb()
                                mm_group(PS[bk][:], bk, [(wv[:, dv, 128 * mt:128 * mt + 128], Z[:, dv, :]) for dv in range(4)], reads=[bRING[s_], bZ])
                                resid_add(bk, 4 * pj + mt, s, w, 0, 16)
                p.barrier()

        for L in layers:
            kind = L % 3
            has_ctx = L <= 1
            ctx_in = L <= 2
            plain = PLAIN_LAT + (PLAIN_CTX if has_ctx else [])
            halo = HALO_LAT + (HALO_CTX if has_ctx else [])
            if L == layers[0]:
                for _ in ada_gen(L, standalone=True):
                    pass
            set_layer(L)
            nxt = layers[layers.index(L) + 1] if layers.index(L) + 1 < len(layers) else None
            norm_phase(0, PLAIN_LAT + (PLAIN_CTX if ctx_in else []))
            if kind == 0:
                conv_phase(L, halo, plain)
            elif kind == 1:
                attn_phase(L)
            else:
                ret_phase(L)
            if not DBG_SKIP_FFN:
                norm_phase(1, plain)
                ffn_phase(L, halo, plain, ada_gen(nxt) if nxt is not None else iter(()))
            elif nxt is not None:
                for _ in ada_gen(nxt):
                    pass

        with ExitStack() as ph:
            SQ = [sb(f"fSQ{i}", [128, KC, 128], BF16, ph) for i in range(2)]
            RR = [sb(f"fRR{i}", [128, 128], F32, ph) for i in range(2)]
            YT = [sb(f"fYT{i}", [128, KC, 128], F32, ph) for i in range(2)]
            OS = [sb(f"fOS{i}", [128, D], F32, ph) for i in range(2)]
            bSQ, bRR, bYT, bOS = [Buf(), Buf()], [Buf(), Buf()], [Buf(), Buf()], [Buf(), Buf()]
            GF = sb("GF32", [128, KC], F32, ph)
            bGF = Buf()
            p.op("dve", lambda e: e.tensor_scalar(out=GF[:], in0=VEC[:, O_FIN:O_FIN + 8], scalar1=32.0, scalar2=None, op0=ALU.mult),
                 reads=[bVEC], writes=[bGF])
            for t in range(16):
                s = LAT0 + 128 * t
                sq, rr, yt, os_ = SQ[t % 2], RR[t % 2], YT[t % 2], OS[t % 2]
                bsq, brr, byt, bos = bSQ[t % 2], bRR[t % 2], bYT[t % 2], bOS[t % 2]
                xb = segs(bXT, s, s + 128)
                p.op("act", lambda e, sq=sq, s=s: e.activation(out=sq[:], in_=XT[:, :, s:s + 128], func=AF.Square), reads=xb, writes=[bsq])
                bk = next_bank()
                mm_group(PS[bk][:, 0:128], bk, [(ONES[:], sq[:, kc, :]) for kc in range(KC)], reads=[bsq, bONES])
                p.op("act", lambda e, rr=rr, bk=bk: e.activation(out=rr[:], in_=PS[bk][:, 0:128], func=AF.Sqrt, bias=EPSV[:, 0:1], scale=1.0),
                     reads=[bPS[bk], bEPS], writes=[brr])
                p.op("dve", lambda e, rr=rr: e.reciprocal(out=rr[:], in_=rr[:]), reads=[brr], writes=[brr])
                p.op("dve", lambda e, yt=yt, rr=rr, s=s: e.tensor_tensor(out=yt[:], in0=XT[:, :, s:s + 128],
                                                                          in1=rr[:].unsqueeze(1).broadcast_to([128, KC, 128]), op=ALU.mult),
                     reads=xb + [brr], writes=[byt])
                p.op("dve", lambda e, yt=yt: e.tensor_tensor(out=yt[:], in0=yt[:], in1=GF[:].unsqueeze(2).broadcast_to([128, KC, 128]), op=ALU.mult),
                     reads=[byt, bGF], writes=[byt])
                for half in range(2):
                    bk = next_bank()
                    for q in range(4):
                        kc = 4 * half + q
                        p.op("pe", lambda e, bk=bk, q=q, kc=kc, yt=yt: e.transpose(out=PS[bk][:, 128 * q:128 * q + 128], in_=yt[:, kc, :], identity=IDF[:]),
                             reads=[byt, bIDF], writes=[bPS[bk]], inc=(q == 3))
                    if half == 0:
                        p.op("act", lambda e, os_=os_, bk=bk: e.activation(out=os_[:, 0:512], in_=PS[bk][:], func=AF.Copy), reads=[bPS[bk]], writes=[bos])
                    else:
                        p.op("dve", lambda e, os_=os_, bk=bk: e.tensor_copy(out=os_[:, 512:1024], in_=PS[bk][:]), reads=[bPS[bk]], writes=[bos])
                p.dma("sp", lambda e, os_=os_, t=t: e.dma_start(out=out_d[128 * t:128 * t + 128, :], in_=os_[:]), reads=[bos], owner=bos)
            p.barrier()
        print(f"[build] ops={p.n_op} waits={p.n_wait} cnt={p.cnt}")
    return nc


def _cols(v):
    v = np.asarray(v, np.float32).reshape(-1, 128)
    return np.ascontiguousarray(v.T)


def _rope_tables():
    rows = np.repeat(np.arange(32, dtype=np.float32), 64)
    cols = np.tile(np.arange(64, dtype=np.float32), 32)
    q = 16
    inv = (10000.0 ** (-np.arange(q, dtype=np.float32) / q)).astype(np.float32)
    ca = np.zeros((128, SEQ), np.float32)
    sa = np.zeros((128, SEQ), np.float32)
    for pp in range(128):
        d = pp % 64
        seg, half, i = d // 32, (d % 32) // 16, d % 16
        ang = (rows if seg == 0 else cols) * inv[i]
        ca[pp] = np.cos(ang)
        sa[pp] = np.sin(ang) * (-1.0 if half == 0 else 1.0)
    q = 64
    inv = (10000.0 ** (-np.arange(q, dtype=np.float32) / q)).astype(np.float32)
    cr = np.zeros((128, 96), np.float32)
    sr = np.zeros((128, 96), np.float32)
    for pp in range(128):
        half, i = pp // 64, pp % 64
        sgn = -1.0 if half == 0 else 1.0
        a_row = np.arange(32, dtype=np.float32) * inv[i]
        a_col = np.arange(64, dtype=np.float32) * inv[i]
        cr[pp, 0:32] = np.cos(a_row)
        cr[pp, 32:96] = np.cos(a_col)
        sr[pp, 0:32] = np.sin(a_row) * sgn
        sr[pp, 32:96] = np.sin(a_col) * sgn
    return np.stack([ca, sa]), np.stack([cr, sr])


def _pack(inputs, layers):
    f = lambda a: np.ascontiguousarray(np.asarray(a, np.float32))
    vec = np.zeros((128, NV), np.float32)
    for L in range(4):
        b = L * LW
        vec[:, b + O_ADAB:b + O_ADAB + 48] = _cols(inputs["ada_b"][L])
        vec[:, b + O_GMIX:b + O_GMIX + 8] = _cols(inputs["norm_mix_g"][L])
        vec[:, b + O_GFFN:b + O_GFFN + 8] = _cols(inputs["norm_ffn_g"][L])
        for d in range(3):
            vec[:, b + O_FK + 44 * d:b + O_FK + 44 * d + 44] = _cols(inputs["ffn_conv_k"][L][d])
        vec[:, b + O_FB:b + O_FB + 44] = _cols(inputs["ffn_conv_b"][L])
        if L % 3 == 0:
            for d in range(3):
                vec[:, b + O_MK + 8 * d:b + O_MK + 8 * d + 8] = _cols(inputs["conv_k"][L // 3][d])
    vec[:, O_FIN:O_FIN + 8] = _cols(inputs["final_norm_g"])
    vec[:, O_QG] = np.tile(np.asarray(inputs["attn_q_norm_g"][0], np.float32), 2)
    vec[:, O_KG] = np.tile(np.asarray(inputs["attn_k_norm_g"][0], np.float32), 2)
    vec[:, O_DEC:O_DEC + 8] = np.broadcast_to(np.asarray(inputs["ret_decay"][0], np.float32).reshape(1, 8), (128, 8))
    ra, rr = _rope_tables()
    iota = (np.arange(512, dtype=np.float32)[None, :] - np.arange(128, dtype=np.float32)[:, None])
    wq = f(inputs["attn_w_qkv"][0])
    pidx = np.arange(128)
    bd = (pidx[:, None] // 64 == pidx[None, :] // 64).astype(np.float32)
    pa = ((pidx[:, None] ^ 16) == pidx[None, :]).astype(np.float32)
    pr = ((pidx[:, None] ^ 64) == pidx[None, :]).astype(np.float32)
    cmat = np.ascontiguousarray(np.concatenate([bd, pa, pr], axis=1))
    shared = {
        "vec": vec, "ada_w": f(inputs["ada_w"]), "conv_w_in": f(inputs["conv_w_in"]), "conv_w_out": f(inputs["conv_w_out"]),
        "attn_wqkv": wq, "attn_w_out": f(inputs["attn_w_out"][0]), "ret_w_in": f(inputs["ret_w_in"][0]),
        "ret_w_out": f(inputs["ret_w_out"][0]), "ffn_w_up": f(inputs["ffn_w_up"]), "ffn_w_down": f(inputs["ffn_w_down"]),
        "rope_a": ra, "rope_r": rr, "iota_d": np.ascontiguousarray(iota), "cmat": cmat,
    }
    maps = []
    for b in range(8):
        cv = np.stack([_cols(inputs["c"][b]), _cols(inputs["c_ctx"])], axis=-1).reshape(128, 16)
        m = dict(shared)
        m["x"] = f(inputs["x"][b])
        m["ctx"] = f(inputs["ctx"][b])
        m["cvec"] = np.ascontiguousarray(cv)
        maps.append(m)
    return maps


_NC_CACHE = {}


def kernel(_layers=(0, 1, 2, 3), _cores=8, **inputs):
    key = tuple(_layers)
    if key not in _NC_CACHE:
        _NC_CACHE[key] = build(key)
    nc = _NC_CACHE[key]
    maps = _pack(inputs, key)[:_cores]
    res = run_bass_kernel_spmd(nc, maps, core_ids=list(range(_cores)))
    out = np.stack([np.asarray(r["out"], np.float32) for r in res.results], axis=0)
    return out
```
